# Optimizing a Trainium2 kernel written in Bass

```python
import jax, jax.numpy as jnp
from jax import lax
import numpy as np

D_MODEL = 1024
BATCH = 16
SEQ = 256
DEPTH = 2
DEC_BATCH = 8
DEC_SEQ = 1024
PAST_LEN = 256

GRID_W = 64
N_BRANCH = 4
BRANCH_W = D_MODEL // N_BRANCH
CHUNK = 64
Q_BLOCK = 128
D_FF = 2816
N_MOD = 9
RMS_EPS = 1e-6
A_N = 64
A_HEADS = BRANCH_W // A_N
A_W = A_HEADS * A_N
A_LORA_W = 64
A_LORA_A = 64
A_LORA_G = 160
A_GN_EPS = 64e-5
A_SIZES = (A_W, A_W, A_W, A_LORA_W, A_LORA_W, A_LORA_A, A_LORA_A, A_LORA_G)
A_COLS = 3 * A_W + 2 * A_LORA_W + 2 * A_LORA_A + A_LORA_G
B_HEADS = 4
B_DK = BRANCH_W // (2 * B_HEADS)
B_DV = BRANCH_W // B_HEADS
B_LORA = 16
GLA_TAU = 16.0
B_QK = B_HEADS * B_DK
B_V = B_HEADS * B_DV
B_SIZES = (B_QK, B_QK, B_V, B_LORA, B_LORA, B_V)
B_COLS = 2 * B_QK + 2 * B_V + 2 * B_LORA
C_HEADS = 4
C_DK = BRANCH_W // C_HEADS
C_DV = BRANCH_W // C_HEADS
CONV_W = 3
C_QKV = C_HEADS * (2 * C_DK + C_DV)
C_V = C_HEADS * C_DV
C_SIZES = (C_QKV, C_V, C_HEADS, C_HEADS, C_HEADS, C_HEADS)
C_COLS = C_QKV + C_V + 4 * C_HEADS
D_HD = 64
D_HEADS = BRANCH_W // D_HD
D_KV_HEADS = D_HEADS // 2
ROPE_THETA = 10000.0
D_SIZES = (D_HEADS * D_HD, D_KV_HEADS * D_HD, D_KV_HEADS * D_HD)
D_COLS = (D_HEADS + 2 * D_KV_HEADS) * D_HD
GATE_COLS = N_BRANCH * D_MODEL
BLOCK_SIZES = (A_COLS, B_COLS, C_COLS, D_COLS, GATE_COLS)
N_IN = A_COLS + B_COLS + C_COLS + D_COLS + GATE_COLS

kernel_name = "hybrid_diffusion_parallel_mixer_step"


def split_last(x, sizes):
    idx, acc = [], 0
    for s in sizes[:-1]:
        acc += s
        idx.append(acc)
    return jnp.split(x, idx, axis=-1)


def rms_norm(x, g, eps=RMS_EPS):
    xf = x.astype(jnp.float32)
    y = xf * lax.rsqrt(jnp.mean(xf * xf, axis=-1, keepdims=True) + eps)
    return (y * g.astype(jnp.float32)).astype(x.dtype)


def l2_normalize(x, eps=1e-6):
    xf = x.astype(jnp.float32)
    return xf * lax.rsqrt(jnp.sum(xf * xf, axis=-1, keepdims=True) + eps)


def adaln(x, g, shift, scale):
    return rms_norm(x, g) * (1.0 + scale) + shift


def swiglu(h, w_in, w_out):
    gate, up = jnp.split(h @ w_in, 2, axis=-1)
    return (jax.nn.silu(gate) * up) @ w_out


def modulation(cond, w_mod, b_mod):
    m = jax.nn.silu(cond) @ w_mod + b_mod
    return m.reshape(cond.shape[0], N_MOD, D_MODEL)


def centred_shift(p):
    zeros = jnp.zeros_like(p[:, :1])
    prev = jnp.concatenate([zeros, p[:, :-1]], axis=1)
    nxt = jnp.concatenate([p[:, 1:], zeros], axis=1)
    return 0.5 * (prev + nxt)


def depthwise_conv(x, w):
    ch = x.shape[-1]
    pad = CONV_W // 2
    return lax.conv_general_dilated(x, w[:, None, :].astype(x.dtype), window_strides=(1,),
                                    padding=((pad, pad),), dimension_numbers=("NWC", "WIO", "NWC"),
                                    feature_group_count=ch)


def to_chunks(t):
    b, s, h = t.shape[:3]
    t = t.reshape((b, s // CHUNK, CHUNK, h) + t.shape[3:])
    return jnp.moveaxis(t, 3, 1)


def from_chunks(o):
    nc, b, h, c, d = o.shape
    return jnp.transpose(o, (1, 0, 3, 2, 4)).reshape(b, nc * c, h, d)


def rwkv7_scan(r, w, k, v, kk, a, s0):
    def step(s, inp):
        r_t, w_t, k_t, v_t, kk_t, a_t = inp
        sa = jnp.einsum("bhvk,bhk->bhv", s, -kk_t)
        s = (s * w_t[:, :, None, :] + sa[..., None] * (kk_t * a_t)[:, :, None, :]
             + v_t[..., None] * k_t[:, :, None, :])
        return s, jnp.einsum("bhvk,bhk->bhv", s, r_t)
    xs = tuple(jnp.moveaxis(t, 1, 0) for t in (r, w, k, v, kk, a))
    s_fin, ys = lax.scan(step, s0, xs)
    return jnp.moveaxis(ys, 0, 1), s_fin


def rwkv7_mixer(pa, lp, s0):
    b, s, _ = pa.shape
    pa = pa.astype(jnp.float32)
    pa = pa + lp["rwkv_mu"] * (centred_shift(pa) - pa)
    r, k, v, wd_f, wd_b, ad_f, ad_b, gd = split_last(pa, A_SIZES)
    heads = lambda t: t.reshape(b, s, A_HEADS, A_N)
    r_h, v_h = heads(r), heads(v)
    kk = l2_normalize(heads(k * lp["rwkv_k_k"]))
    g = jax.nn.sigmoid(gd) @ lp["rwkv_g_up"]
    ys, bonuses, finals = [], [], []
    for d, (wd, ad) in enumerate(((wd_f, ad_f), (wd_b, ad_b))):
        w_log = -jax.nn.softplus(-(lp["rwkv_w0"][d] + jnp.tanh(wd) @ lp["rwkv_w_up"][d])) - 0.5
        decay = heads(jnp.exp(-jnp.exp(w_log)))
        a = jax.nn.sigmoid(lp["rwkv_a0"][d] + ad @ lp["rwkv_a_up"][d])
        k_d = heads(k * (1.0 + (a - 1.0) * lp["rwkv_k_a"]))
        seqs = (r_h, decay, k_d, v_h, kk, heads(a))
        if d == 1:
            seqs = tuple(jnp.flip(t, axis=1) for t in seqs)
        y, s_fin = rwkv7_scan(*seqs, s0[:, d].astype(jnp.float32))
        ys.append(jnp.flip(y, axis=1) if d == 1 else y)
        finals.append(s_fin)
        bonuses.append(jnp.sum(r_h * k_d * lp["rwkv_r_k"], axis=-1, keepdims=True) * v_h)
    y = ys[0] + ys[1]
    mu = jnp.mean(y, axis=-1, keepdims=True)
    var = jnp.mean(jnp.square(y - mu), axis=-1, keepdims=True)
    y = ((y - mu) * lax.rsqrt(var + A_GN_EPS)).reshape(b, s, A_W) * lp["rwkv_ln_g"] + lp["rwkv_ln_b"]
    out = (y + (bonuses[0] + bonuses[1]).reshape(b, s, A_W)) * g
    return out, jnp.stack(finals, axis=1)


def gla_chunked(q, k, v, log_a, s0):
    q, k, v = to_chunks(q), to_chunks(k), to_chunks(v)
    bcum = jnp.cumsum(to_chunks(log_a), axis=-2)
    b_last = bcum[..., -1:, :]
    qg = q * jnp.exp(bcum)
    kd = k * jnp.exp(b_last - bcum)
    gl = jnp.exp(b_last[..., 0, :])
    incl = jnp.tril(jnp.ones((CHUNK, CHUNK), dtype=bool))[:, :, None]

    def step(s, inp):
        q_c, k_c, v_c, b_c, qg_c, kd_c, gl_c = inp
        rel = jnp.exp(jnp.where(incl, b_c[:, :, :, None, :] - b_c[:, :, None, :, :], -jnp.inf))
        att = jnp.einsum("bhtsd,bhtd,bhsd->bhts", rel, q_c, k_c)
        o = jnp.einsum("bhts,bhsv->bhtv", att, v_c) + jnp.einsum("bhtd,bhdv->bhtv", qg_c, s)
        s = gl_c[..., None] * s + jnp.einsum("bhsd,bhsv->bhdv", kd_c, v_c)
        return s, o
    xs = tuple(jnp.moveaxis(t, 2, 0) for t in (q, k, v, bcum, qg, kd, gl))
    s_fin, o = lax.scan(step, s0, xs)
    return from_chunks(o), s_fin


def gla_mixer(pb, lp, s0):
    b, s, _ = pb.shape
    pb = pb.astype(jnp.float32)
    q, k, v, af, ab, r = split_last(pb, B_SIZES)
    q = q.reshape(b, s, B_HEADS, B_DK) * (B_DK ** -0.5)
    k = k.reshape(b, s, B_HEADS, B_DK)
    v = v.reshape(b, s, B_HEADS, B_DV)
    outs, finals = [], []
    for d, ad in enumerate((af, ab)):
        log_a = jax.nn.log_sigmoid(ad @ lp["gla_alpha_up"][d] + lp["gla_alpha_b"][d]) / GLA_TAU
        seqs = (q, k, v, log_a.reshape(b, s, B_HEADS, B_DK))
        if d == 1:
            seqs = tuple(jnp.flip(t, axis=1) for t in seqs)
        o, s_fin = gla_chunked(*seqs, s0[:, d].astype(jnp.float32))
        outs.append(jnp.flip(o, axis=1) if d == 1 else o)
        finals.append(s_fin)
    o = rms_norm(outs[0] + outs[1], lp["gla_norm_g"]).reshape(b, s, B_V) * jax.nn.silu(r)
    return o, jnp.stack(finals, axis=1)


def delta_chunked(q, k, v, beta, g, s0):
    q, k, v = to_chunks(q), to_chunks(k), to_chunks(v)
    beta, g = to_chunks(beta), to_chunks(g)
    gc = jnp.cumsum(g, axis=-1)
    incl = jnp.tril(jnp.ones((CHUNK, CHUNK), dtype=bool))
    strict = jnp.tril(jnp.ones((CHUNK, CHUNK), dtype=bool), -1)
    decay = jnp.exp(jnp.where(incl, gc[..., :, None] - gc[..., None, :], -jnp.inf))
    kb = k * beta[..., None]
    lower = jnp.where(strict, jnp.einsum("bhntd,bhnsd->bhnts", kb, k) * decay, 0.0)
    m = lower + jnp.eye(CHUNK, dtype=lower.dtype)
    u = lax.linalg.triangular_solve(m, v * beta[..., None], left_side=True, lower=True, unit_diagonal=True)
    w = lax.linalg.triangular_solve(m, kb * jnp.exp(gc)[..., None], left_side=True, lower=True,
                                    unit_diagonal=True)
    qk = jnp.einsum("bhntd,bhnsd->bhnts", q, k) * decay
    qg = q * jnp.exp(gc)[..., None]
    kd = k * jnp.exp(gc[..., -1:] - gc)[..., None]
    gl = jnp.exp(gc[..., -1])

    def step(s, inp):
        u_c, w_c, qk_c, qg_c, kd_c, gl_c = inp
        v_new = u_c - jnp.einsum("bhtd,bhdv->bhtv", w_c, s)
        o = jnp.einsum("bhtd,bhdv->bhtv", qg_c, s) + jnp.einsum("bhts,bhsv->bhtv", qk_c, v_new)
        s = gl_c[..., None, None] * s + jnp.einsum("bhsd,bhsv->bhdv", kd_c, v_new)
        return s, o
    xs = tuple(jnp.moveaxis(t, 2, 0) for t in (u, w, qk, qg, kd, gl))
    s_fin, o = lax.scan(step, s0, xs)
    return from_chunks(o), s_fin


def delta_mixer(pc, lp, s0):
    b, s, _ = pc.shape
    pc = pc.astype(jnp.float32)
    qkv, gate, bf, bb, af, ab = split_last(pc, C_SIZES)
    qkv = jax.nn.silu(depthwise_conv(qkv, lp["delta_conv"]))
    q, k, v = split_last(qkv, (C_HEADS * C_DK, C_HEADS * C_DK, C_HEADS * C_DV))
    q = l2_normalize(q.reshape(b, s, C_HEADS, C_DK)) * (C_DK ** -0.5)
    k = l2_normalize(k.reshape(b, s, C_HEADS, C_DK))
    v = v.reshape(b, s, C_HEADS, C_DV)
    outs, finals = [], []
    for d, (bd, ad) in enumerate(((bf, af), (bb, ab))):
        beta = jax.nn.sigmoid(bd)
        g = -jnp.exp(lp["delta_a_log"][d]) * jax.nn.softplus(ad + lp["delta_dt_bias"][d])
        seqs = (q, k, v, beta, g)
        if d == 1:
            seqs = tuple(jnp.flip(t, axis=1) for t in seqs)
        o, s_fin = delta_chunked(*seqs, s0[:, d].astype(jnp.float32))
        outs.append(jnp.flip(o, axis=1) if d == 1 else o)
        finals.append(s_fin)
    o = rms_norm(outs[0] + outs[1], lp["delta_norm_g"]).reshape(b, s, C_V) * jax.nn.silu(gate)
    return o, jnp.stack(finals, axis=1)


def axial_rope_tables(n_tokens):
    rows = n_tokens // GRID_W
    t = jnp.arange(rows * GRID_W)
    row_pos = (t // GRID_W).astype(jnp.float32)
    col_pos = (t % GRID_W).astype(jnp.float32)
    half = D_HD // 2
    inv_freq = 1.0 / (ROPE_THETA ** (jnp.arange(0, half, 2, dtype=jnp.float32) / half))
    ang = jnp.stack([row_pos[:, None] * inv_freq, col_pos[:, None] * inv_freq], axis=1)
    return jnp.cos(ang), jnp.sin(ang)


def apply_axial_rope(x, cos, sin):
    b, s, h, d = x.shape
    xr = x.astype(jnp.float32).reshape(b, s, h, 2, 2, d // 4)
    x1, x2 = xr[..., 0, :], xr[..., 1, :]
    cs, sn = cos[None, :, None], sin[None, :, None]
    out = jnp.stack([x1 * cs - x2 * sn, x2 * cs + x1 * sn], axis=-2)
    return out.reshape(b, s, h, d).astype(x.dtype)


def attn_qkv(pd, lp, rope):
    b, s, _ = pd.shape
    q, k, v = split_last(pd, D_SIZES)
    q = rms_norm(q.reshape(b, s, D_HEADS, D_HD), lp["attn_q_norm"])
    k = rms_norm(k.reshape(b, s, D_KV_HEADS, D_HD), lp["attn_k_norm"])
    v = v.reshape(b, s, D_KV_HEADS, D_HD)
    if rope is not None:
        q = apply_axial_rope(q, *rope)
        k = apply_axial_rope(k, *rope)
    return q, k, v


def block_attention(q, k, v):
    b, s, hq, hd = q.shape
    kvh = k.shape[2]
    nb = s // Q_BLOCK
    qb = jnp.moveaxis(q.reshape(b, nb, Q_BLOCK, kvh, hq // kvh, hd), 1, 0)
    scale = hd ** -0.5

    def attend(q_blk):
        sc = jnp.einsum("bqngd,blnd->bngql", q_blk, k).astype(jnp.float32) * scale
        p = jax.nn.softmax(sc, axis=-1)
        return jnp.einsum("bngql,blnd->bqngd", p.astype(v.dtype), v)
    o = lax.map(attend, qb)
    return jnp.moveaxis(o, 0, 1).reshape(b, s, hq, hd)


def token_mixing(h, lp, init, ctx_kv, rope):
    b, s, _ = h.shape
    pa, pb, pc, pd, pg = split_last(h @ lp["w_in"], BLOCK_SIZES)
    y_a, st_a = rwkv7_mixer(pa, lp, init[0])
    y_b, st_b = gla_mixer(pb, lp, init[1])
    y_c, st_c = delta_mixer(pc, lp, init[2])
    q, k, v = attn_qkv(pd, lp, rope)
    if ctx_kv is None:
        keys, vals = k, v
    else:
        keys = jnp.concatenate([ctx_kv[0].astype(k.dtype), k], axis=1)
        vals = jnp.concatenate([ctx_kv[1].astype(v.dtype), v], axis=1)
    y_d = block_attention(q, keys, vals).reshape(b, s, D_HEADS * D_HD)
    branches = jnp.stack([y_a, y_b, y_c, y_d.astype(jnp.float32)], axis=2).astype(h.dtype)
    up = jnp.einsum("bsmc,mcd->bsmd", branches, lp["w_branch"])
    gates = jax.nn.sigmoid(pg.reshape(b, s, N_BRANCH, D_MODEL))
    merged = jnp.sum(gates * up, axis=2)
    return merged @ lp["w_out"], (st_a, st_b, st_c, k, v)


def trunk_layer(x, mod, lp, init, ctx_kv, rope):
    m = mod[:, None]
    h = adaln(x, lp["norm_g"][0], m[:, :, 0], m[:, :, 1])
    x = x + 0.5 * m[:, :, 2] * swiglu(h, lp["ffn_w_in"][0], lp["ffn_w_out"][0])
    h = adaln(x, lp["norm_g"][1], m[:, :, 3], m[:, :, 4])
    y, aux = token_mixing(h, lp, init, ctx_kv, rope)
    x = x + m[:, :, 5] * y
    h = adaln(x, lp["norm_g"][2], m[:, :, 6], m[:, :, 7])
    x = x + 0.5 * m[:, :, 8] * swiglu(h, lp["ffn_w_in"][1], lp["ffn_w_out"][1])
    return x, aux


def setup_inputs(seed: int = 0) -> dict:
    key = jax.random.key(seed)
    ks = iter(jax.random.split(key, 48))
    nrm = lambda shape, scale: scale * jax.random.normal(next(ks), shape, jnp.float32)
    uni = lambda shape, lo, hi: jax.random.uniform(next(ks), shape, jnp.float32, lo, hi)
    dt = jnp.exp(uni((DEPTH, 2, C_HEADS), float(np.log(1e-3)), float(np.log(1e-1))))
    return {
        "x_prompt": nrm((BATCH, SEQ, D_MODEL), 1.0),
        "x_sample": nrm((DEC_BATCH, DEC_SEQ, D_MODEL), 1.0),
        "c": nrm((DEC_BATCH, D_MODEL), 1.0),
        "cache_attn_k": nrm((DEC_BATCH, DEPTH, PAST_LEN, D_KV_HEADS, D_HD), 1.0),
        "cache_attn_v": nrm((DEC_BATCH, DEPTH, PAST_LEN, D_KV_HEADS, D_HD), 1.0),
        "state_rwkv": nrm((DEC_BATCH, DEPTH, 2, A_HEADS, A_N, A_N), 0.1),
        "state_gla": nrm((DEC_BATCH, DEPTH, 2, B_HEADS, B_DK, B_DV), 0.3),
        "state_delta": nrm((DEC_BATCH, DEPTH, 2, C_HEADS, C_DK, C_DV), 0.1),
        "c_ctx": nrm((D_MODEL,), 1.0),
        "norm_g": 1.0 + nrm((DEPTH, 3, D_MODEL), 0.02),
        "w_mod": nrm((DEPTH, D_MODEL, N_MOD * D_MODEL), 0.5 * D_MODEL ** -0.5),
        "b_mod": nrm((DEPTH, N_MOD * D_MODEL), 0.02),
        "ffn_w_in": nrm((DEPTH, 2, D_MODEL, 2 * D_FF), D_MODEL ** -0.5),
        "ffn_w_out": nrm((DEPTH, 2, D_FF, D_MODEL), D_FF ** -0.5),
        "w_in": nrm((DEPTH, D_MODEL, N_IN), D_MODEL ** -0.5),
        "rwkv_mu": uni((DEPTH, A_COLS), 0.0, 1.0),
        "rwkv_w0": uni((DEPTH, 2, A_W), -6.0, -1.0),
        "rwkv_w_up": nrm((DEPTH, 2, A_LORA_W, A_W), 0.5 * A_LORA_W ** -0.5),
        "rwkv_a0": nrm((DEPTH, 2, A_W), 0.1),
        "rwkv_a_up": nrm((DEPTH, 2, A_LORA_A, A_W), A_LORA_A ** -0.5),
        "rwkv_g_up": nrm((DEPTH, A_LORA_G, A_W), A_LORA_G ** -0.5),
        "rwkv_k_k": 0.85 + nrm((DEPTH, A_W), 0.02),
        "rwkv_k_a": 1.0 + nrm((DEPTH, A_W), 0.02),
        "rwkv_r_k": nrm((DEPTH, A_HEADS, A_N), 0.1),
        "rwkv_ln_g": 1.0 + nrm((DEPTH, A_W), 0.02),
        "rwkv_ln_b": nrm((DEPTH, A_W), 0.02),
        "gla_alpha_up": nrm((DEPTH, 2, B_LORA, B_QK), B_LORA ** -0.5),
        "gla_alpha_b": nrm((DEPTH, 2, B_QK), 0.1),
        "gla_norm_g": 1.0 + nrm((DEPTH, B_DV), 0.02),
        "delta_conv": nrm((DEPTH, CONV_W, C_QKV), CONV_W ** -0.5),
        "delta_a_log": jnp.log(uni((DEPTH, 2, C_HEADS), 1.0, 16.0)),
        "delta_dt_bias": dt + jnp.log(-jnp.expm1(-dt)),
        "delta_norm_g": 1.0 + nrm((DEPTH, C_DV), 0.02),
        "attn_q_norm": 1.0 + nrm((DEPTH, D_HD), 0.02),
        "attn_k_norm": 1.0 + nrm((DEPTH, D_HD), 0.02),
        "w_branch": nrm((DEPTH, N_BRANCH, BRANCH_W, D_MODEL), BRANCH_W ** -0.5),
        "w_out": nrm((DEPTH, D_MODEL, D_MODEL), D_MODEL ** -0.5),
        "final_norm_g": 1.0 + nrm((D_MODEL,), 0.02),
    }


def reference(x_prompt, x_sample, c, cache_attn_k, cache_attn_v, state_rwkv, state_gla, state_delta, c_ctx,
              norm_g, w_mod, b_mod, ffn_w_in, ffn_w_out, w_in, rwkv_mu, rwkv_w0, rwkv_w_up, rwkv_a0, rwkv_a_up,
              rwkv_g_up, rwkv_k_k, rwkv_k_a, rwkv_r_k, rwkv_ln_g, rwkv_ln_b, gla_alpha_up, gla_alpha_b,
              gla_norm_g, delta_conv, delta_a_log, delta_dt_bias, delta_norm_g, attn_q_norm, attn_k_norm,
              w_branch, w_out, final_norm_g):
    stacked = dict(norm_g=norm_g, w_mod=w_mod, b_mod=b_mod, ffn_w_in=ffn_w_in, ffn_w_out=ffn_w_out, w_in=w_in,
                   rwkv_mu=rwkv_mu, rwkv_w0=rwkv_w0, rwkv_w_up=rwkv_w_up, rwkv_a0=rwkv_a0, rwkv_a_up=rwkv_a_up,
                   rwkv_g_up=rwkv_g_up, rwkv_k_k=rwkv_k_k, rwkv_k_a=rwkv_k_a, rwkv_r_k=rwkv_r_k,
                   rwkv_ln_g=rwkv_ln_g, rwkv_ln_b=rwkv_ln_b, gla_alpha_up=gla_alpha_up, gla_alpha_b=gla_alpha_b,
                   gla_norm_g=gla_norm_g, delta_conv=delta_conv, delta_a_log=delta_a_log,
                   delta_dt_bias=delta_dt_bias, delta_norm_g=delta_norm_g, attn_q_norm=attn_q_norm,
                   attn_k_norm=attn_k_norm, w_branch=w_branch, w_out=w_out)
    layers = [{name: arr[l] for name, arr in stacked.items()} for l in range(DEPTH)]

    bp = x_prompt.shape[0]
    zero_init = (jnp.zeros((bp, 2, A_HEADS, A_N, A_N), jnp.float32),
                 jnp.zeros((bp, 2, B_HEADS, B_DK, B_DV), jnp.float32),
                 jnp.zeros((bp, 2, C_HEADS, C_DK, C_DV), jnp.float32))
    x = x_prompt
    ks, vs, srs, sgs, sds = [], [], [], [], []
    for l in range(DEPTH):
        lp = layers[l]
        mod = modulation(c_ctx[None, :], lp["w_mod"], lp["b_mod"])
        x, (s_r, s_g, s_d, k, v) = trunk_layer(x, mod, lp, zero_init, None, None)
        ks.append(k)
        vs.append(v)
        srs.append(s_r)
        sgs.append(s_g)
        sds.append(s_d)
    y_prompt = rms_norm(x, final_norm_g)
    new_cache_attn_k = jnp.stack(ks, axis=1)
    new_cache_attn_v = jnp.stack(vs, axis=1)
    new_state_rwkv = jnp.stack(srs, axis=1)
    new_state_gla = jnp.stack(sgs, axis=1)
    new_state_delta = jnp.stack(sds, axis=1)

    rope = axial_rope_tables(x_sample.shape[1])
    x = x_sample
    for l in range(DEPTH):
        lp = layers[l]
        mod = modulation(c, lp["w_mod"], lp["b_mod"])
        init = (state_rwkv[:, l], state_gla[:, l], state_delta[:, l])
        x, _ = trunk_layer(x, mod, lp, init, (cache_attn_k[:, l], cache_attn_v[:, l]), rope)
    y_sample = rms_norm(x, final_norm_g)

    return (y_prompt, y_sample, new_cache_attn_k, new_cache_attn_v, new_state_rwkv, new_state_gla, new_state_delta)
```

```python
import os
import numpy as np
import concourse.bass as bass
import concourse.mybir as mybir
from concourse.bass_utils import run_bass_kernel_spmd

F32 = mybir.dt.float32
F32R = mybir.dt.float32r
BF16 = mybir.dt.bfloat16
AF = mybir.ActivationFunctionType
ALU = mybir.AluOpType
AX = mybir.AxisListType

ENGS = ("pe", "act", "dve", "pool", "sp")

D = 1024
NT = 1536
TS = 1024
TP = 256
DFF = 2816
NIN = 7632
DEPTH = 2
import os
SIMMODE = bool(os.environ.get('SIMMODE'))
EMBED_WAIT = os.environ.get('KEMBED', '1') == '1'
POOL2DVE = os.environ.get('KP2D', '1') == '1'
MIXERS_OFF = tuple(int(c) for c in os.environ.get('KOFF', ''))
A0, B0, C0, D0, G0 = 0, 1184, 1984, 3024, 3536


class Tile:
    __slots__ = ("ap", "w", "r", "name", "excl")

    def __init__(self, ap, name="", excl=False):
        self.ap = ap
        self.w = {}
        self.r = {}
        self.name = name
        self.excl = excl

    def __getitem__(self, k):
        return self.ap[k]


class Prog:
    def __init__(self, nc, n_dma_sems=96, n_q_sems=0):
        self.nc = nc
        self.ops = {e: [] for e in ENGS}
        self.cnt = {e: 0 for e in ENGS}
        self.sems = {e: nc.alloc_semaphore("s_" + e) for e in ENGS}
        self.dsems = [nc.alloc_semaphore("d%d" % i) for i in range(n_dma_sems)]
        self.dcnt = [0] * n_dma_sems
        self.dnext = 0
        self.wnext = 0
        self.qsems = [nc.alloc_semaphore("q%d" % i) for i in range(n_q_sems)]
        self.qgen = [0] * n_q_sems
        self.qnext = 0
        self.known = {e: {} for e in ENGS}
        self.snap = {e: {} for e in ENGS}
        self.ninst = 0
        self.nwaits = 0
        self._nm = 0
        self.debug = bool(os.environ.get("KDEBUG"))
        self.names = {}

    def _name(self, name):
        self._nm += 1
        return "%s_%d" % (name or "t", self._nm)

    def sb(self, shape, dtype=F32, name=None):
        t = self.nc.alloc_sbuf_tensor(self._name(name), list(shape), dtype)
        return Tile(t[:], name or "sb")

    def ps(self, shape, dtype=F32, name=None):
        t = self.nc.alloc_psum_tensor(self._name(name), list(shape), dtype)
        return Tile(t[:], name or "ps", excl=True)

    def _collect(self, eng, reads, writes):
        deps = {}
        for t in reads:
            for k, v in t.w.items():
                if deps.get(k, 0) < v:
                    deps[k] = v
            if t.excl:
                for k, v in t.r.items():
                    if k != eng and deps.get(k, 0) < v:
                        deps[k] = v
        for t in writes:
            for d in (t.w, t.r):
                for k, v in d.items():
                    if deps.get(k, 0) < v:
                        deps[k] = v
        kn = self.known[eng]
        waits = []
        for k, v in sorted(deps.items(), key=lambda kv: -kv[1] if isinstance(kv[0], str) else 0):
            if eng == "pe" and k == "pe":
                continue
            if kn.get(k, 0) < v:
                kn[k] = v
                waits.append((k, v))
                if isinstance(k, str) and k != eng:
                    snap = self.snap[k].get(v)
                    if snap:
                        for k2, v2 in snap.items():
                            if kn.get(k2, 0) < v2:
                                kn[k2] = v2
        return waits

    def _mark(self, tok, reads, writes):
        k, v = tok
        for t in writes:
            t.w = {k: v}
            t.r = {}
        for t in reads:
            if t.r.get(k, 0) < v:
                t.r[k] = v

    def op(self, eng, fn, reads=(), writes=(), inc=True):
        if eng == "pool" and POOL2DVE:
            eng = "dve"
        if self.debug:
            import sys as _s
            f = _s._getframe(1)
            if f.f_code.co_name in ("mm", "tr"):
                f = f.f_back
            fn._tag = "%s:%d r=%s w=%s" % (f.f_code.co_name, f.f_lineno, [t.name for t in reads], [t.name for t in writes])
        waits = self._collect(eng, reads, writes)
        if inc:
            self.cnt[eng] += 1
            tok = (eng, self.cnt[eng])
            self.ops[eng].append((waits, fn, (eng, 1)))
            self.snap[eng][self.cnt[eng]] = {k: v for k, v in self.known[eng].items() if isinstance(k, str)}
        else:
            assert eng == "pe"
            tok = (eng, self.cnt[eng] + 1)
            self.ops[eng].append((waits, fn, None))
        self._mark(tok, reads, writes)
        return tok

    def dma(self, eng, out, in_, reads=(), writes=(), **kw):
        if eng == "pool" and SIMMODE:
            eng = "sp"
        waits = self._collect(eng, reads, writes)
        if kw.pop("wpool", False):
            i = self.wnext
            self.wnext = (self.wnext + 1) % 32
        else:
            i = 32 + self.dnext
            self.dnext = (self.dnext + 1) % (len(self.dsems) - 32)
        self.dcnt[i] += 16
        tok = (("d", i), self.dcnt[i])
        self.ops[eng].append((waits, lambda e: e.dma_start(out=out, in_=in_, **kw), (("d", i), 16)))
        self._mark(tok, reads, writes)
        return tok

    def _sem(self, k):
        if isinstance(k, str):
            return self.sems[k]
        return self.qsems[k[1]] if k[0] == "q" else self.dsems[k[1]]

    def _dma_targets(self):
        t = [(("d", i), c) for i, c in enumerate(self.dcnt) if c]
        t += [(("q", i, g), 16) for i, g in enumerate(self.qgen) if g]
        return t

    def barrier(self):
        targets = self._dma_targets()
        targets += [(e, self.cnt[e]) for e in ENGS if self.cnt[e]]
        for e in ENGS:
            kn = self.known[e]
            waits = []
            for k, v in targets:
                if k == e:
                    continue
                if kn.get(k, 0) < v:
                    kn[k] = v
                    waits.append((k, v))
            if waits:
                self.ops[e].append((waits, None, None))

    def finish(self):
        waits = self._dma_targets()
        for e in ENGS:
            if e != "sp" and self.cnt[e]:
                waits.append((e, self.cnt[e]))
        self.ops["sp"].append((waits, None, None))

    def check(self):
        val = {}
        pos = {e: 0 for e in ENGS}
        progress = True
        while progress:
            progress = False
            for e in ENGS:
                q = self.ops[e]
                while pos[e] < len(q):
                    waits, fn, inc = q[pos[e]]
                    if any(val.get(k, 0) < v for k, v in waits):
                        break
                    if inc is not None:
                        val[inc[0]] = val.get(inc[0], 0) + inc[1]
                    pos[e] += 1
                    progress = True
        bad = {e: (pos[e], len(self.ops[e])) for e in ENGS if pos[e] < len(self.ops[e])}
        for e, (i, n) in bad.items():
            waits = self.ops[e][i][0]
            print("DEADLOCK", e, "op", i, "of", n, [(k, v, val.get(k, 0)) for k, v in waits if val.get(k, 0) < v],
                  self.ops[e][i][3] if len(self.ops[e][i]) > 3 else "")
        return not bad

    def emit(self):
        nc = self.nc
        prog = self
        assert self.check(), "deadlock in schedule"

        def replay(name, e):
            for waits, fn, inc in prog.ops[name]:
                emb = None
                if fn is not None and waits and EMBED_WAIT:
                    emb = waits[-1]
                    waits = waits[:-1]
                for k, v in waits:
                    e.wait_ge(prog._sem(k), v)
                    prog.nwaits += 1
                if fn is not None:
                    ins = fn(e)
                    if emb is not None:
                        ins._wait_ge(prog._sem(emb[0]), emb[1])
                    if prog.debug:
                        try:
                            prog.names[ins.ins.name] = getattr(fn, "_tag", "")
                        except Exception:
                            pass
                    if inc is not None:
                        ins.then_inc(prog._sem(inc[0]), inc[1])
                    prog.ninst += 1

        with nc.Block() as block:
            @block.tensor
            def _(e):
                replay("pe", e)

            @block.scalar
            def _(e):
                replay("act", e)

            @block.vector
            def _(e):
                replay("dve", e)

            @block.gpsimd
            def _(e):
                replay("pool", e)

            @block.sync
            def _(e):
                replay("sp", e)


class Builder:
    def __init__(self, stop_after=None, dumps=()):
        self.stop_after = stop_after
        self.want_dumps = set(dumps)
        self.dump_list = []
        nc = bass.Bass("TRN2", target_bir_lowering=False)
        self.nc = nc
        self.p = Prog(nc)
        self.inp = {}
        self.out = {}
        self.bf16_inputs = set()
        self._evac_rr = 0

    def arena_init(self, cols):
        self.arena = self.nc.alloc_sbuf_tensor("arena", [128, cols], F32)[:]
        self.arena_cols = cols
        self.arena_off = 0

    def arena_reset(self):
        self.p.barrier()
        self.arena_off = 0

    def al(self, shape, dtype=F32, name="a"):
        n = int(np.prod(shape[1:]))
        if dtype == BF16:
            assert n % 2 == 0
            n //= 2
        assert self.arena_off + n <= self.arena_cols, (name, self.arena_off, n, self.arena_cols)
        ap = self.arena[0:shape[0], self.arena_off:self.arena_off + n]
        self.arena_off += n
        if dtype != F32:
            ap = ap.bitcast(dtype)
        if len(shape) == 3:
            ap = ap.rearrange("p (a b) -> p a b", a=shape[1])
        return Tile(ap, name)

    def din(self, name, shape, cast=False):
        dt_ = BF16 if (cast and SIMMODE) else F32
        if cast and SIMMODE:
            self.bf16_inputs.add(name)
        ap = self.nc.dram_tensor(name, list(shape), dt_, kind="ExternalInput").ap()
        self.inp[name] = ap
        return ap

    def dout(self, name, shape):
        ap = self.nc.dram_tensor(name, list(shape), F32, kind="ExternalOutput").ap()
        self.out[name] = ap
        return ap

    def dump(self, name, tile, shape):
        if name not in self.want_dumps:
            return
        if SIMMODE and tile.ap.dtype == BF16:
            o = self.nc.dram_tensor("dbg_" + name, list(shape), BF16, kind="ExternalOutput").ap()
            self.out["dbg_" + name] = o
        else:
            o = self.dout("dbg_" + name, shape)
        self.p.dma("pool", o, tile.ap, reads=[tile])

    def mm(self, out_t, out_ap, lhsT, rhs, start, stop, reads, inc=None):
        self.p.op("pe", lambda e: e.matmul(out_ap, lhsT, rhs, start=start, stop=stop),
                  reads=reads, writes=[out_t], inc=(stop if inc is None else inc))

    def tr(self, out_t, out_ap, in_ap, reads, inc=True):
        ident = self.ident
        kp = in_ap.shape[0]
        self.p.op("pe", lambda e: e.transpose(out_ap, in_ap, ident[0:kp, 0:kp]), reads=list(reads) + [ident],
                  writes=[out_t], inc=inc)


def build_program(stop_after=None, dumps=()):
    B = Builder(stop_after, dumps)
    nc, p = B.nc, B.p

    x_tm = B.din("x_tm", [NT, D])
    c2 = B.din("c2", [2, D])
    norm_g = B.din("norm_g", [DEPTH, 3, D])
    w_mod = B.din("w_mod", [DEPTH, D, 9 * D])
    b_mod = B.din("b_mod", [DEPTH, 9 * D])
    ffn_w_in = B.din("ffn_w_in", [DEPTH, 2, D, 2 * DFF], cast=True)
    ffn_w_out = B.din("ffn_w_out", [DEPTH, 2, DFF, D], cast=True)
    final_norm_g = B.din("final_norm_g", [D])
    c_ident = B.din("c_ident", [128, 128])
    c_ones = B.din("c_ones", [128, 128])
    y_tm = B.dout("y_tm", [NT, D])
    w_in = B.din("w_in", [DEPTH, D, NIN], cast=True)
    w_branch = B.din("w_branch", [DEPTH, 4, 256, D], cast=True)
    w_out = B.din("w_out", [DEPTH, D, D], cast=True)
    attn_q_norm = B.din("attn_q_norm", [DEPTH, 64])
    attn_k_norm = B.din("attn_k_norm", [DEPTH, 64])
    cache_k = B.din("cache_k", [DEPTH, 256, 128])
    cache_v = B.din("cache_v", [DEPTH, 256, 128], cast=True)
    c_rot = B.din("c_rot", [64, 64])
    c_cos = B.din("c_cos", [64, TS])
    c_sin = B.din("c_sin", [64, TS])
    gla_alpha_up = B.din("gla_alpha_up", [DEPTH, 2, 16, 128])
    gla_alpha_b = B.din("gla_alpha_b", [DEPTH, 2, 128])
    gla_norm_g = B.din("gla_norm_g", [DEPTH, 64])
    state_gla = B.din("state_gla", [DEPTH, 2, 4, 32, 64])
    c_tri = B.din("c_tri", [2, 128, 3, 128])
    c_mult = B.din("c_mult", [2, 128, 4, 128])
    c_ms = B.din("c_ms", [2, 128, 128])
    c_nm = B.din("c_nm", [2, 128, 3, 128])
    st_gla = B.dout("st_gla", [2, DEPTH, 2, 4, 32, 64])
    delta_conv = B.din("delta_conv", [DEPTH, 3, 768])
    delta_a_log = B.din("delta_a_log", [DEPTH, 2, 4])
    delta_dt_bias = B.din("delta_dt_bias", [DEPTH, 2, 4])
    delta_norm_g = B.din("delta_norm_g", [DEPTH, 64])
    state_delta = B.din("state_delta", [DEPTH, 2, 4, 64, 64])
    c_sel = B.din("c_sel", [16, 16, 64])
    st_delta = B.dout("st_delta", [2, DEPTH, 2, 4, 64, 64])
    rwkv_mu = B.din("rwkv_mu", [DEPTH, 1184])
    rwkv_w0 = B.din("rwkv_w0", [DEPTH, 2, 256])
    rwkv_w_up = B.din("rwkv_w_up", [DEPTH, 2, 64, 256])
    rwkv_a0 = B.din("rwkv_a0", [DEPTH, 2, 256])
    rwkv_a_up = B.din("rwkv_a_up", [DEPTH, 2, 64, 256])
    rwkv_g_up = B.din("rwkv_g_up", [DEPTH, 160, 256])
    rwkv_k_k = B.din("rwkv_k_k", [DEPTH, 256])
    rwkv_k_a = B.din("rwkv_k_a", [DEPTH, 256])
    rwkv_r_k = B.din("rwkv_r_k", [DEPTH, 4, 64])
    rwkv_ln_g = B.din("rwkv_ln_g", [DEPTH, 256])
    rwkv_ln_b = B.din("rwkv_ln_b", [DEPTH, 256])
    state_rwkv = B.din("state_rwkv", [DEPTH, 2, 4, 64, 64])
    st_rwkv = B.dout("st_rwkv", [2, DEPTH, 2, 4, 64, 64])
    kc_out = B.dout("kc_out", [2, DEPTH, 256, 128])
    vc_out = B.dout("vc_out", [2, DEPTH, 256, 128])

    B.ident = p.sb([128, 128], F32, "ident")
    B.ones = p.sb([128, 128], F32, "ones")
    B.eps6 = p.sb([128, 1], F32, "eps6")
    p.op("dve", lambda e: e.memset(B.eps6.ap, 1e-6), writes=[B.eps6])
    B.one1 = p.sb([128, 1], F32, "one1")
    p.op("dve", lambda e: e.memset(B.one1.ap, 1.0), writes=[B.one1])
    p.dma("sp", B.ident.ap, c_ident, writes=[B.ident])
    p.dma("sp", B.ones.ap, c_ones, writes=[B.ones])

    w_in_bf = nc.dram_tensor("w_in_bf", [DEPTH, D, NIN], BF16).ap()
    wbfT = Tile(w_in_bf, "w_in_bf")
    for l in range(DEPTH):
        src_v = w_in[l].rearrange("(kt p) c -> p kt c", p=128)
        dst_v = w_in_bf[l].rearrange("(kt p) c -> p kt c", p=128)
        for c0_ in range(0, NIN, 954):
            p.dma("pool", dst_v[:, :, c0_:c0_ + 954], src_v[:, :, c0_:c0_ + 954], writes=[wbfT])

    PS = [p.ps([128, 512], F32, "bank%d" % i) for i in range(8)]
    rr = {"b": 0, "n": 8}

    def nb():
        rr["b"] = (rr["b"] + 1) % rr["n"]
        return PS[rr["b"]]

    ones_b = p.sb([128, 64], BF16, "ones_b")
    p.op("dve", lambda e: e.memset(ones_b.ap, 1.0), writes=[ones_b])
    rot_t = p.sb([64, 64], F32, "rot_t")
    p.dma("sp", rot_t.ap, c_rot, writes=[rot_t])
    qkng = p.sb([64, 2 * DEPTH], F32, "qkng")
    for l in range(DEPTH):
        p.dma("sp", qkng[:, 2 * l:2 * l + 1], attn_q_norm[l:l + 1, :].rearrange("o d -> d o"), writes=[qkng], allow_slow_non_contiguous=True)
        p.dma("sp", qkng[:, 2 * l + 1:2 * l + 2], attn_k_norm[l:l + 1, :].rearrange("o d -> d o"), writes=[qkng], allow_slow_non_contiguous=True)
    qkng2 = p.sb([64, 2 * DEPTH], F32, "qkng2")
    p.op("dve", lambda e: e.tensor_copy(out=qkng2.ap, in_=qkng.ap), reads=[qkng], writes=[qkng2])
    for l in range(DEPTH):
        p.op("dve", lambda e, l=l: e.tensor_scalar(out=qkng2[:, 2 * l:2 * l + 1], in0=qkng[:, 2 * l:2 * l + 1], scalar1=0.125, scalar2=None, op0=ALU.mult),
             reads=[qkng], writes=[qkng2])

    xT = [p.sb([128, 8, 512], F32, "xT%d" % tb) for tb in range(3)]
    rstd = [p.sb([128, 512], F32, "rstd%d" % tb) for tb in range(3)]
    B.arena_init(37000)
    stage_tm = [B.al([128, D], F32, "stage_tm%d" % i) for i in range(2)]

    for tt in range(12):
        st = stage_tm[tt % 2]
        p.dma("sp", st.ap, x_tm[tt * 128:(tt + 1) * 128, :], writes=[st])
        tb, tq = divmod(tt, 4)
        for half in range(2):
            bank = PS[(tt * 2 + half) % 8]
            for j in range(4):
                ft = half * 4 + j
                B.tr(bank, bank[:, j * 128:(j + 1) * 128], st[:, ft * 128:(ft + 1) * 128], [st], inc=(j == 3))
            dst = xT[tb][:, half * 4:half * 4 + 4, tq * 128:(tq + 1) * 128]
            src = bank.ap.rearrange("p (j t) -> p j t", j=4)
            if half == 0:
                p.op("dve", lambda e, dst=dst, src=src: e.tensor_copy(out=dst, in_=src), reads=[bank], writes=[xT[tb]])
            else:
                p.op("act", lambda e, dst=dst, src=src: e.copy(out=dst, in_=src), reads=[bank], writes=[xT[tb]])

    cT = p.sb([128, 8, 2], F32, "cT")
    for j in range(2):
        p.dma("sp", cT[:, :, j], c2[j, :].rearrange("(kt p) -> p kt", p=128), writes=[cT], allow_slow_non_contiguous=True)
    scT = p.sb([128, 8, 2], F32, "scT")
    p.op("act", lambda e: e.activation(out=scT.ap, in_=cT.ap, func=AF.Silu), reads=[cT], writes=[scT])
    modT = [p.sb([128, 72, 2], F32, "modT%d" % l) for l in range(DEPTH)]
    bmT = [p.sb([128, 72], F32, "bmT%d" % l) for l in range(DEPTH)]
    ngT = [p.sb([128, 24], F32, "ngT%d" % l) for l in range(DEPTH)]
    fngT = p.sb([128, 8], F32, "fngT")
    p.dma("sp", fngT.ap, final_norm_g.rearrange("(kt p) -> p kt", p=128), writes=[fngT], allow_slow_non_contiguous=True)
    B.arena_reset()
    wmod_st = [B.al([128, 8, 512], F32, "wmod_st%d" % i) for i in range(2)]
    for l in range(DEPTH):
        p.dma("sp", bmT[l].ap, b_mod[l, :].rearrange("(o p) -> p o", p=128), writes=[bmT[l]], allow_slow_non_contiguous=True)
        p.dma("sp", ngT[l].ap, norm_g[l].rearrange("m (kt p) -> p (m kt)", p=128), writes=[ngT[l]], allow_slow_non_contiguous=True)
        wv = w_mod[l].rearrange("(kt p) c -> p kt c", p=128)
        for ch in range(18):
            st = wmod_st[ch % 2]
            p.dma("sp" if ch % 2 == 0 else "act", st.ap, wv[:, :, ch * 512:(ch + 1) * 512], writes=[st])
            bank = PS[ch % 8]
            for ob in range(4):
                for kt in range(8):
                    B.mm(bank, bank[:, ob * 2:ob * 2 + 2], st[:, kt, ob * 128:(ob + 1) * 128], scT[:, kt, :],
                         kt == 0, kt == 7, [st, scT])
            o0 = ch * 4
            dst = modT[l][:, o0:o0 + 4, :]
            src = bank[:, 0:8].rearrange("p (o j) -> p o j", j=2)
            bias = bmT[l][:, o0:o0 + 4].unsqueeze(2).to_broadcast([128, 4, 2])
            p.op("dve", lambda e, dst=dst, src=src, bias=bias: e.tensor_tensor(out=dst, in0=src, in1=bias, op=ALU.add),
                 reads=[bank, bmT[l]], writes=[modT[l]])
    gmT = [p.sb([128, 24, 2], F32, "gmT%d" % l) for l in range(DEPTH)]
    gtT = [p.sb([128, 24, 2], F32, "gtT%d" % l) for l in range(DEPTH)]
    for l in range(DEPTH):
        for i in range(3):
            sc = modT[l][:, (3 * i + 1) * 8:(3 * i + 2) * 8, :]
            g = ngT[l][:, i * 8:(i + 1) * 8].unsqueeze(2).to_broadcast([128, 8, 2])
            dst = gmT[l][:, i * 8:(i + 1) * 8, :]
            p.op("dve", lambda e, dst=dst, sc=sc, g=g: e.scalar_tensor_tensor(out=dst, in0=sc, scalar=1.0, in1=g, op0=ALU.add, op1=ALU.mult),
                 reads=[modT[l], ngT[l]], writes=[gmT[l]])
            gt = modT[l][:, (3 * i + 2) * 8:(3 * i + 3) * 8, :]
            dst2 = gtT[l][:, i * 8:(i + 1) * 8, :]
            fac = 1.0 if i == 1 else 0.5
            p.op("dve", lambda e, dst2=dst2, gt=gt, fac=fac: e.tensor_scalar(out=dst2, in0=gt, scalar1=fac, scalar2=None, op0=ALU.mult),
                 reads=[modT[l]], writes=[gtT[l]])
    B.dump("modT0", modT[0], [128, 72, 2])
    B.dump("gmT0", gmT[0], [128, 24, 2])

    S = {}

    def norm_setup():
        S["sq_scr"] = [B.al([128, 512], F32, "sq_scr%d" % i) for i in range(2)]
        S["t_scr"] = [B.al([128, 512], F32, "t_scr%d" % i) for i in range(2)]

    def rms_stats(tb, bank):
        sq_scr = S["sq_scr"]
        for ft in range(8):
            s = sq_scr[ft % 2]
            src = xT[tb][:, ft, :]
            p.op("act", lambda e, s=s, src=src: e.activation(out=s.ap, in_=src, func=AF.Square), reads=[xT[tb]], writes=[s])
            B.mm(bank, bank.ap, B.ones.ap, s.ap, ft == 0, ft == 7, [B.ones, s], inc=True)
        r = rstd[tb]
        p.op("act", lambda e: e.activation(out=r.ap, in_=bank.ap, func=AF.Sqrt, bias=B.eps6[:, 0:1], scale=1.0 / D),
             reads=[bank, B.eps6], writes=[r])
        p.op("dve", lambda e: e.reciprocal(out=r.ap, in_=r.ap), reads=[r], writes=[r])

    def adaln(l, i, tb, h, bank, hap=None):
        hap = h.ap if hap is None else hap
        j = 0 if tb < 2 else 1
        rms_stats(tb, bank)
        t_scr = S["t_scr"]
        for ft in range(8):
            t = t_scr[ft % 2]
            src = xT[tb][:, ft, :]
            p.op("dve", lambda e, t=t, src=src: e.tensor_tensor(out=t.ap, in0=src, in1=rstd[tb].ap, op=ALU.mult),
                 reads=[xT[tb], rstd[tb]], writes=[t])
            gm = gmT[l][:, i * 8 + ft, j:j + 1]
            sh = modT[l][:, (3 * i) * 8 + ft, j:j + 1]
            dst = hap[:, ft, :]
            p.op("act", lambda e, dst=dst, t=t, gm=gm, sh=sh: e.activation(out=dst, in_=t.ap, func=AF.Identity, bias=sh, scale=gm),
                 reads=[t, gmT[l], modT[l]], writes=[h])

    cnt = {"w": 0, "o": 0}

    def ffn(l, f, i):
        B.arena_reset()
        norm_setup()
        hT = [B.al([128, 8, 512], BF16, "hT%d" % k) for k in range(3)]
        wg_st = [B.al([128, 8, 256], BF16, "wg_st%d" % k) for k in range(2)]
        wu_st = [B.al([128, 8, 256], BF16, "wu_st%d" % k) for k in range(2)]
        wo_st = [B.al([128, 22, 128], BF16, "wo_st%d" % k) for k in range(2)]
        aT = [B.al([128, 22, 512], BF16, "aT%d" % k) for k in range(3)]
        sg_scr = [B.al([128, 512], F32, "sg_scr%d" % k) for k in range(2)]
        win = ffn_w_in[l, f].rearrange("(kt p) c -> p kt c", p=128)
        wout = ffn_w_out[l, f].rearrange("(ft p) c -> p ft c", p=128)
        for tb in range(3):
            adaln(l, i, tb, hT[tb], PS[6])
        n = 0
        for ch in range(11):
            k = cnt["w"] % 2
            cnt["w"] += 1
            wg, wu = wg_st[k], wu_st[k]
            p.dma("pool", wg.ap, win[:, :, ch * 256:(ch + 1) * 256], writes=[wg])
            p.dma("pool", wu.ap, win[:, :, DFF + ch * 256:DFF + (ch + 1) * 256], writes=[wu])
            for half in range(2):
                fo = ch * 2 + half
                for tb in range(3):
                    h = hT[tb]
                    bg, bu = PS[(n % 2) * 2], PS[(n % 2) * 2 + 1]
                    for kt in range(8):
                        B.mm(bg, bg.ap, wg[:, kt, half * 128:(half + 1) * 128], h[:, kt, :], kt == 0, kt == 7, [wg, h])
                    for kt in range(8):
                        B.mm(bu, bu.ap, wu[:, kt, half * 128:(half + 1) * 128], h[:, kt, :], kt == 0, kt == 7, [wu, h])
                    s = sg_scr[n % 2]
                    n += 1
                    p.op("act", lambda e, s=s, bg=bg: e.activation(out=s.ap, in_=bg.ap, func=AF.Silu), reads=[bg], writes=[s])
                    dst = aT[tb][:, fo, :]
                    p.op("dve", lambda e, dst=dst, s=s, bu=bu: e.tensor_tensor(out=dst, in0=bu.ap, in1=s.ap, op=ALU.mult),
                         reads=[bu, s], writes=[aT[tb]])
        n = 0
        for ob in range(8):
            k = cnt["o"] % 2
            cnt["o"] += 1
            wo = wo_st[k]
            p.dma("pool", wo.ap, wout[:, :, ob * 128:(ob + 1) * 128], writes=[wo])
            for tb in range(3):
                j = 0 if tb < 2 else 1
                bank = PS[4 + n % 2]
                n += 1
                for ft in range(22):
                    B.mm(bank, bank.ap, wo[:, ft, :], aT[tb][:, ft, :], ft == 0, ft == 21, [wo, aT[tb]])
                gt = gtT[l][:, i * 8 + ob, j:j + 1]
                dst = xT[tb][:, ob, :]
                p.op("dve", lambda e, dst=dst, bank=bank, gt=gt: e.scalar_tensor_tensor(out=dst, in0=bank.ap, scalar=gt, in1=dst, op0=ALU.mult, op1=ALU.add),
                     reads=[bank, gtT[l], xT[tb]], writes=[xT[tb]])

    SEQS = [
        dict(name="a", T=TP, j=1, pi=0, hoff=0, xs=[(2, 0)]),
        dict(name="b", T=TP, j=1, pi=1, hoff=256, xs=[(2, 256)]),
        dict(name="s", T=TS, j=0, pi=None, hoff=0, xs=[(0, 0), (1, 0)]),
    ]
    wcnt = {"n": 0}

    def mixing(l):
        B.arena_reset()
        hT_ = B.al([128, 8, TS], BF16, "hT_")
        wsm = [B.al([128, 8, 64], BF16, "wsm%d" % k) for k in range(4)]
        ring = {"big": None, "nb": 0, "ns": 0}
        yT = B.al([128, 8, TS], BF16, "yT")
        ytmp = B.al([64, 4, 128], BF16, "ytmp")

        def yput(m, t0):
            ev = ytmp.ap.rearrange("p (t hh) n -> p hh t n", hh=2)
            p.dma("sp", yT[0:64, m * 2:m * 2 + 2, t0:t0 + 128], ev[:, 0, :, :], reads=[ytmp], writes=[yT])
            p.dma("sp", yT[64:128, m * 2:m * 2 + 2, t0:t0 + 128], ev[:, 1, :, :], reads=[ytmp], writes=[yT])

        mark = B.arena_off
        winv = w_in_bf[l].rearrange("(kt p) c -> p kt c", p=128)
        KMIX = int(os.environ.get("KMIX", "9"))

        def wbig():
            w = ring["big"][ring["nb"] % 2]
            ring["nb"] += 1
            return w

        def wload(col0, M):
            if M <= 64:
                w = wsm[ring["ns"] % 4]
                ring["ns"] += 1
            else:
                w = wbig()
            p.dma("sp", w[:, :, 0:M], winv[:, :, col0:col0 + M], reads=[wbfT], writes=[w], wpool=True)
            return w

        def proj(col0, M, sq, evac, ta=0, tb_=None):
            T = sq["T"]
            tb_ = T if tb_ is None else tb_
            w = wload(col0, M)
            for s0 in range(ta, tb_, 512):
                n = min(512, tb_ - s0)
                bank = nb()
                for kt in range(8):
                    B.mm(bank, bank[0:M, 0:n], w[:, kt, 0:M], hT_[:, kt, sq["hoff"] + s0:sq["hoff"] + s0 + n], kt == 0, kt == 7, [w, hT_])
                evac(bank, bank[0:M, 0:n], s0, n)

        def scan_pass(sq, dirn, dk, dv, lowrank, scalar, prep, S0, yout, ysum_in, post, cst):
            T = sq["T"]
            G = cst["G"]
            tri, mult, msk, nm = cst["tri"], cst["mult"], cst["ms"], cst["nm"]
            NI = cst["NI"]
            nblk = T // 128
            order = range(nblk) if dirn == 0 else range(nblk - 1, -1, -1)
            dkv = dk + dv
            order = list(order)
            pf = cst.get("pf", False)
            nxt_ops = prep(order[0] * 128, 0) if pf else None
            for oi, bi in enumerate(order):
                t0 = bi * 128
                ops = nxt_ops if pf else prep(t0, 0)
                if pf and oi + 1 < len(order):
                    nxt_ops = prep(order[oi + 1] * 128, (oi + 1) % 2)
                ld, q, k, v = ops["ld"], ops["q"], ops["k"], ops["v"]
                z, b = ops.get("z"), ops.get("b")
                bankA = nb()
                for h in range(4):
                    B.tr(bankA, bankA[:, h * dk:(h + 1) * dk], ld[:, h, :], [ld], inc=(h == 3))
                p.op("act", lambda e, bankA=bankA: e.copy(out=G["ldtm"].ap, in_=bankA[:, 0:4 * dk].rearrange("p (h d) -> p h d", h=4)),
                     reads=[bankA], writes=[G["ldtm"]])
                banks = []
                for vi in range(3):
                    bk = nb()
                    banks.append(bk)
                    for h in range(4):
                        B.mm(bk, bk[0:dk, h * 128:(h + 1) * 128], G["ldtm"][:, h, :], tri[:, vi, :], True, True, [G["ldtm"], tri], inc=(h == 3))
                b_in, b_ex, b_rev = banks
                v3 = lambda t: t.ap.rearrange("p h t -> p (h t)")
                p.op("act", lambda e, b_in=b_in: e.activation(out=v3(G["EGin"]), in_=b_in[0:dk, :], func=AF.Exp), reads=[b_in], writes=[G["EGin"]])
                p.op("act", lambda e, b_rev=b_rev: e.activation(out=v3(G["EGrev"]), in_=b_rev[0:dk, :], func=AF.Exp), reads=[b_rev], writes=[G["EGrev"]])
                if lowrank:
                    p.op("act", lambda e, b_ex=b_ex: e.activation(out=v3(G["EGex"]), in_=b_ex[0:dk, :], func=AF.Exp), reads=[b_ex], writes=[G["EGex"]])
                if scalar:
                    p.op("act", lambda e, b_in=b_in: e.copy(out=v3(G["GinB"]), in_=b_in[0:dk, :]), reads=[b_in], writes=[G["GinB"]])
                    p.op("act", lambda e, b_in=b_in: e.mul(out=v3(G["nGinB"]), in_=b_in[0:dk, :], mul=-1.0), reads=[b_in], writes=[G["nGinB"]])
                    p.op("act", lambda e, b_ex=b_ex: e.copy(out=v3(G["GexB"]), in_=b_ex[0:dk, :]), reads=[b_ex], writes=[G["GexB"]])
                else:
                    p.op("act", lambda e, b_in=b_in: e.activation(out=v3(G["EnGin"]), in_=b_in[0:dk, :], func=AF.Exp, scale=-1.0), reads=[b_in], writes=[G["EnGin"]])
                def ew(eng, dst, a, b_):
                    p.op(eng, lambda e: e.tensor_tensor(out=dst.ap, in0=a.ap, in1=b_.ap, op=ALU.mult), reads=[a, b_], writes=[dst])
                ew("pool", G["QtT"], q, G["EGin"])
                ew("dve", G["KdT"], k, G["EGrev"])
                if lowrank:
                    ew("pool", G["ZtT"], z, G["EGex"])
                    ew("dve", G["BdT"], b, G["EGrev"])
                def cp(eng, dst, src):
                    if eng == "act":
                        p.op("act", lambda e: e.copy(out=dst.ap, in_=src.ap), reads=[src], writes=[dst])
                    else:
                        p.op(eng, lambda e: e.tensor_copy(out=dst.ap, in_=src.ap), reads=[src], writes=[dst])
                if not scalar:
                    ew("pool", G["KrT"], k, G["EnGin"])
                    cp("pool", G["QLb"], G["QtT"])
                    if lowrank:
                        ew("pool", G["BrT"], b, G["EnGin"])
                        cp("act", G["ZLb"], G["ZtT"])
                    QL, KR, ZL, BR = G["QLb"], G["KrT"], G.get("ZLb"), G.get("BrT")
                else:
                    cp("pool", G["QLb"], q)
                    cp("act", G["KRb"], k)
                    if lowrank:
                        cp("pool", G["ZLb"], z)
                        cp("act", G["BRb"], b)
                    QL, KR, ZL, BR = G["QLb"], G["KRb"], G.get("ZLb"), G.get("BRb")
                bankV = nb()
                for h in range(4):
                    B.tr(bankV, bankV[:, h * dv:(h + 1) * dv], v[:, h, :], [v], inc=(h == 3))
                p.op("act", lambda e, bankV=bankV: e.copy(out=G["Vtm"].ap, in_=bankV[:, 0:4 * dv].rearrange("p (h d) -> p h d", h=4)), reads=[bankV], writes=[G["Vtm"]])
                bankK = nb()
                for h in range(4):
                    B.tr(bankK, bankK[:, h * dk:(h + 1) * dk], G["KdT"][:, h, :], [G["KdT"]], inc=(h == 3))
                p.op("dve", lambda e, bankK=bankK: e.tensor_copy(out=G["Kdtm"].ap, in_=bankK[:, 0:4 * dk].rearrange("p (h d) -> p h d", h=4)), reads=[bankK], writes=[G["Kdtm"]])
                if lowrank:
                    bankB = nb()
                    for h in range(4):
                        B.tr(bankB, bankB[:, h * dk:(h + 1) * dk], G["BdT"][:, h, :], [G["BdT"]], inc=(h == 3))
                    p.op("act", lambda e, bankB=bankB: e.copy(out=G["Bdtm"].ap, in_=bankB[:, 0:4 * dk].rearrange("p (h d) -> p h d", h=4)), reads=[bankB], writes=[G["Bdtm"]])
                    bankZ = nb()
                    for h in range(4):
                        B.tr(bankZ, bankZ[:, h * dk:(h + 1) * dk], G["ZtT"][:, h, :], [G["ZtT"]], inc=(h == 3))
                    p.op("dve", lambda e, bankZ=bankZ: e.tensor_copy(out=G["RHS"][:, :, 0:dk], in_=bankZ[:, 0:4 * dk].rearrange("p (h d) -> p h d", h=4)), reads=[bankZ], writes=[G["RHS"]])
                def head_gen(h, KR=KR, QL=QL, ZL=ZL, BR=BR, t0=t0):
                    sl = h % NI
                    M5 = G["mats%d" % sl]
                    LbT_t = G.get("LbT%d" % sl)
                    NT0 = G.get("NT0_%d" % sl)
                    bankM = nb()
                    B.mm(bankM, bankM[:, 0:128], KR[:, h, :], QL[:, h, :], True, True, [KR, QL], inc=not lowrank)
                    if lowrank:
                        B.mm(bankM, bankM[:, 128:256], BR[:, h, :], QL[:, h, :], True, True, [BR, QL], inc=False)
                        B.mm(bankM, bankM[:, 256:384], KR[:, h, :], ZL[:, h, :], True, True, [KR, ZL], inc=False)
                        B.mm(bankM, bankM[:, 384:512], BR[:, h, :], ZL[:, h, :], True, True, [BR, ZL], inc=True)
                        bankL = nb()
                        B.mm(bankL, bankL[:, 0:128], ZL[:, h, :], BR[:, h, :], True, True, [ZL, BR], inc=True)
                    if scalar:
                        bankE = nb()
                        c64 = G["c64"]
                        combos = [(G["nGinB"], G["GinB"], 0), (G["nGinB"], G["GexB"], 1)]
                        for ci, (La, Rb, ni) in enumerate(combos):
                            o = bankE[:, ci * 128:(ci + 1) * 128]
                            B.mm(bankE, o, La[:, h, :], c64.ap, True, False, [La, c64], inc=False)
                            B.mm(bankE, o, c64.ap, Rb[:, h, :], False, False, [Rb, c64], inc=False)
                            B.mm(bankE, o, B.ident.ap, nm[:, ni, :], False, True, [B.ident, nm], inc=False)
                        o = bankE[:, 256:384]
                        B.mm(bankE, o, G["GexB"][:, h, :], c64.ap, True, False, [G["GexB"], c64], inc=False)
                        B.mm(bankE, o, c64.ap, G["nGinB"][:, h, :], False, False, [G["nGinB"], c64], inc=False)
                        B.mm(bankE, o, B.ident.ap, nm[:, 2, :], False, True, [B.ident, nm], inc=True)
                        Em = G["Emat%d" % sl]
                        p.op("act", lambda e, bankE=bankE, Em=Em: e.activation(out=Em.ap.rearrange("p a t -> p (a t)"), in_=bankE[:, 0:384], func=AF.Exp), reads=[bankE], writes=[Em])
                        nmm = 2 if lowrank else 1
                        p.op("dve", lambda e, bankM=bankM, Em=Em, nmm=nmm: e.tensor_tensor(out=M5[:, 0:nmm, :], in0=bankM[:, 0:128 * nmm].rearrange("p (a t) -> p a t", a=nmm),
                                                                                 in1=Em[:, 0:1, :].to_broadcast([128, nmm, 128]), op=ALU.mult), reads=[bankM, Em], writes=[M5])
                        if lowrank:
                            p.op("dve", lambda e, bankM=bankM, Em=Em: e.tensor_tensor(out=M5[:, 2, :], in0=bankM[:, 256:384], in1=Em[:, 1, :], op=ALU.mult), reads=[bankM, Em], writes=[M5])
                            p.op("dve", lambda e, bankM=bankM, Em=Em: e.tensor_tensor(out=LbT_t.ap, in0=bankM[:, 384:512], in1=Em[:, 1, :], op=ALU.mult), reads=[bankM, Em], writes=[LbT_t])
                            p.op("dve", lambda e, bankL=bankL, Em=Em: e.tensor_tensor(out=NT0.ap, in0=bankL[:, 0:128], in1=Em[:, 2, :], op=ALU.mult), reads=[bankL, Em], writes=[NT0])
                    else:
                        nmm = 3 if lowrank else 1
                        p.op("dve", lambda e, bankM=bankM, nmm=nmm: e.tensor_tensor(out=M5[:, 0:nmm, :], in0=bankM[:, 0:128 * nmm].rearrange("p (a t) -> p a t", a=nmm),
                                                                             in1=mult[:, 0:nmm, :], op=ALU.mult), reads=[bankM, mult], writes=[M5])
                        if lowrank:
                            p.op("dve", lambda e, bankM=bankM: e.tensor_tensor(out=LbT_t.ap, in0=bankM[:, 384:512], in1=mult[:, 3, :], op=ALU.mult), reads=[bankM, mult], writes=[LbT_t])
                            p.op("dve", lambda e, bankL=bankL: e.tensor_tensor(out=NT0.ap, in0=bankL[:, 0:128], in1=msk.ap, op=ALU.mult), reads=[bankL, msk], writes=[NT0])
                    MkT = M5[:, 0, :]
                    MbT, LkT = (M5[:, 1, :], M5[:, 2, :]) if lowrank else (None, None)
                    LbT = LbT_t.ap if lowrank else None
                    yield
                    if lowrank:
                        Ncur, NTcur = None, NT0
                        R = G["R0_%d" % sl]
                        p.op("pool", lambda e, R=R, LbT=LbT: e.tensor_tensor(out=R.ap, in0=LbT, in1=B.ident.ap, op=ALU.add), reads=[LbT_t, B.ident], writes=[R])
                        N_ap, N_t = LbT, LbT_t
                        for lv in range(5):
                            if lv % 2 == 0:
                                Pn_ap, Pn_t, PTn = G["P0_%d" % sl].ap, G["P0_%d" % sl], G["PT0_%d" % sl]
                            else:
                                Pn_ap, Pn_t, PTn = LbT_t.ap, LbT_t, NT0
                            Rn = G["R%d_%d" % ((lv + 1) % 2, sl)]
                            bp = nb()
                            last = lv == 4
                            if not last:
                                B.mm(bp, bp[:, 0:128], NTcur.ap, N_ap, True, True, [NTcur, N_t], inc=False)
                            B.mm(bp, bp[:, 128:256], N_ap, NTcur.ap, True, True, [NTcur, N_t], inc=True)
                            if not last:
                                p.op("act", lambda e, bp=bp, Pn_ap=Pn_ap: e.copy(out=Pn_ap, in_=bp[:, 0:128]), reads=[bp], writes=[Pn_t])
                            p.op("act", lambda e, bp=bp, PTn=PTn: e.copy(out=PTn.ap, in_=bp[:, 128:256]), reads=[bp], writes=[PTn])
                            yield
                            br = nb()
                            B.mm(br, br[:, 0:128], PTn.ap, R.ap, True, True, [PTn, R], inc=True)
                            p.op("dve", lambda e, br=br, R=R, Rn=Rn: e.tensor_tensor(out=Rn.ap, in0=br[:, 0:128], in1=R.ap, op=ALU.add), reads=[br, R], writes=[Rn])
                            R = Rn
                            N_ap, N_t, NTcur = Pn_ap, Pn_t, PTn
                            yield
                        bl = nb()
                        B.mm(bl, bl[:, 0:dv], LkT, G["Vtm"][:, h, :], True, True, [M5, G["Vtm"]], inc=True)
                        p.op("act", lambda e, bl=bl, h=h: e.copy(out=G["RHS"][:, h, dk:dkv], in_=bl[:, 0:dv]), reads=[bl], writes=[G["RHS"]])
                        yield
                        bw = nb()
                        B.mm(bw, bw[:, 0:dkv], R.ap, G["RHS"][:, h, :], True, True, [R, G["RHS"]], inc=True)
                        WU = G["WU%d" % sl]
                        p.op("act", lambda e, bw=bw, WU=WU: e.copy(out=WU.ap, in_=bw[:, 0:dkv]), reads=[bw], writes=[WU])
                        yield
                        bq = nb()
                        B.mm(bq, bq[0:dk, 0:128], WU[:, 0:dk], MbT, True, True, [WU, M5], inc=True)
                        p.op("dve", lambda e, bq=bq, h=h: e.tensor_tensor(out=G["QeffT"][:, h, :], in0=bq[0:dk, 0:128], in1=G["QtT"][:, h, :], op=ALU.add),
                             reads=[bq, G["QtT"]], writes=[G["QeffT"]])
                        by = nb()
                        B.mm(by, by[0:dv, 0:128], WU[:, dk:dkv], MbT, True, False, [WU, M5], inc=False)
                        B.mm(by, by[0:dv, 0:128], G["Vtm"][:, h, :], MkT, False, True, [G["Vtm"], M5], inc=True)
                        p.op("act", lambda e, by=by, h=h: e.copy(out=G["Y0T"][:, h, :], in_=by[0:dv, 0:128]), reads=[by], writes=[G["Y0T"]])
                        for c in range(2):
                            r0 = 64 * c
                            gcol = G["EGin"][:, h, (r0 + 63 if dirn == 0 else r0):(r0 + 64 if dirn == 0 else r0 + 1)]
                            ba = nb()
                            B.mm(ba, ba[0:dk, 0:dk], WU[r0:r0 + 64, 0:dk], G["Bdtm"][r0:r0 + 64, h, :], True, True, [WU, G["Bdtm"]], inc=True)
                            p.op("dve", lambda e, ba=ba, gcol=gcol, h=h, c=c: e.scalar_tensor_tensor(out=G["AcT"][:, h * 2 + c, :], in0=B.ident[0:dk, 0:dk], scalar=gcol, in1=ba[0:dk, 0:dk],
                                                                                      op0=ALU.mult, op1=ALU.add), reads=[ba, G["EGin"], B.ident], writes=[G["AcT"]])
                            bb = nb()
                            B.mm(bb, bb[0:dk, 0:dv], G["Bdtm"][r0:r0 + 64, h, :], WU[r0:r0 + 64, dk:dkv], True, False, [WU, G["Bdtm"]], inc=False)
                            B.mm(bb, bb[0:dk, 0:dv], G["Kdtm"][r0:r0 + 64, h, :], G["Vtm"][r0:r0 + 64, h, :], False, True, [G["Kdtm"], G["Vtm"]], inc=True)
                            p.op("act", lambda e, bb=bb, h=h, c=c: e.copy(out=G["Bc"][:, h * 2 + c, :], in_=bb[0:dk, 0:dv]), reads=[bb], writes=[G["Bc"]])
                    else:
                        by = nb()
                        B.mm(by, by[0:dv, 0:128], G["Vtm"][:, h, :], MkT, True, True, [G["Vtm"], M5], inc=True)
                        p.op("act", lambda e, by=by, h=h: e.copy(out=G["Y0T"][:, h, :], in_=by[0:dv, 0:128]), reads=[by], writes=[G["Y0T"]])
                        for c in range(2):
                            r0 = 64 * c
                            bb = nb()
                            B.mm(bb, bb[0:dk, 0:dv], G["Kdtm"][r0:r0 + 64, h, :], G["Vtm"][r0:r0 + 64, h, :], True, True, [G["Kdtm"], G["Vtm"]], inc=True)
                            p.op("act", lambda e, bb=bb, h=h, c=c: e.copy(out=G["Bc"][:, h * 2 + c, :], in_=bb[0:dk, 0:dv]), reads=[bb], writes=[G["Bc"]])
                gens_all = [head_gen(h) for h in range(4)]
                for g0 in range(0, 4, NI):
                    live = gens_all[g0:g0 + NI]
                    while live:
                        for g_ in list(live):
                            try:
                                next(g_)
                            except StopIteration:
                                live.remove(g_)
                QE = G["QeffT"] if lowrank else G["QtT"]
                for c in ((0, 1) if dirn == 0 else (1, 0)):
                    r0 = 64 * c
                    bY = nb()
                    for h in range(4):
                        B.mm(bY, bY[0:dv, h * 64:(h + 1) * 64], S0[:, h, :], QE[:, h, r0:r0 + 64], True, True, [S0, QE], inc=(h == 3))
                    ysrc = bY[0:dv, 0:256].rearrange("p (h t) -> p h t", h=4)
                    if ysum_in is None:
                        p.op("dve", lambda e, ysrc=ysrc, r0=r0, t0=t0: e.tensor_tensor(out=yout[:, :, t0 + r0:t0 + r0 + 64], in0=ysrc, in1=G["Y0T"][:, :, r0:r0 + 64], op=ALU.add),
                             reads=[bY, G["Y0T"]], writes=[yout])
                    else:
                        p.op("dve", lambda e, ysrc=ysrc, r0=r0: e.tensor_tensor(out=G["ysum"][:, :, r0:r0 + 64], in0=ysrc, in1=G["Y0T"][:, :, r0:r0 + 64], op=ALU.add),
                             reads=[bY, G["Y0T"]], writes=[G["ysum"]])
                        p.op("pool", lambda e, r0=r0, t0=t0: e.tensor_tensor(out=G["ysum"][:, :, r0:r0 + 64], in0=G["ysum"][:, :, r0:r0 + 64], in1=ysum_in[:, :, t0 + r0:t0 + r0 + 64], op=ALU.add),
                             reads=[G["ysum"], ysum_in], writes=[G["ysum"]])
                    if lowrank:
                        bS = nb()
                        for h in range(4):
                            B.mm(bS, bS[0:dk, h * dv:(h + 1) * dv], G["AcT"][:, h * 2 + c, :], S0[:, h, :], True, True, [G["AcT"], S0], inc=(h == 3))
                        p.op("dve", lambda e, bS=bS, c=c: e.tensor_tensor(out=S0.ap, in0=bS[0:dk, 0:4 * dv].rearrange("p (h d) -> p h d", h=4),
                                                                        in1=G["Bc"].ap.rearrange("p (h c) d -> p h c d", c=2)[:, :, c, :], op=ALU.add),
                             reads=[bS, G["Bc"], S0], writes=[S0])
                    else:
                        for h in range(4):
                            gcol = G["EGin"][:, h, (r0 + 63 if dirn == 0 else r0):(r0 + 64 if dirn == 0 else r0 + 1)]
                            p.op("dve", lambda e, h=h, c=c, gcol=gcol: e.scalar_tensor_tensor(out=S0[:, h, :], in0=S0[:, h, :], scalar=gcol, in1=G["Bc"][:, h * 2 + c, :], op0=ALU.mult, op1=ALU.add),
                                 reads=[S0, G["EGin"], G["Bc"]], writes=[S0])
                if ysum_in is not None:
                    post(t0, G["ysum"], ops)

        def scan_alloc(dk, dv, lowrank, scalar):
            NI_ = 4 if lowrank else 1
            G = {}
            A = lambda nm_, shp, dt_=F32: G.__setitem__(nm_, B.al(shp, dt_, nm_))
            A("ldtm", [128, 4, dk])
            for nm_ in ("EGin", "EGrev", "QtT", "KdT"):
                A(nm_, [dk, 4, 128])
            if lowrank:
                for nm_ in ("EGex", "ZtT", "BdT"):
                    A(nm_, [dk, 4, 128])
            if scalar:
                for nm_ in ("GinB", "nGinB", "GexB"):
                    A(nm_, [dk, 4, 128])
                for sl_ in range(NI_):
                    A("Emat%d" % sl_, [128, 3, 128])
                A("c64", [dk, 128])
                A("KRb", [dk, 4, 128], BF16)
                if lowrank:
                    A("BRb", [dk, 4, 128], BF16)
                p.op("pool", lambda e: e.memset(G["c64"].ap, 1.0 / dk), writes=[G["c64"]])
            else:
                A("EnGin", [dk, 4, 128])
                A("KrT", [dk, 4, 128], BF16)
                if lowrank:
                    A("BrT", [dk, 4, 128], BF16)
            A("QLb", [dk, 4, 128], BF16)
            if lowrank:
                A("ZLb", [dk, 4, 128], BF16)
            A("Vtm", [128, 4, dv], BF16)
            A("Kdtm", [128, 4, dk], BF16)
            for sl_ in range(NI_):
                A("mats%d" % sl_, [128, 3 if lowrank else 1, 128] if not lowrank else [128, 3, 128], BF16) if lowrank else A("mats%d" % sl_, [128, 2, 128], BF16)
            A("Y0T", [dv, 4, 128])
            A("Bc", [dk, 8, dv])
            A("ysum", [dv, 4, 128])
            if lowrank:
                A("Bdtm", [128, 4, dk], BF16)
                A("RHS", [128, 4, dk + dv])
                for sl_ in range(NI_):
                    for nm_ in ("NT0", "R0", "R1", "P0", "PT0"):
                        A("%s_%d" % (nm_, sl_), [128, 128])
                    A("WU%d" % sl_, [128, dk + dv], BF16)
                    A("LbT%d" % sl_, [128, 128])
                A("QeffT", [dk, 4, 128])
                A("AcT", [dk, 8, dk])
            cst = dict(G=G, NI=NI_, tri=B.al([128, 3, 128], F32, "tri"),
                       mult=None if scalar else B.al([128, 4, 128], F32, "mult"),
                       ms=None if scalar else B.al([128, 128], F32, "ms"),
                       nm=B.al([128, 3, 128], F32, "nm") if scalar else None)
            return cst

        def load_dir_consts(cst, dirn):
            p.dma("sp", cst["tri"].ap, c_tri[dirn], writes=[cst["tri"]])
            if cst["mult"] is not None:
                p.dma("sp", cst["mult"].ap, c_mult[dirn], writes=[cst["mult"]])
                p.dma("sp", cst["ms"].ap, c_ms[dirn], writes=[cst["ms"]])
            if cst["nm"] is not None:
                p.dma("sp", cst["nm"].ap, c_nm[dirn], writes=[cst["nm"]])

        def gla(sq):
            T = sq["T"]
            dk, dv = 32, 64
            cst = scan_alloc(dk, dv, False, False)
            cst["pf"] = True
            yf = B.al([dv, 4, T], BF16, "yf")
            S0 = B.al([dk, 4, dv], F32, "S0")
            Pq_ = [B.al([dk, 4, 128], F32, "Pq%d" % k_) for k_ in range(2)]
            Pk_ = [B.al([dk, 4, 128], F32, "Pk%d" % k_) for k_ in range(2)]
            Pv_ = [B.al([dv, 4, 128], F32, "Pv%d" % k_) for k_ in range(2)]
            Pr_ = [B.al([dv, 4, 128], F32, "Pr%d" % k_) for k_ in range(2)]
            ad = B.al([16, 128], F32, "ad")
            ee = B.al([dk, 4, 128], F32, "ee")
            ld_ = [B.al([dk, 4, 128], F32, "ld%d" % k_) for k_ in range(2)]
            aup = B.al([16, 128], F32, "aup")
            nab = B.al([dk, 4], F32, "nab")
            gng = B.al([dv, 1], F32, "gng")
            sqs = B.al([dv, 512], F32, "sqs")
            rr_ = B.al([dv, 512], F32, "rr_")
            p.dma("sp", gng.ap, gla_norm_g[l:l + 1, :].rearrange("o d -> d o"), writes=[gng], allow_slow_non_contiguous=True)
            for dirn in range(2):
                load_dir_consts(cst, dirn)
                p.dma("sp", aup.ap, gla_alpha_up[l, dirn], writes=[aup])
                p.dma("sp", nab.ap, gla_alpha_b[l, dirn].rearrange("(h d) -> d h", d=dk), writes=[nab], allow_slow_non_contiguous=True)
                p.op("dve", lambda e: e.tensor_scalar(out=nab.ap, in0=nab.ap, scalar1=-1.0, scalar2=None, op0=ALU.mult), reads=[nab], writes=[nab])
                if sq["pi"] is None:
                    p.dma("sp", S0.ap, state_gla[l, dirn].rearrange("h k v -> k h v"), writes=[S0])
                else:
                    p.op("pool", lambda e: e.memset(S0.ap, 0.0), writes=[S0])

                def prep(t0, sl, dirn=dirn):
                    Pq, Pk, Pv, Pr, ld = Pq_[sl], Pk_[sl], Pv_[sl], Pr_[sl], ld_[sl]
                    for h in range(4):
                        proj(B0 + 32 * h, 32, sq, lambda bank, ap, s0, n, h=h: p.op("act", lambda e: e.mul(out=Pq[:, h, :], in_=ap, mul=float(32 ** -0.5)), reads=[bank], writes=[Pq]), t0, t0 + 128)
                        proj(B0 + 128 + 32 * h, 32, sq, lambda bank, ap, s0, n, h=h: p.op("act", lambda e: e.copy(out=Pk[:, h, :], in_=ap), reads=[bank], writes=[Pk]), t0, t0 + 128)
                        proj(B0 + 256 + 64 * h, 64, sq, lambda bank, ap, s0, n, h=h: p.op("dve", lambda e: e.tensor_copy(out=Pv[:, h, :], in_=ap), reads=[bank], writes=[Pv]), t0, t0 + 128)
                        if dirn == 1:
                            proj(B0 + 544 + 64 * h, 64, sq, lambda bank, ap, s0, n, h=h: p.op("act", lambda e: e.activation(out=Pr[:, h, :], in_=ap, func=AF.Silu), reads=[bank], writes=[Pr]), t0, t0 + 128)
                    proj(B0 + 512 + 16 * dirn, 16, sq, lambda bank, ap, s0, n: p.op("act", lambda e: e.copy(out=ad.ap, in_=ap), reads=[bank], writes=[ad]), t0, t0 + 128)
                    bk = nb()
                    for h in range(4):
                        B.mm(bk, bk[0:dk, h * 128:(h + 1) * 128], aup[:, h * dk:(h + 1) * dk], ad.ap, True, True, [aup, ad], inc=(h == 3))
                    for h in range(4):
                        p.op("act", lambda e, h=h, bk=bk: e.activation(out=ee[:, h, :], in_=bk[0:dk, h * 128:(h + 1) * 128], func=AF.Exp, bias=nab[:, h:h + 1], scale=-1.0),
                             reads=[bk, nab], writes=[ee])
                    p.op("act", lambda e: e.activation(out=ld.ap, in_=ee.ap, func=AF.Ln, bias=B.one1[0:dk, 0:1], scale=1.0), reads=[ee, B.one1], writes=[ld])
                    p.op("pool", lambda e: e.tensor_scalar(out=ld.ap, in0=ld.ap, scalar1=-1.0 / 16.0, scalar2=None, op0=ALU.mult), reads=[ld], writes=[ld])
                    return dict(ld=ld, q=Pq, k=Pk, v=Pv, r=Pr)

                def post(t0, ys, ops):
                    yv = ys.ap.rearrange("p h t -> p (h t)")
                    p.op("act", lambda e: e.activation(out=sqs.ap, in_=yv, func=AF.Square), reads=[ys], writes=[sqs])
                    bk = nb()
                    B.mm(bk, bk[0:dv, :], B.ones[0:dv, 0:dv], sqs.ap, True, True, [B.ones, sqs], inc=True)
                    p.op("act", lambda e, bk=bk: e.activation(out=rr_.ap, in_=bk[0:dv, :], func=AF.Sqrt, bias=B.eps6[0:dv, 0:1], scale=1.0 / dv), reads=[bk, B.eps6], writes=[rr_])
                    p.op("dve", lambda e: e.reciprocal(out=rr_.ap, in_=rr_.ap), reads=[rr_], writes=[rr_])
                    p.op("dve", lambda e: e.scalar_tensor_tensor(out=sqs.ap, in0=yv, scalar=gng[:, 0:1], in1=rr_.ap, op0=ALU.mult, op1=ALU.mult), reads=[ys, gng, rr_], writes=[sqs])
                    p.op("pool", lambda e: e.tensor_tensor(out=ytmp.ap, in0=sqs.ap.rearrange("p (h t) -> p h t", h=4), in1=ops["r"].ap, op=ALU.mult),
                         reads=[sqs, ops["r"]], writes=[ytmp])
                    yput(1, t0)

                scan_pass(sq, dirn, dk, dv, False, False, prep, S0, yf, None if dirn == 0 else yf, post, cst)
                if sq["pi"] is not None:
                    p.dma("sp", st_gla[sq["pi"], l, dirn].rearrange("h k v -> k h v"), S0.ap, reads=[S0])

        def rwkv(sq):
            T = sq["T"]
            dk, dv = 64, 64
            cst = scan_alloc(dk, dv, True, False)
            yf = B.al([dv, 4, T], BF16, "yf")
            S0 = B.al([dk, 4, dv], F32, "S0")
            raw = [B.al([64, 130], F32, "raw%d" % k_) for k_ in range(1)]
            lt_ = B.al([64, 128], F32, "lt_")
            Pr_ = [B.al([64, 4, 128], F32, "Pr%d" % k_) for k_ in range(1)]
            Pk_ = [B.al([64, 4, 128], F32, "Pk%d" % k_) for k_ in range(1)]
            Pv_ = [B.al([64, 4, 128], F32, "Pv%d" % k_) for k_ in range(1)]
            kk_ = [B.al([64, 4, 128], F32, "kk%d" % k_) for k_ in range(1)]
            ld_ = [B.al([64, 4, 128], F32, "ld%d" % k_) for k_ in range(1)]
            aa = B.al([64, 4, 128], F32, "aa")
            af_ = [B.al([64, 4, 128], F32, "af%d" % k_) for k_ in range(1)]
            bop_ = [B.al([64, 4, 128], F32, "bop%d" % k_) for k_ in range(1)]
            gT_ = [B.al([64, 4, 128], F32, "gT%d" % k_) for k_ in range(1)]
            wd = B.al([64, 128], F32, "wd")
            adt = B.al([64, 128], F32, "adt")
            adf = B.al([64, 128], F32, "adf")
            sgd = [B.al([64, 128], F32, "sgd%d" % k_) for k_ in range(3)]
            sqs = B.al([64, 512], F32, "sqs")
            rr_ = B.al([64, 512], F32, "rr_")
            wup = B.al([64, 256], F32, "wup")
            aup = B.al([64, 256], F32, "aup")
            aupf = B.al([64, 256], F32, "aupf")
            gup = [B.al([64, 256], F32, "gup%d" % k_) for k_ in range(3)]
            hm = B.al([64, 19], F32, "hm")
            omm = B.al([64, 19], F32, "omm")
            prm = B.al([64, 8, 4], F32, "prm")
            epsg = B.al([64, 1], F32, "epsg")
            stt_ap = sqs.ap[:, 0:256].rearrange("p (h d) -> p h d", h=4)
            p.op("pool", lambda e: e.memset(hm.ap, 0.0), writes=[hm])
            p.dma("sp", hm[:, 0:18], rwkv_mu[l, 0:1152].rearrange("(g d) -> d g", d=64), writes=[hm], allow_slow_non_contiguous=True)
            p.dma("sp", hm[0:32, 18:19], rwkv_mu[l, 1152:1184].rearrange("(g d) -> d g", d=32), writes=[hm], allow_slow_non_contiguous=True)
            p.op("dve", lambda e: e.tensor_scalar(out=omm.ap, in0=hm.ap, scalar1=-1.0, scalar2=1.0, op0=ALU.mult, op1=ALU.add), reads=[hm], writes=[omm])
            p.op("dve", lambda e: e.tensor_scalar(out=hm.ap, in0=hm.ap, scalar1=0.5, scalar2=None, op0=ALU.mult), reads=[hm], writes=[hm])
            hd = lambda apx: apx.rearrange("(h d) -> d h", d=64)
            p.dma("sp", prm[:, 0, :], hd(rwkv_k_k[l]), writes=[prm], allow_slow_non_contiguous=True)
            p.dma("sp", prm[:, 1, :], hd(rwkv_k_a[l]), writes=[prm], allow_slow_non_contiguous=True)
            p.dma("sp", prm[:, 2, :], rwkv_r_k[l].rearrange("h d -> d h"), writes=[prm], allow_slow_non_contiguous=True)
            p.dma("sp", prm[:, 5, :], hd(rwkv_a0[l, 0]), writes=[prm], allow_slow_non_contiguous=True)
            p.dma("sp", prm[:, 6, :], hd(rwkv_ln_g[l]), writes=[prm], allow_slow_non_contiguous=True)
            p.dma("sp", prm[:, 7, :], hd(rwkv_ln_b[l]), writes=[prm], allow_slow_non_contiguous=True)
            p.op("pool", lambda e: e.memset(epsg.ap, 64e-5), writes=[epsg])
            p.dma("sp", aupf.ap, rwkv_a_up[l, 0], writes=[aupf])
            p.dma("sp", gup[0].ap, rwkv_g_up[l, 0:64, :], writes=[gup[0]])
            p.dma("sp", gup[1].ap, rwkv_g_up[l, 64:128, :], writes=[gup[1]])
            p.dma("sp", gup[2][0:32, :], rwkv_g_up[l, 128:160, :], writes=[gup[2]])
            rawc = {"n": 0}
            bc = lambda col: prm[:, col, :].unsqueeze(2).to_broadcast([64, 4, 128])

            def lerp_group(col0, M, g, dst_ap, dst_t, t0):
                r = raw[0]
                rawc["n"] += 1
                ta, tb_ = max(t0 - 1, 0), min(t0 + 129, T)
                o = ta - (t0 - 1)
                if o > 0 or tb_ < t0 + 129:
                    p.op("pool", lambda e, r=r: e.memset(r.ap, 0.0), writes=[r])
                proj(col0, M, sq, lambda bank, ap, s0, n, r=r, o=o: p.op("act", lambda e: e.copy(out=r[0:M, o:o + n], in_=ap), reads=[bank], writes=[r]), ta, tb_)
                p.op("pool", lambda e, r=r: e.tensor_tensor(out=lt_[0:M, :], in0=r[0:M, 0:128], in1=r[0:M, 2:130], op=ALU.add), reads=[r], writes=[lt_])
                p.op("dve", lambda e: e.tensor_scalar(out=lt_[0:M, :], in0=lt_[0:M, :], scalar1=hm[0:M, g:g + 1], scalar2=None, op0=ALU.mult), reads=[lt_, hm], writes=[lt_])
                p.op("dve", lambda e, r=r: e.scalar_tensor_tensor(out=dst_ap, in0=r[0:M, 1:129], scalar=omm[0:M, g:g + 1], in1=lt_[0:M, :], op0=ALU.mult, op1=ALU.add), reads=[r, omm, lt_], writes=[dst_t])

            def l2n(x):
                xv = x.ap.rearrange("p h t -> p (h t)")
                p.op("act", lambda e: e.activation(out=sqs.ap, in_=xv, func=AF.Square), reads=[x], writes=[sqs])
                bk = nb()
                B.mm(bk, bk[0:64, :], B.ones[0:64, 0:64], sqs.ap, True, True, [B.ones, sqs], inc=True)
                p.op("act", lambda e, bk=bk: e.activation(out=rr_.ap, in_=bk[0:64, :], func=AF.Sqrt, bias=B.eps6[0:64, 0:1], scale=1.0), reads=[bk, B.eps6], writes=[rr_])
                p.op("dve", lambda e: e.reciprocal(out=rr_.ap, in_=rr_.ap), reads=[rr_], writes=[rr_])
                p.op("dve", lambda e: e.tensor_tensor(out=xv, in0=xv, in1=rr_.ap, op=ALU.mult), reads=[x, rr_], writes=[x])

            def lora_sig(dst, src, wmat, bias_col):
                bk = nb()
                for h in range(4):
                    B.mm(bk, bk[0:64, h * 128:(h + 1) * 128], wmat[:, h * 64:(h + 1) * 64], src.ap, True, True, [wmat, src], inc=(h == 3))
                for h in range(4):
                    p.op("act", lambda e, h=h, bk=bk: e.activation(out=dst[:, h, :], in_=bk[0:64, h * 128:(h + 1) * 128], func=AF.Sigmoid, bias=prm[:, bias_col, h:h + 1], scale=1.0),
                         reads=[bk, prm], writes=[dst])

            for dirn in range(2):
                load_dir_consts(cst, dirn)
                p.dma("sp", wup.ap, rwkv_w_up[l, dirn], writes=[wup])
                p.dma("sp", aup.ap, rwkv_a_up[l, dirn], writes=[aup])
                p.dma("sp", prm[:, 3, :], hd(rwkv_w0[l, dirn]), writes=[prm], allow_slow_non_contiguous=True)
                p.dma("sp", prm[:, 4, :], hd(rwkv_a0[l, dirn]), writes=[prm], allow_slow_non_contiguous=True)
                if sq["pi"] is None:
                    p.dma("sp", stt_ap, state_rwkv[l, dirn].rearrange("h v k -> v h k"), writes=[sqs])
                    bk = nb()
                    for h in range(4):
                        B.tr(bk, bk[0:64, h * 64:(h + 1) * 64], stt_ap[:, h, :], [sqs], inc=(h == 3))
                    p.op("act", lambda e, bk=bk: e.copy(out=S0.ap, in_=bk[0:64, 0:256].rearrange("p (h d) -> p h d", h=4)), reads=[bk], writes=[S0])
                else:
                    p.op("pool", lambda e: e.memset(S0.ap, 0.0), writes=[S0])

                def prep(t0, sl, dirn=dirn):
                    Pr, Pk, Pv, kk, ld, af, bop, gT = Pr_[0], Pk_[0], Pv_[0], kk_[0], ld_[0], af_[0], bop_[0], gT_[0]
                    for h in range(4):
                        lerp_group(A0 + 64 * h, 64, h, Pr[:, h, :], Pr, t0)
                        lerp_group(A0 + 256 + 64 * h, 64, 4 + h, Pk[:, h, :], Pk, t0)
                        lerp_group(A0 + 512 + 64 * h, 64, 8 + h, Pv[:, h, :], Pv, t0)
                    lerp_group(A0 + 768 + 64 * dirn, 64, 12 + dirn, wd.ap, wd, t0)
                    lerp_group(A0 + 896 + 64 * dirn, 64, 14 + dirn, adt.ap, adt, t0)
                    p.op("dve", lambda e: e.tensor_tensor(out=kk.ap, in0=Pk.ap, in1=bc(0), op=ALU.mult), reads=[Pk, prm], writes=[kk])
                    l2n(kk)
                    p.op("act", lambda e: e.activation(out=wd.ap, in_=wd.ap, func=AF.Tanh), reads=[wd], writes=[wd])
                    lora_sig(ld, wd, wup, 3)
                    p.op("pool", lambda e: e.tensor_scalar(out=ld.ap, in0=ld.ap, scalar1=-float(np.exp(-0.5)), scalar2=None, op0=ALU.mult), reads=[ld], writes=[ld])
                    lora_sig(aa, adt, aup, 4)
                    if dirn == 1:
                        lerp_group(A0 + 896, 64, 14, adf.ap, adf, t0)
                        lora_sig(af, adf, aupf, 5)
                        for gi, (c0_, M_) in enumerate(((1024, 64), (1088, 64), (1152, 32))):
                            lerp_group(A0 + c0_, M_, 16 + gi, sgd[gi][0:M_, :], sgd[gi], t0)
                            p.op("act", lambda e, gi=gi, M_=M_: e.activation(out=sgd[gi][0:M_, :], in_=sgd[gi][0:M_, :], func=AF.Sigmoid), reads=[sgd[gi]], writes=[sgd[gi]])
                        bk = nb()
                        for h in range(4):
                            for gi, M_ in enumerate((64, 64, 32)):
                                B.mm(bk, bk[0:64, h * 128:(h + 1) * 128], gup[gi][0:M_, h * 64:(h + 1) * 64], sgd[gi][0:M_, :], gi == 0, gi == 2, [gup[gi], sgd[gi]], inc=(h == 3 and gi == 2))
                        p.op("act", lambda e, bk=bk: e.copy(out=gT.ap.rearrange("p h t -> p (h t)"), in_=bk[0:64, :]), reads=[bk], writes=[gT])
                        p.op("pool", lambda e: e.tensor_tensor(out=af.ap, in0=af.ap, in1=aa.ap, op=ALU.add), reads=[af, aa], writes=[af])
                        p.op("dve", lambda e: e.scalar_tensor_tensor(out=af.ap, in0=af.ap, scalar=-2.0, in1=bc(1), op0=ALU.add, op1=ALU.mult), reads=[af, prm], writes=[af])
                        p.op("dve", lambda e: e.scalar_tensor_tensor(out=af.ap, in0=af.ap, scalar=2.0, in1=Pk.ap, op0=ALU.add, op1=ALU.mult), reads=[af, Pk], writes=[af])
                        p.op("pool", lambda e: e.tensor_tensor(out=af.ap, in0=af.ap, in1=Pr.ap, op=ALU.mult), reads=[af, Pr], writes=[af])
                        p.op("pool", lambda e: e.tensor_tensor(out=af.ap, in0=af.ap, in1=bc(2), op=ALU.mult), reads=[af, prm], writes=[af])
                        bk2 = nb()
                        B.mm(bk2, bk2[0:64, :], B.ones[0:64, 0:64], af.ap.rearrange("p h t -> p (h t)"), True, True, [B.ones, af], inc=True)
                        p.op("dve", lambda e, bk2=bk2: e.tensor_tensor(out=af.ap.rearrange("p h t -> p (h t)"), in0=bk2[0:64, :], in1=Pv.ap.rearrange("p h t -> p (h t)"), op=ALU.mult), reads=[bk2, Pv], writes=[af])
                    p.op("pool", lambda e: e.tensor_tensor(out=bop.ap, in0=kk.ap, in1=aa.ap, op=ALU.mult), reads=[kk, aa], writes=[bop])
                    p.op("pool", lambda e: e.tensor_scalar(out=kk.ap, in0=kk.ap, scalar1=-1.0, scalar2=None, op0=ALU.mult), reads=[kk], writes=[kk])
                    p.op("dve", lambda e: e.scalar_tensor_tensor(out=aa.ap, in0=aa.ap, scalar=-1.0, in1=bc(1), op0=ALU.add, op1=ALU.mult), reads=[aa, prm], writes=[aa])
                    p.op("dve", lambda e: e.scalar_tensor_tensor(out=Pk.ap, in0=aa.ap, scalar=1.0, in1=Pk.ap, op0=ALU.add, op1=ALU.mult), reads=[aa, Pk], writes=[Pk])
                    return dict(ld=ld, q=Pr, k=Pk, v=Pv, z=kk, b=bop, bonus=af, g=gT)

                def post(t0, ys, ops):
                    yv = ys.ap.rearrange("p h t -> p (h t)")
                    bk = nb()
                    B.mm(bk, bk[0:64, :], B.ones[0:64, 0:64], yv, True, True, [B.ones, ys], inc=True)
                    p.op("dve", lambda e, bk=bk: e.scalar_tensor_tensor(out=yv, in0=bk[0:64, :], scalar=-1.0 / 64, in1=yv, op0=ALU.mult, op1=ALU.add), reads=[bk, ys], writes=[ys])
                    p.op("act", lambda e: e.activation(out=sqs.ap, in_=yv, func=AF.Square), reads=[ys], writes=[sqs])
                    bk2 = nb()
                    B.mm(bk2, bk2[0:64, :], B.ones[0:64, 0:64], sqs.ap, True, True, [B.ones, sqs], inc=True)
                    p.op("act", lambda e, bk2=bk2: e.activation(out=rr_.ap, in_=bk2[0:64, :], func=AF.Sqrt, bias=epsg[:, 0:1], scale=1.0 / 64), reads=[bk2, epsg], writes=[rr_])
                    p.op("dve", lambda e: e.reciprocal(out=rr_.ap, in_=rr_.ap), reads=[rr_], writes=[rr_])
                    p.op("dve", lambda e: e.tensor_tensor(out=yv, in0=yv, in1=rr_.ap, op=ALU.mult), reads=[ys, rr_], writes=[ys])
                    p.op("pool", lambda e: e.tensor_tensor(out=ys.ap, in0=ys.ap, in1=bc(6), op=ALU.mult), reads=[ys, prm], writes=[ys])
                    p.op("pool", lambda e: e.tensor_tensor(out=ys.ap, in0=ys.ap, in1=bc(7), op=ALU.add), reads=[ys, prm], writes=[ys])
                    p.op("pool", lambda e: e.tensor_tensor(out=ys.ap, in0=ys.ap, in1=ops["bonus"].ap, op=ALU.add), reads=[ys, ops["bonus"]], writes=[ys])
                    p.op("pool", lambda e: e.tensor_tensor(out=ytmp.ap, in0=ys.ap, in1=ops["g"].ap, op=ALU.mult), reads=[ys, ops["g"]], writes=[ytmp])
                    yput(0, t0)

                scan_pass(sq, dirn, dk, dv, True, False, prep, S0, yf, None if dirn == 0 else yf, post, cst)
                if sq["pi"] is not None:
                    bk = nb()
                    for h in range(4):
                        B.tr(bk, bk[0:64, h * 64:(h + 1) * 64], S0[:, h, :], [S0], inc=(h == 3))
                    p.op("act", lambda e, bk=bk: e.copy(out=stt_ap, in_=bk[0:64, 0:256].rearrange("p (h d) -> p h d", h=4)), reads=[bk], writes=[sqs])
                    p.dma("sp", st_rwkv[sq["pi"], l, dirn].rearrange("h v k -> v h k"), stt_ap, reads=[sqs])

        def delta(sq):
            T = sq["T"]
            dk, dv = 64, 64
            cst = scan_alloc(dk, dv, True, True)
            yf = B.al([dv, 4, T], BF16, "yf")
            S0 = B.al([dk, 4, dv], F32, "S0")
            raw = [B.al([64, 130], F32, "raw%d" % k_) for k_ in range(2)]
            cq_ = [B.al([64, 4, 128], F32, "cq%d" % k_) for k_ in range(1)]
            ck_ = [B.al([64, 4, 128], F32, "ck%d" % k_) for k_ in range(1)]
            cv_ = [B.al([64, 4, 128], F32, "cv%d" % k_) for k_ in range(1)]
            sqs = B.al([64, 512], F32, "sqs")
            rr_ = B.al([64, 512], F32, "rr_")
            X16 = B.al([16, 128], F32, "X16")
            betaB = B.al([64, 4, 128], F32, "betaB")
            ld_ = [B.al([64, 4, 128], F32, "ld%d" % k_) for k_ in range(1)]
            nbt = B.al([64, 4, 128], F32, "nbt")
            bop_ = [B.al([64, 4, 128], F32, "bop%d" % k_) for k_ in range(1)]
            Pg_ = [B.al([64, 4, 128], F32, "Pg%d" % k_) for k_ in range(1)]
            sel = B.al([16, 16, 64], F32, "sel")
            wc = B.al([64, 3, 12], F32, "wc")
            pa = B.al([64, 8], F32, "pa")
            pb = B.al([64, 8], F32, "pb")
            dng = B.al([64, 1], F32, "dng")
            p.dma("sp", sel.ap, c_sel, writes=[sel])
            p.dma("sp", wc.ap, delta_conv[l].rearrange("k (g d) -> d k g", d=64), writes=[wc], allow_slow_non_contiguous=True)
            p.dma("sp", pa.ap, delta_a_log[l].rearrange("d h -> (d h)").partition_broadcast(64), writes=[pa])
            p.dma("sp", pb.ap, delta_dt_bias[l].rearrange("d h -> (d h)").partition_broadcast(64), writes=[pb])
            p.dma("sp", dng.ap, delta_norm_g[l:l + 1, :].rearrange("o d -> d o"), writes=[dng], allow_slow_non_contiguous=True)
            p.op("act", lambda e: e.activation(out=pa.ap, in_=pa.ap, func=AF.Exp), reads=[pa], writes=[pa])
            p.op("dve", lambda e: e.tensor_scalar(out=pa.ap, in0=pa.ap, scalar1=-1.0, scalar2=None, op0=ALU.mult), reads=[pa], writes=[pa])
            rawc = {"n": 0}

            def conv_group(col0, g, dst_ap, dst_t, t0):
                r = raw[rawc["n"] % 2]
                rawc["n"] += 1
                ta, tb_ = max(t0 - 1, 0), min(t0 + 129, T)
                o = ta - (t0 - 1)
                if o > 0 or tb_ < t0 + 129:
                    p.op("pool", lambda e, r=r: e.memset(r.ap, 0.0), writes=[r])
                proj(col0, 64, sq, lambda bank, ap, s0, n, r=r, o=o: p.op("act", lambda e: e.copy(out=r[:, o:o + n], in_=ap), reads=[bank], writes=[r]), ta, tb_)
                p.op("dve", lambda e, r=r: e.tensor_scalar(out=dst_ap, in0=r[:, 0:128], scalar1=wc[:, 0, g:g + 1], scalar2=None, op0=ALU.mult), reads=[r, wc], writes=[dst_t])
                p.op("dve", lambda e, r=r: e.scalar_tensor_tensor(out=dst_ap, in0=r[:, 1:129], scalar=wc[:, 1, g:g + 1], in1=dst_ap, op0=ALU.mult, op1=ALU.add), reads=[r, wc, dst_t], writes=[dst_t])
                p.op("dve", lambda e, r=r: e.scalar_tensor_tensor(out=dst_ap, in0=r[:, 2:130], scalar=wc[:, 2, g:g + 1], in1=dst_ap, op0=ALU.mult, op1=ALU.add), reads=[r, wc, dst_t], writes=[dst_t])

            def l2n(x, scale):
                xv = x.ap.rearrange("p h t -> p (h t)")
                p.op("act", lambda e: e.activation(out=sqs.ap, in_=xv, func=AF.Square), reads=[x], writes=[sqs])
                bk = nb()
                B.mm(bk, bk[0:64, :], B.ones[0:64, 0:64], sqs.ap, True, True, [B.ones, sqs], inc=True)
                p.op("act", lambda e, bk=bk: e.activation(out=rr_.ap, in_=bk[0:64, :], func=AF.Sqrt, bias=B.eps6[0:64, 0:1], scale=1.0), reads=[bk, B.eps6], writes=[rr_])
                p.op("dve", lambda e: e.reciprocal(out=rr_.ap, in_=rr_.ap), reads=[rr_], writes=[rr_])
                p.op("dve", lambda e: e.scalar_tensor_tensor(out=xv, in0=xv, scalar=float(scale), in1=rr_.ap, op0=ALU.mult, op1=ALU.mult), reads=[x, rr_], writes=[x])

            for dirn in range(2):
                load_dir_consts(cst, dirn)
                if sq["pi"] is None:
                    p.dma("sp", S0.ap, state_delta[l, dirn].rearrange("h k v -> k h v"), writes=[S0])
                else:
                    p.op("pool", lambda e: e.memset(S0.ap, 0.0), writes=[S0])

                def prep(t0, sl, dirn=dirn):
                    cq, ck, cv, ld, bop, Pg = cq_[0], ck_[0], cv_[0], ld_[0], bop_[0], Pg_[0]
                    for h in range(4):
                        conv_group(C0 + 64 * h, h, cq[:, h, :], cq, t0)
                        conv_group(C0 + 256 + 64 * h, 4 + h, ck[:, h, :], ck, t0)
                        conv_group(C0 + 512 + 64 * h, 8 + h, cv[:, h, :], cv, t0)
                        if dirn == 1:
                            proj(C0 + 768 + 64 * h, 64, sq, lambda bank, ap, s0, n, h=h: p.op("act", lambda e: e.activation(out=Pg[:, h, :], in_=ap, func=AF.Silu), reads=[bank], writes=[Pg]), t0, t0 + 128)
                    for x in (cq, ck, cv):
                        p.op("act", lambda e, x=x: e.activation(out=x.ap, in_=x.ap, func=AF.Silu), reads=[x], writes=[x])
                    l2n(cq, 0.125)
                    l2n(ck, 1.0)
                    proj(C0 + 1024, 16, sq, lambda bank, ap, s0, n: p.op("act", lambda e: e.copy(out=X16.ap, in_=ap), reads=[bank], writes=[X16]), t0, t0 + 128)
                    bkb, bka = nb(), nb()
                    for h in range(4):
                        B.mm(bkb, bkb[0:64, h * 128:(h + 1) * 128], sel[:, dirn * 4 + h, :], X16.ap, True, True, [sel, X16], inc=(h == 3))
                    for h in range(4):
                        B.mm(bka, bka[0:64, h * 128:(h + 1) * 128], sel[:, 8 + dirn * 4 + h, :], X16.ap, True, True, [sel, X16], inc=(h == 3))
                    p.op("act", lambda e, bkb=bkb: e.activation(out=betaB.ap.rearrange("p h t -> p (h t)"), in_=bkb[0:64, :], func=AF.Sigmoid), reads=[bkb], writes=[betaB])
                    for h in range(4):
                        p.op("act", lambda e, bka=bka, h=h: e.activation(out=ld[:, h, :], in_=bka[0:64, h * 128:(h + 1) * 128], func=AF.Exp, bias=pb[:, dirn * 4 + h:dirn * 4 + h + 1], scale=1.0),
                             reads=[bka, pb], writes=[ld])
                    p.op("act", lambda e: e.activation(out=ld.ap, in_=ld.ap, func=AF.Ln, bias=B.one1[0:64, 0:1], scale=1.0), reads=[ld, B.one1], writes=[ld])
                    for h in range(4):
                        p.op("dve", lambda e, h=h: e.tensor_scalar(out=ld[:, h, :], in0=ld[:, h, :], scalar1=pa[:, dirn * 4 + h:dirn * 4 + h + 1], scalar2=None, op0=ALU.mult), reads=[ld, pa], writes=[ld])
                    p.op("act", lambda e: e.activation(out=nbt.ap, in_=ld.ap, func=AF.Exp), reads=[ld], writes=[nbt])
                    p.op("dve", lambda e: e.scalar_tensor_tensor(out=nbt.ap, in0=betaB.ap, scalar=-1.0, in1=nbt.ap, op0=ALU.mult, op1=ALU.mult), reads=[betaB, nbt], writes=[nbt])
                    p.op("pool", lambda e: e.tensor_tensor(out=bop.ap, in0=ck.ap, in1=nbt.ap, op=ALU.mult), reads=[ck, nbt], writes=[bop])
                    p.op("pool", lambda e: e.tensor_tensor(out=cv.ap, in0=cv.ap, in1=betaB.ap, op=ALU.mult), reads=[cv, betaB], writes=[cv])
                    if t0 == 0 and dirn == 0:
                        for nm_, tl in (("d_ld", ld), ("d_q", cq), ("d_k", ck), ("d_v", cv), ("d_b", bop)):
                            B.dump(nm_ + "_" + sq["name"], tl, [64, 4, 128])
                    return dict(ld=ld, q=cq, k=ck, v=cv, z=ck, b=bop, r=Pg)

                def post(t0, ys, ops):
                    yv = ys.ap.rearrange("p h t -> p (h t)")
                    p.op("act", lambda e: e.activation(out=sqs.ap, in_=yv, func=AF.Square), reads=[ys], writes=[sqs])
                    bk = nb()
                    B.mm(bk, bk[0:dv, :], B.ones[0:dv, 0:dv], sqs.ap, True, True, [B.ones, sqs], inc=True)
                    p.op("act", lambda e, bk=bk: e.activation(out=rr_.ap, in_=bk[0:dv, :], func=AF.Sqrt, bias=B.eps6[0:dv, 0:1], scale=1.0 / dv), reads=[bk, B.eps6], writes=[rr_])
                    p.op("dve", lambda e: e.reciprocal(out=rr_.ap, in_=rr_.ap), reads=[rr_], writes=[rr_])
                    p.op("dve", lambda e: e.scalar_tensor_tensor(out=sqs.ap, in0=yv, scalar=dng[:, 0:1], in1=rr_.ap, op0=ALU.mult, op1=ALU.mult), reads=[ys, dng, rr_], writes=[sqs])
                    p.op("pool", lambda e: e.tensor_tensor(out=ytmp.ap, in0=sqs.ap.rearrange("p (h t) -> p h t", h=4), in1=ops["r"].ap, op=ALU.mult),
                         reads=[sqs, ops["r"]], writes=[ytmp])
                    yput(2, t0)

                scan_pass(sq, dirn, dk, dv, True, True, prep, S0, yf, None if dirn == 0 else yf, post, cst)
                if dirn == 0:
                    B.dump("d_yf_" + sq["name"], yf, [dv, 4, T])
                if sq["pi"] is not None:
                    p.dma("sp", st_delta[sq["pi"], l, dirn].rearrange("h k v -> k h v"), S0.ap, reads=[S0])

        def attention(sq):
            T = sq["T"]
            rr["n"] = 6
            rr["b"] = 0
            ctx = sq["pi"] is None
            L = T + (256 if ctx else 0)
            koff = 256 if ctx else 0
            nlt = L // 128
            ring["big"] = [B.al([128, 8, 128], BF16, "wbig%d" % k) for k in range(2)]
            qraw = B.al([64, 4, T], F32, "qraw")
            kraw = B.al([64, 2, T], F32, "kraw")
            qb = B.al([64, 4, T], BF16, "qb")
            kb = B.al([64, 2, L], BF16, "kb")
            Vb = B.al([128, nlt, 128], BF16, "Vb")
            scr = [B.al([64, 512], F32, "ascr%d" % k) for k in range(5)]
            Eb = [B.al([128, 512], BF16, "Eb%d" % k) for k in range(2)]
            rs = B.al([64, 512], F32, "rs")
            atmp = B.al([64, 512], BF16, "atmp")
            if ctx:
                cosT = B.al([64, TS], F32, "cosT")
                sinT = B.al([64, TS], F32, "sinT")
                p.dma("sp", cosT.ap, c_cos, writes=[cosT])
                p.dma("sp", sinT.ap, c_sin, writes=[sinT])
                kst = B.al([128, 2, 128], F32, "kst")
                p.dma("sp", kst.ap, cache_k[l].rearrange("(t p) f -> p t f", p=128), writes=[kst])
                p.dma("pool", Vb[:, 0:2, :], cache_v[l].rearrange("(t p) f -> p t f", p=128), writes=[Vb])
                for t in range(2):
                    for n in range(2):
                        bank = nb()
                        B.tr(bank, bank[0:64, 0:128], kst[:, t, n * 64:(n + 1) * 64], [kst])
                        p.op("act", lambda e, bank=bank, t=t, n=n: e.copy(out=kb[:, n, t * 128:(t + 1) * 128], in_=bank[0:64, 0:128]),
                             reads=[bank], writes=[kb])
            else:
                Vf = B.al([128, 2, 128], F32, "Vf")
                kTf = B.al([64, 2, T], F32, "kTf")
                kout = B.al([128, 2, 128], F32, "kout")
            KSUB = int(os.environ.get("KSUB", "9"))
            if KSUB <= 1:
                return
            for h in range(4):
                proj(D0 + 64 * h, 64, sq,
                     lambda bank, ap, s0, n, h=h: p.op("act", lambda e: e.copy(out=qraw[:, h, s0:s0 + n], in_=ap), reads=[bank], writes=[qraw]))
            if KSUB <= 2:
                return
            for n_ in range(2):
                proj(D0 + 256 + 64 * n_, 64, sq,
                     lambda bank, ap, s0, n, n_=n_: p.op("act", lambda e: e.copy(out=kraw[:, n_, s0:s0 + n], in_=ap), reads=[bank], writes=[kraw]))
            if KSUB <= 3:
                return
            w = wload(D0 + 384, 128)
            for s0 in range(0, T, 128):
                bank = nb()
                a = sq["hoff"] + s0
                for kt in range(8):
                    B.mm(bank, bank[:, 0:128], hT_[:, kt, a:a + 128], w[:, kt, :], kt == 0, kt == 7, [w, hT_])
                lt = (koff + s0) // 128
                if ctx:
                    p.op("act", lambda e, bank=bank, lt=lt: e.copy(out=Vb[:, lt, :], in_=bank[:, 0:128]), reads=[bank], writes=[Vb])
                else:
                    p.op("act", lambda e, bank=bank, lt=lt: e.copy(out=Vf[:, lt, :], in_=bank[:, 0:128]), reads=[bank], writes=[Vf])
                    p.op("dve", lambda e, lt=lt: e.tensor_copy(out=Vb[:, lt, :], in_=Vf[:, lt, :]), reads=[Vf], writes=[Vb])
            if not ctx and not os.environ.get("KNOVC"):
                p.dma("sp", vc_out[sq["pi"], l].rearrange("(t p) f -> p t f", p=128), Vf.ap, reads=[Vf])
            KS = int(os.environ.get("KSTOP", "9"))
            if KS <= 1:
                return

            def normrope(raw, nh, gcol, dst, doff, keepf=None):
                for h in range(nh):
                    for c0 in range(0, T, 512):
                        n = min(512, T - c0)
                        x = raw[:, h, c0:c0 + n]
                        s_sq, s_r, s_x, s_1, s_2 = scr[0], scr[1], scr[2], scr[3], scr[4]
                        p.op("act", lambda e, x=x, n=n: e.activation(out=s_sq[:, 0:n], in_=x, func=AF.Square), reads=[raw], writes=[s_sq])
                        bank = nb()
                        B.mm(bank, bank[0:64, 0:n], B.ones[0:64, 0:64], s_sq[:, 0:n], True, True, [B.ones, s_sq])
                        p.op("act", lambda e, bank=bank, n=n: e.activation(out=s_r[:, 0:n], in_=bank[0:64, 0:n], func=AF.Sqrt, bias=B.eps6[0:64, 0:1], scale=1.0 / 64),
                             reads=[bank, B.eps6], writes=[s_r])
                        p.op("dve", lambda e, n=n: e.reciprocal(out=s_r[:, 0:n], in_=s_r[:, 0:n]), reads=[s_r], writes=[s_r])
                        d = dst[:, h, doff + c0:doff + c0 + n]
                        if ctx:
                            p.op("dve", lambda e, x=x, n=n: e.scalar_tensor_tensor(out=s_x[:, 0:n], in0=x, scalar=gcol, in1=s_r[:, 0:n], op0=ALU.mult, op1=ALU.mult),
                                 reads=[raw, s_r, qkng2], writes=[s_x])
                            bank2 = nb()
                            B.mm(bank2, bank2[0:64, 0:n], rot_t.ap, s_x[:, 0:n], True, True, [rot_t, s_x])
                            p.op("dve", lambda e, n=n, c0=c0: e.tensor_tensor(out=s_1[:, 0:n], in0=s_x[:, 0:n], in1=cosT[:, c0:c0 + n], op=ALU.mult),
                                 reads=[s_x, cosT], writes=[s_1])
                            p.op("dve", lambda e, n=n, c0=c0, bank2=bank2: e.tensor_tensor(out=s_2[:, 0:n], in0=bank2[0:64, 0:n], in1=sinT[:, c0:c0 + n], op=ALU.mult),
                                 reads=[bank2, sinT], writes=[s_2])
                            p.op("pool", lambda e, n=n, d=d: e.tensor_tensor(out=d, in0=s_1[:, 0:n], in1=s_2[:, 0:n], op=ALU.add),
                                 reads=[s_1, s_2], writes=[dst])
                        else:
                            p.op("dve", lambda e, x=x, n=n, d=d: e.scalar_tensor_tensor(out=d, in0=x, scalar=gcol, in1=s_r[:, 0:n], op0=ALU.mult, op1=ALU.mult),
                                 reads=[raw, s_r, qkng2], writes=[dst])
                            if keepf is not None:
                                kf = keepf[:, h, c0:c0 + n]
                                p.op("dve", lambda e, x=x, n=n, kf=kf: e.scalar_tensor_tensor(out=kf, in0=x, scalar=gcol, in1=s_r[:, 0:n], op0=ALU.mult, op1=ALU.mult),
                                     reads=[raw, s_r, qkng2], writes=[keepf])

            normrope(qraw, 4, qkng2[:, 2 * l:2 * l + 1], qb, 0)
            normrope(kraw, 2, qkng2[:, 2 * l + 1:2 * l + 2], kb, koff, None if ctx else kTf)
            if KS <= 2:
                return
            if not ctx:
                for t in range(2):
                    bank = nb()
                    for n_ in range(2):
                        B.tr(bank, bank[:, n_ * 64:(n_ + 1) * 64], kTf[:, n_, t * 128:(t + 1) * 128], [kTf], inc=(n_ == 1))
                    p.op("act", lambda e, bank=bank, t=t: e.copy(out=kout[:, t, :], in_=bank[:, 0:128]), reads=[bank], writes=[kout])
                p.dma("sp", kc_out[sq["pi"], l].rearrange("(t p) f -> p t f", p=128), kout.ap, reads=[kout])
            if KS <= 3:
                return
            ei = 0
            for h in range(4):
                nk = h // 2
                for q0 in range(0, T, 512):
                    n = min(512, T - q0)
                    b1, b2 = PS[6], PS[7]
                    for lt in range(nlt):
                        bs = nb()
                        B.mm(bs, bs[:, 0:n], kb[:, nk, lt * 128:(lt + 1) * 128], qb[:, h, q0:q0 + n], True, True, [kb, qb])
                        E = Eb[ei % 2]
                        ei += 1
                        p.op("act", lambda e, E=E, bs=bs, n=n: e.activation(out=E[:, 0:n], in_=bs[:, 0:n], func=AF.Exp), reads=[bs], writes=[E])
                        B.mm(b1, b1[0:64, 0:n], Vb[:, lt, nk * 64:(nk + 1) * 64], E[:, 0:n], lt == 0, lt == nlt - 1, [Vb, E], inc=True)
                        B.mm(b2, b2[0:64, 0:n], ones_b.ap, E[:, 0:n], lt == 0, lt == nlt - 1, [ones_b, E], inc=True)
                    p.op("dve", lambda e, b2=b2, n=n: e.reciprocal(out=rs[:, 0:n], in_=b2[0:64, 0:n]), reads=[b2], writes=[rs])
                    if h % 2 == 0:
                        d = yT[0:64, 6 + h // 2, q0:q0 + n]
                        p.op("dve", lambda e, b1=b1, n=n, d=d: e.tensor_tensor(out=d, in0=b1[0:64, 0:n], in1=rs[:, 0:n], op=ALU.mult),
                             reads=[b1, rs], writes=[yT])
                    else:
                        p.op("dve", lambda e, b1=b1, n=n: e.tensor_tensor(out=atmp[:, 0:n], in0=b1[0:64, 0:n], in1=rs[:, 0:n], op=ALU.mult),
                             reads=[b1, rs], writes=[atmp])
                        p.dma("sp", yT[64:128, 6 + h // 2, q0:q0 + n], atmp[:, 0:n], reads=[atmp], writes=[yT])

        def merge(sq):
            T = sq["T"]
            j = sq["j"]
            ring["big"] = [B.al([128, 8, 128], BF16, "wbig%d" % k) for k in range(2)]
            mergedT = B.al([128, 8, T], BF16, "mergedT")
            acc = B.al([128, T], F32, "acc")
            tmpm = [B.al([128, 512], F32, "tmpm%d" % k) for k in range(2)]
            sgm = [B.al([128, 512], F32, "sgm%d" % k) for k in range(2)]
            wbr = [B.al([128, 2, 128], BF16, "wbr%d" % k) for k in range(2)]
            n_ = 0
            for ob in range(8):
                for m in range(4):
                    wg = wload(G0 + m * 1024 + ob * 128, 128)
                    wb = wbr[(ob * 4 + m) % 2]
                    p.dma("pool", wb.ap, w_branch[l, m].rearrange("(t p) c -> p t c", p=128)[:, :, ob * 128:(ob + 1) * 128], writes=[wb])
                    for s0 in range(0, T, 512):
                        n = min(512, T - s0)
                        bg, bu = nb(), nb()
                        for kt in range(8):
                            B.mm(bg, bg[:, 0:n], wg[:, kt, :], hT_[:, kt, sq["hoff"] + s0:sq["hoff"] + s0 + n], kt == 0, kt == 7, [wg, hT_])
                        for t_ in range(2):
                            B.mm(bu, bu[:, 0:n], wb[:, t_, :], yT[:, m * 2 + t_, s0:s0 + n], t_ == 0, t_ == 1, [wb, yT])
                        sg = sgm[n_ % 2]
                        n_ += 1
                        p.op("act", lambda e, sg=sg, bg=bg, n=n: e.activation(out=sg[:, 0:n], in_=bg[:, 0:n], func=AF.Sigmoid), reads=[bg], writes=[sg])
                        a = acc[:, s0:s0 + n]
                        if m == 0:
                            p.op("dve", lambda e, a=a, sg=sg, bu=bu, n=n: e.tensor_tensor(out=a, in0=bu[:, 0:n], in1=sg[:, 0:n], op=ALU.mult), reads=[bu, sg], writes=[acc])
                        else:
                            tm = tmpm[n_ % 2]
                            p.op("dve", lambda e, tm=tm, sg=sg, bu=bu, n=n: e.tensor_tensor(out=tm[:, 0:n], in0=bu[:, 0:n], in1=sg[:, 0:n], op=ALU.mult), reads=[bu, sg], writes=[tm])
                            if m < 3:
                                p.op("dve", lambda e, a=a, tm=tm, n=n: e.tensor_tensor(out=a, in0=a, in1=tm[:, 0:n], op=ALU.add), reads=[acc, tm], writes=[acc])
                            else:
                                d = mergedT[:, ob, s0:s0 + n]
                                p.op("dve", lambda e, a=a, tm=tm, n=n, d=d: e.tensor_tensor(out=d, in0=a, in1=tm[:, 0:n], op=ALU.add), reads=[acc, tm], writes=[mergedT, acc])
            woutv = w_out[l].rearrange("(kt p) c -> p kt c", p=128)
            for ob in range(8):
                w = wbig()
                p.dma("pool", w.ap, woutv[:, :, ob * 128:(ob + 1) * 128], writes=[w])
                for xi, (tb, c0) in enumerate(sq["xs"]):
                    s0 = xi * 512
                    n = min(512, T - s0)
                    bank = nb()
                    for kt in range(8):
                        B.mm(bank, bank[:, 0:n], w[:, kt, :], mergedT[:, kt, s0:s0 + n], kt == 0, kt == 7, [w, mergedT])
                    gt = gtT[l][:, 8 + ob, j:j + 1]
                    dst = xT[tb][:, ob, c0:c0 + n]
                    p.op("dve", lambda e, dst=dst, bank=bank, gt=gt, n=n: e.scalar_tensor_tensor(out=dst, in0=bank[:, 0:n], scalar=gt, in1=dst, op0=ALU.mult, op1=ALU.add),
                         reads=[bank, gtT[l], xT[tb]], writes=[xT[tb]])

        hdone = {"p": False, "s": False}
        for si, sq in enumerate(SEQS):
            if si >= int(os.environ.get("KSEQ", "9")):
                break
            grp = "s" if sq["pi"] is None else "p"
            p.barrier()
            B.arena_off = mark
            if not hdone[grp]:
                hdone[grp] = True
                norm_setup()
                if grp == "p":
                    adaln(l, 1, 2, hT_, PS[6], hT_[:, :, 0:512])
                else:
                    adaln(l, 1, 0, hT_, PS[6], hT_[:, :, 0:512])
                    adaln(l, 1, 1, hT_, PS[6], hT_[:, :, 512:1024])
                p.barrier()
                B.arena_off = mark
            for m in MIXERS_OFF:
                d = yT[:, m * 2:(m + 1) * 2, 0:sq["T"]]
                p.op("pool", lambda e, d=d: e.memset(d, 0.0), writes=[yT])
            if 0 not in MIXERS_OFF:
                rwkv(sq)
                B.dump("yTa_" + sq["name"], yT, [128, 8, TS])
                p.barrier()
                B.arena_off = mark
            if 1 not in MIXERS_OFF:
                gla(sq)
                B.dump("yT_" + sq["name"], yT, [128, 8, TS])
                p.barrier()
                B.arena_off = mark
            if 2 not in MIXERS_OFF:
                delta(sq)
                B.dump("yTc_" + sq["name"], yT, [128, 8, TS])
                p.barrier()
                B.arena_off = mark
            if 3 not in MIXERS_OFF:
                attention(sq)
                rr["n"] = 8
                p.barrier()
                B.arena_off = mark
            if KMIX > 1:
                merge(sq)

    def final_out():
        B.arena_reset()
        norm_setup()
        t_scr = S["t_scr"]
        for tb in range(3):
            rms_stats(tb, PS[6])
            for ft in range(8):
                t = t_scr[ft % 2]
                src = xT[tb][:, ft, :]
                p.op("dve", lambda e, t=t, src=src, rs_=rstd[tb]: e.tensor_tensor(out=t.ap, in0=src, in1=rs_.ap, op=ALU.mult),
                     reads=[xT[tb], rstd[tb]], writes=[t])
                g = fngT[:, ft:ft + 1]
                dst = xT[tb][:, ft, :]
                p.op("act", lambda e, dst=dst, t=t, g=g: e.activation(out=dst, in_=t.ap, func=AF.Copy, scale=g),
                     reads=[t, fngT], writes=[xT[tb]])
        store_x()

    def store_x():
        stage_tm = [B.al([128, D], F32, "stage_tm%d" % i) for i in range(2)]
        for tt in range(12):
            st = stage_tm[tt % 2]
            tb, tq = divmod(tt, 4)
            for half in range(2):
                bank = PS[(tt * 2 + half) % 8]
                for jj in range(4):
                    ft = half * 4 + jj
                    B.tr(bank, bank[:, jj * 128:(jj + 1) * 128], xT[tb][:, ft, tq * 128:(tq + 1) * 128], [xT[tb]], inc=(jj == 3))
                dst = st[:, half * 512:(half + 1) * 512]
                if half == 0:
                    p.op("dve", lambda e, dst=dst, bank=bank: e.tensor_copy(out=dst, in_=bank.ap), reads=[bank], writes=[st])
                else:
                    p.op("act", lambda e, dst=dst, bank=bank: e.copy(out=dst, in_=bank.ap), reads=[bank], writes=[st])
            p.dma("sp", y_tm[tt * 128:(tt + 1) * 128, :], st.ap, reads=[st])

    if stop_after == "in":
        B.arena_reset()
        store_x()
    else:
        for l in range(DEPTH):
            ffn(l, 0, 0)
            if stop_after == "ffn1" or (stop_after == "l1ffn1" and l == 1):
                break
            mixing(l)
            if stop_after == "mix1" or (stop_after == "l1mix" and l == 1):
                break
            ffn(l, 1, 2)
            if stop_after == "l0":
                break
        if stop_after in ("ffn1", "mix1", "l0", "l1ffn1", "l1mix"):
            B.arena_reset()
            store_x()
        else:
            final_out()

    p.finish()
    p.emit()
    return B


CONSTS = None


def make_consts():
    c = {}
    c["c_ident"] = np.eye(128, dtype=np.float32)
    c["c_ones"] = np.ones((128, 128), dtype=np.float32)
    rot = np.zeros((64, 64), np.float32)
    for base in (0, 32):
        for d in range(16):
            rot[base + d + 16, base + d] = -1.0
            rot[base + d, base + d + 16] = 1.0
    c["c_rot"] = rot
    t = np.arange(TS)
    inv = (1.0 / (np.float32(10000.0) ** (np.arange(0, 32, 2, dtype=np.float32) / np.float32(32)))).astype(np.float32)
    ang = np.zeros((64, TS), np.float32)
    for d in range(64):
        pos = (t // 64) if d < 32 else (t % 64)
        ang[d] = pos.astype(np.float32) * inv[d % 16]
    r_, c_ = np.meshgrid(np.arange(128), np.arange(128), indexing="ij")
    same = (r_ // 64) == (c_ // 64)
    IU = (same & (r_ <= c_)).astype(np.float32)
    SU = (same & (r_ < c_)).astype(np.float32)
    SL = (same & (r_ > c_)).astype(np.float32)
    IL = (same & (r_ >= c_)).astype(np.float32)
    neg = lambda m_: ((1.0 - m_) * -30000.0).astype(np.float32)
    c["c_tri"] = np.ascontiguousarray(np.stack([np.stack([IU, SU, SL], 1), np.stack([IL, SL, SU], 1)], 0))
    c["c_mult"] = np.ascontiguousarray(np.stack([np.stack([IU, IU, SU, SU], 1), np.stack([IL, IL, SL, SL], 1)], 0))
    c["c_ms"] = np.ascontiguousarray(np.stack([SL, SU], 0))
    c["c_nm"] = np.ascontiguousarray(np.stack([np.stack([neg(IU), neg(SU), neg(SL)], 1), np.stack([neg(IL), neg(SL), neg(SU)], 1)], 0))
    sel = np.zeros((16, 16, 64), np.float32)
    for r__ in range(16):
        sel[r__, r__, :] = 1.0
    c["c_sel"] = sel
    c["c_cos"] = np.cos(ang).astype(np.float32)
    c["c_sin"] = np.sin(ang).astype(np.float32)
    return c


def make_in_maps(inputs, names, bf16_inputs=()):
    consts = make_consts()
    maps = []
    f = lambda a: np.ascontiguousarray(np.asarray(a, dtype=np.float32))
    shared = {k: f(inputs[k]) for k in ("norm_g", "w_mod", "b_mod", "ffn_w_in", "ffn_w_out", "final_norm_g",
                                        "w_in", "w_branch", "w_out", "attn_q_norm", "attn_k_norm",
                                        "gla_alpha_up", "gla_alpha_b", "gla_norm_g",
                                        "delta_conv", "delta_a_log", "delta_dt_bias", "delta_norm_g",
                                        "rwkv_mu", "rwkv_w0", "rwkv_w_up", "rwkv_a0", "rwkv_a_up", "rwkv_g_up", "rwkv_k_k", "rwkv_k_a",
                                        "rwkv_r_k", "rwkv_ln_g", "rwkv_ln_b") if k in names}
    xs, xp = f(inputs["x_sample"]), f(inputs["x_prompt"])
    cc, cctx = f(inputs["c"]), f(inputs["c_ctx"])
    ck, cv = f(inputs["cache_attn_k"]), f(inputs["cache_attn_v"])
    for b in range(8):
        m = dict(shared)
        m.update(consts)
        m["x_tm"] = np.ascontiguousarray(np.concatenate([xs[b], xp[2 * b], xp[2 * b + 1]], axis=0))
        m["c2"] = np.ascontiguousarray(np.stack([cc[b], cctx], axis=0))
        m["cache_k"] = np.ascontiguousarray(ck[b].reshape(DEPTH, 256, 128))
        m["cache_v"] = np.ascontiguousarray(cv[b].reshape(DEPTH, 256, 128))
        m["state_gla"] = f(inputs["state_gla"][b])
        m["state_delta"] = f(inputs["state_delta"][b])
        m["state_rwkv"] = f(inputs["state_rwkv"][b])
        if bf16_inputs:
            import ml_dtypes
            for k in bf16_inputs:
                if k in m:
                    key = ("bf", k) if k in shared else None
                    if key and key in _bfcache:
                        m[k] = _bfcache[key]
                    else:
                        m[k] = m[k].astype(ml_dtypes.bfloat16)
                        if key:
                            _bfcache[key] = m[k]
        maps.append({k: v for k, v in m.items() if k in names})
    return maps


_bfcache = {}


def kernel(**inputs):
    B = build_program()
    maps = make_in_maps(inputs, set(B.inp.keys()), B.bf16_inputs)
    res = run_bass_kernel_spmd(B.nc, maps, core_ids=list(range(8)))
    r = res.results
    f32 = lambda a: np.ascontiguousarray(np.asarray(a, dtype=np.float32))
    y_sample = np.stack([f32(r[b]["y_tm"])[:TS] for b in range(8)], axis=0)
    y_prompt = np.stack([f32(r[b // 2]["y_tm"])[TS + (b % 2) * TP:TS + (b % 2 + 1) * TP] for b in range(16)], axis=0)
    kc = np.stack([f32(r[b // 2]["kc_out"])[b % 2].reshape(DEPTH, 256, 2, 64) for b in range(16)], axis=0)
    vc = np.stack([f32(r[b // 2]["vc_out"])[b % 2].reshape(DEPTH, 256, 2, 64) for b in range(16)], axis=0)
    sr = np.stack([f32(r[b // 2]["st_rwkv"])[b % 2] for b in range(16)], axis=0)
    sg = np.stack([f32(r[b // 2]["st_gla"])[b % 2] for b in range(16)], axis=0)
    sd = np.stack([f32(r[b // 2]["st_delta"])[b % 2] for b in range(16)], axis=0)
    return (y_prompt, y_sample, kc, vc, sr, sg, sd)
```

```python
import os
import numpy as np
import concourse.bass as bass
import concourse.mybir as mybir
from concourse.bass_utils import run_bass_kernel_spmd

F32 = mybir.dt.float32
F32R = mybir.dt.float32r
BF16 = mybir.dt.bfloat16
AF = mybir.ActivationFunctionType
ALU = mybir.AluOpType
AX = mybir.AxisListType

ENGS = ("pe", "act", "dve", "pool", "sp")

D = 1024
NT = 1536
TS = 1024
TP = 256
DFF = 2816
NIN = 7632
DEPTH = 2
import os
SIMMODE = bool(os.environ.get('SIMMODE'))
EMBED_WAIT = os.environ.get('KEMBED', '1') == '1'
POOL2DVE = os.environ.get('KP2D', '1') == '1'
MIXERS_OFF = tuple(int(c) for c in os.environ.get('KOFF', ''))
A0, B0, C0, D0, G0 = 0, 1184, 1984, 3024, 3536


class Tile:
    __slots__ = ("ap", "w", "r", "name", "excl")

    def __init__(self, ap, name="", excl=False):
        self.ap = ap
        self.w = {}
        self.r = {}
        self.name = name
        self.excl = excl

    def __getitem__(self, k):
        return self.ap[k]


class Prog:
    def __init__(self, nc, n_dma_sems=96, n_q_sems=0):
        self.nc = nc
        self.ops = {e: [] for e in ENGS}
        self.cnt = {e: 0 for e in ENGS}
        self.sems = {e: nc.alloc_semaphore("s_" + e) for e in ENGS}
        self.dsems = [nc.alloc_semaphore("d%d" % i) for i in range(n_dma_sems)]
        self.dcnt = [0] * n_dma_sems
        self.dnext = 0
        self.wnext = 0
        self.qsems = [nc.alloc_semaphore("q%d" % i) for i in range(n_q_sems)]
        self.qgen = [0] * n_q_sems
        self.qnext = 0
        self.known = {e: {} for e in ENGS}
        self.snap = {e: {} for e in ENGS}
        self.ninst = 0
        self.nwaits = 0
        self._nm = 0
        self.debug = bool(os.environ.get("KDEBUG"))
        self.names = {}

    def _name(self, name):
        self._nm += 1
        return "%s_%d" % (name or "t", self._nm)

    def sb(self, shape, dtype=F32, name=None):
        t = self.nc.alloc_sbuf_tensor(self._name(name), list(shape), dtype)
        return Tile(t[:], name or "sb")

    def ps(self, shape, dtype=F32, name=None):
        t = self.nc.alloc_psum_tensor(self._name(name), list(shape), dtype)
        return Tile(t[:], name or "ps", excl=True)

    def _collect(self, eng, reads, writes):
        deps = {}
        for t in reads:
            for k, v in t.w.items():
                if deps.get(k, 0) < v:
                    deps[k] = v
            if t.excl:
                for k, v in t.r.items():
                    if k != eng and deps.get(k, 0) < v:
                        deps[k] = v
        for t in writes:
            for d in (t.w, t.r):
                for k, v in d.items():
                    if deps.get(k, 0) < v:
                        deps[k] = v
        kn = self.known[eng]
        waits = []
        for k, v in sorted(deps.items(), key=lambda kv: -kv[1] if isinstance(kv[0], str) else 0):
            if eng == "pe" and k == "pe":
                continue
            if kn.get(k, 0) < v:
                kn[k] = v
                waits.append((k, v))
                if isinstance(k, str) and k != eng:
                    snap = self.snap[k].get(v)
                    if snap:
                        for k2, v2 in snap.items():
                            if kn.get(k2, 0) < v2:
                                kn[k2] = v2
        return waits

    def _mark(self, tok, reads, writes):
        k, v = tok
        for t in writes:
            t.w = {k: v}
            t.r = {}
        for t in reads:
            if t.r.get(k, 0) < v:
                t.r[k] = v

    def op(self, eng, fn, reads=(), writes=(), inc=True):
        if eng == "pool" and POOL2DVE:
            eng = "dve"
        if self.debug:
            import sys as _s
            f = _s._getframe(1)
            if f.f_code.co_name in ("mm", "tr"):
                f = f.f_back
            fn._tag = "%s:%d r=%s w=%s" % (f.f_code.co_name, f.f_lineno, [t.name for t in reads], [t.name for t in writes])
        waits = self._collect(eng, reads, writes)
        if inc:
            self.cnt[eng] += 1
            tok = (eng, self.cnt[eng])
            self.ops[eng].append((waits, fn, (eng, 1)))
            self.snap[eng][self.cnt[eng]] = {k: v for k, v in self.known[eng].items() if isinstance(k, str)}
        else:
            assert eng == "pe"
            tok = (eng, self.cnt[eng] + 1)
            self.ops[eng].append((waits, fn, None))
        self._mark(tok, reads, writes)
        return tok

    def dma(self, eng, out, in_, reads=(), writes=(), **kw):
        if eng == "pool" and SIMMODE:
            eng = "sp"
        waits = self._collect(eng, reads, writes)
        if kw.pop("wpool", False):
            i = self.wnext
            self.wnext = (self.wnext + 1) % 32
        else:
            i = 32 + self.dnext
            self.dnext = (self.dnext + 1) % (len(self.dsems) - 32)
        self.dcnt[i] += 16
        tok = (("d", i), self.dcnt[i])
        self.ops[eng].append((waits, lambda e: e.dma_start(out=out, in_=in_, **kw), (("d", i), 16)))
        self._mark(tok, reads, writes)
        return tok

    def _sem(self, k):
        if isinstance(k, str):
            return self.sems[k]
        return self.qsems[k[1]] if k[0] == "q" else self.dsems[k[1]]

    def _dma_targets(self):
        t = [(("d", i), c) for i, c in enumerate(self.dcnt) if c]
        t += [(("q", i, g), 16) for i, g in enumerate(self.qgen) if g]
        return t

    def barrier(self):
        targets = self._dma_targets()
        targets += [(e, self.cnt[e]) for e in ENGS if self.cnt[e]]
        for e in ENGS:
            kn = self.known[e]
            waits = []
            for k, v in targets:
                if k == e:
                    continue
                if kn.get(k, 0) < v:
                    kn[k] = v
                    waits.append((k, v))
            if waits:
                self.ops[e].append((waits, None, None))

    def finish(self):
        waits = self._dma_targets()
        for e in ENGS:
            if e != "sp" and self.cnt[e]:
                waits.append((e, self.cnt[e]))
        self.ops["sp"].append((waits, None, None))

    def check(self):
        val = {}
        pos = {e: 0 for e in ENGS}
        progress = True
        while progress:
            progress = False
            for e in ENGS:
                q = self.ops[e]
                while pos[e] < len(q):
                    waits, fn, inc = q[pos[e]]
                    if any(val.get(k, 0) < v for k, v in waits):
                        break
                    if inc is not None:
                        val[inc[0]] = val.get(inc[0], 0) + inc[1]
                    pos[e] += 1
                    progress = True
        bad = {e: (pos[e], len(self.ops[e])) for e in ENGS if pos[e] < len(self.ops[e])}
        for e, (i, n) in bad.items():
            waits = self.ops[e][i][0]
            print("DEADLOCK", e, "op", i, "of", n, [(k, v, val.get(k, 0)) for k, v in waits if val.get(k, 0) < v],
                  self.ops[e][i][3] if len(self.ops[e][i]) > 3 else "")
        return not bad

    def emit(self):
        nc = self.nc
        prog = self
        assert self.check(), "deadlock in schedule"

        def replay(name, e):
            for waits, fn, inc in prog.ops[name]:
                emb = None
                if fn is not None and waits and EMBED_WAIT:
                    emb = waits[-1]
                    waits = waits[:-1]
                for k, v in waits:
                    e.wait_ge(prog._sem(k), v)
                    prog.nwaits += 1
                if fn is not None:
                    ins = fn(e)
                    if emb is not None:
                        ins._wait_ge(prog._sem(emb[0]), emb[1])
                    if prog.debug:
                        try:
                            prog.names[ins.ins.name] = getattr(fn, "_tag", "")
                        except Exception:
                            pass
                    if inc is not None:
                        ins.then_inc(prog._sem(inc[0]), inc[1])
                    prog.ninst += 1

        with nc.Block() as block:
            @block.tensor
            def _(e):
                replay("pe", e)

            @block.scalar
            def _(e):
                replay("act", e)

            @block.vector
            def _(e):
                replay("dve", e)

            @block.gpsimd
            def _(e):
                replay("pool", e)

            @block.sync
            def _(e):
                replay("sp", e)


class Builder:
    def __init__(self, stop_after=None, dumps=()):
        self.stop_after = stop_after
        self.want_dumps = set(dumps)
        self.dump_list = []
        nc = bass.Bass("TRN2", target_bir_lowering=False)
        self.nc = nc
        self.p = Prog(nc)
        self.inp = {}
        self.out = {}
        self.bf16_inputs = set()
        self._evac_rr = 0

    def arena_init(self, cols):
        self.arena = self.nc.alloc_sbuf_tensor("arena", [128, cols], F32)[:]
        self.arena_cols = cols
        self.arena_off = 0

    def arena_reset(self):
        self.p.barrier()
        self.arena_off = 0

    def al(self, shape, dtype=F32, name="a"):
        n = int(np.prod(shape[1:]))
        if dtype == BF16:
            assert n % 2 == 0
            n //= 2
        assert self.arena_off + n <= self.arena_cols, (name, self.arena_off, n, self.arena_cols)
        ap = self.arena[0:shape[0], self.arena_off:self.arena_off + n]
        self.arena_off += n
        if dtype != F32:
            ap = ap.bitcast(dtype)
        if len(shape) == 3:
            ap = ap.rearrange("p (a b) -> p a b", a=shape[1])
        return Tile(ap, name)

    def din(self, name, shape, cast=False):
        dt_ = BF16 if (cast and SIMMODE) else F32
        if cast and SIMMODE:
            self.bf16_inputs.add(name)
        ap = self.nc.dram_tensor(name, list(shape), dt_, kind="ExternalInput").ap()
        self.inp[name] = ap
        return ap

    def dout(self, name, shape):
        ap = self.nc.dram_tensor(name, list(shape), F32, kind="ExternalOutput").ap()
        self.out[name] = ap
        return ap

    def dump(self, name, tile, shape):
        if name not in self.want_dumps:
            return
        if SIMMODE and tile.ap.dtype == BF16:
            o = self.nc.dram_tensor("dbg_" + name, list(shape), BF16, kind="ExternalOutput").ap()
            self.out["dbg_" + name] = o
        else:
            o = self.dout("dbg_" + name, shape)
        self.p.dma("pool", o, tile.ap, reads=[tile])

    def mm(self, out_t, out_ap, lhsT, rhs, start, stop, reads, inc=None):
        self.p.op("pe", lambda e: e.matmul(out_ap, lhsT, rhs, start=start, stop=stop),
                  reads=reads, writes=[out_t], inc=(stop if inc is None else inc))

    def tr(self, out_t, out_ap, in_ap, reads, inc=True):
        ident = self.ident
        kp = in_ap.shape[0]
        self.p.op("pe", lambda e: e.transpose(out_ap, in_ap, ident[0:kp, 0:kp]), reads=list(reads) + [ident],
                  writes=[out_t], inc=inc)


def build_program(stop_after=None, dumps=()):
    B = Builder(stop_after, dumps)
    nc, p = B.nc, B.p

    x_tm = B.din("x_tm", [NT, D])
    c2 = B.din("c2", [2, D])
    norm_g = B.din("norm_g", [DEPTH, 3, D])
    w_mod = B.din("w_mod", [DEPTH, D, 9 * D])
    b_mod = B.din("b_mod", [DEPTH, 9 * D])
    ffn_w_in = B.din("ffn_w_in", [DEPTH, 2, D, 2 * DFF], cast=True)
    ffn_w_out = B.din("ffn_w_out", [DEPTH, 2, DFF, D], cast=True)
    final_norm_g = B.din("final_norm_g", [D])
    c_ident = B.din("c_ident", [128, 128])
    c_ones = B.din("c_ones", [128, 128])
    y_tm = B.dout("y_tm", [NT, D])
    w_in = B.din("w_in", [DEPTH, D, NIN], cast=True)
    w_branch = B.din("w_branch", [DEPTH, 4, 256, D], cast=True)
    w_out = B.din("w_out", [DEPTH, D, D], cast=True)
    attn_q_norm = B.din("attn_q_norm", [DEPTH, 64])
    attn_k_norm = B.din("attn_k_norm", [DEPTH, 64])
    cache_k = B.din("cache_k", [DEPTH, 256, 128])
    cache_v = B.din("cache_v", [DEPTH, 256, 128], cast=True)
    c_rot = B.din("c_rot", [64, 64])
    c_cos = B.din("c_cos", [64, TS])
    c_sin = B.din("c_sin", [64, TS])
    gla_alpha_up = B.din("gla_alpha_up", [DEPTH, 2, 16, 128])
    gla_alpha_b = B.din("gla_alpha_b", [DEPTH, 2, 128])
    gla_norm_g = B.din("gla_norm_g", [DEPTH, 64])
    state_gla = B.din("state_gla", [DEPTH, 2, 4, 32, 64])
    c_tri = B.din("c_tri", [2, 128, 3, 128])
    c_mult = B.din("c_mult", [2, 128, 4, 128])
    c_ms = B.din("c_ms", [2, 128, 128])
    c_nm = B.din("c_nm", [2, 128, 3, 128])
    st_gla = B.dout("st_gla", [2, DEPTH, 2, 4, 32, 64])
    delta_conv = B.din("delta_conv", [DEPTH, 3, 768])
    delta_a_log = B.din("delta_a_log", [DEPTH, 2, 4])
    delta_dt_bias = B.din("delta_dt_bias", [DEPTH, 2, 4])
    delta_norm_g = B.din("delta_norm_g", [DEPTH, 64])
    state_delta = B.din("state_delta", [DEPTH, 2, 4, 64, 64])
    c_sel = B.din("c_sel", [16, 16, 64])
    st_delta = B.dout("st_delta", [2, DEPTH, 2, 4, 64, 64])
    rwkv_mu = B.din("rwkv_mu", [DEPTH, 1184])
    rwkv_w0 = B.din("rwkv_w0", [DEPTH, 2, 256])
    rwkv_w_up = B.din("rwkv_w_up", [DEPTH, 2, 64, 256])
    rwkv_a0 = B.din("rwkv_a0", [DEPTH, 2, 256])
    rwkv_a_up = B.din("rwkv_a_up", [DEPTH, 2, 64, 256])
    rwkv_g_up = B.din("rwkv_g_up", [DEPTH, 160, 256])
    rwkv_k_k = B.din("rwkv_k_k", [DEPTH, 256])
    rwkv_k_a = B.din("rwkv_k_a", [DEPTH, 256])
    rwkv_r_k = B.din("rwkv_r_k", [DEPTH, 4, 64])
    rwkv_ln_g = B.din("rwkv_ln_g", [DEPTH, 256])
    rwkv_ln_b = B.din("rwkv_ln_b", [DEPTH, 256])
    state_rwkv = B.din("state_rwkv", [DEPTH, 2, 4, 64, 64])
    st_rwkv = B.dout("st_rwkv", [2, DEPTH, 2, 4, 64, 64])
    kc_out = B.dout("kc_out", [2, DEPTH, 256, 128])
    vc_out = B.dout("vc_out", [2, DEPTH, 256, 128])

    B.ident = p.sb([128, 128], F32, "ident")
    B.ones = p.sb([128, 128], F32, "ones")
    B.eps6 = p.sb([128, 1], F32, "eps6")
    p.op("dve", lambda e: e.memset(B.eps6.ap, 1e-6), writes=[B.eps6])
    B.one1 = p.sb([128, 1], F32, "one1")
    p.op("dve", lambda e: e.memset(B.one1.ap, 1.0), writes=[B.one1])
    p.dma("sp", B.ident.ap, c_ident, writes=[B.ident])
    p.dma("sp", B.ones.ap, c_ones, writes=[B.ones])

    w_in_bf = nc.dram_tensor("w_in_bf", [DEPTH, D, NIN], BF16).ap()
    wbfT = Tile(w_in_bf, "w_in_bf")
    for l in range(DEPTH):
        src_v = w_in[l].rearrange("(kt p) c -> p kt c", p=128)
        dst_v = w_in_bf[l].rearrange("(kt p) c -> p kt c", p=128)
        for c0_ in range(0, NIN, 954):
            p.dma("pool", dst_v[:, :, c0_:c0_ + 954], src_v[:, :, c0_:c0_ + 954], writes=[wbfT])

    PS = [p.ps([128, 512], F32, "bank%d" % i) for i in range(8)]
    rr = {"b": 0, "n": 8}

    def nb():
        rr["b"] = (rr["b"] + 1) % rr["n"]
        return PS[rr["b"]]

    ones_b = p.sb([128, 64], BF16, "ones_b")
    p.op("dve", lambda e: e.memset(ones_b.ap, 1.0), writes=[ones_b])
    rot_t = p.sb([64, 64], F32, "rot_t")
    p.dma("sp", rot_t.ap, c_rot, writes=[rot_t])
    qkng = p.sb([64, 2 * DEPTH], F32, "qkng")
    for l in range(DEPTH):
        p.dma("sp", qkng[:, 2 * l:2 * l + 1], attn_q_norm[l:l + 1, :].rearrange("o d -> d o"), writes=[qkng], allow_slow_non_contiguous=True)
        p.dma("sp", qkng[:, 2 * l + 1:2 * l + 2], attn_k_norm[l:l + 1, :].rearrange("o d -> d o"), writes=[qkng], allow_slow_non_contiguous=True)
    qkng2 = p.sb([64, 2 * DEPTH], F32, "qkng2")
    p.op("dve", lambda e: e.tensor_copy(out=qkng2.ap, in_=qkng.ap), reads=[qkng], writes=[qkng2])
    for l in range(DEPTH):
        p.op("dve", lambda e, l=l: e.tensor_scalar(out=qkng2[:, 2 * l:2 * l + 1], in0=qkng[:, 2 * l:2 * l + 1], scalar1=0.125, scalar2=None, op0=ALU.mult),
             reads=[qkng], writes=[qkng2])

    xT = [p.sb([128, 8, 512], F32, "xT%d" % tb) for tb in range(3)]
    rstd = [p.sb([128, 512], F32, "rstd%d" % tb) for tb in range(3)]
    B.arena_init(37000)
    stage_tm = [B.al([128, D], F32, "stage_tm%d" % i) for i in range(2)]

    for tt in range(12):
        st = stage_tm[tt % 2]
        p.dma("sp", st.ap, x_tm[tt * 128:(tt + 1) * 128, :], writes=[st])
        tb, tq = divmod(tt, 4)
        for half in range(2):
            bank = PS[(tt * 2 + half) % 8]
            for j in range(4):
                ft = half * 4 + j
                B.tr(bank, bank[:, j * 128:(j + 1) * 128], st[:, ft * 128:(ft + 1) * 128], [st], inc=(j == 3))
            dst = xT[tb][:, half * 4:half * 4 + 4, tq * 128:(tq + 1) * 128]
            src = bank.ap.rearrange("p (j t) -> p j t", j=4)
            if half == 0:
                p.op("dve", lambda e, dst=dst, src=src: e.tensor_copy(out=dst, in_=src), reads=[bank], writes=[xT[tb]])
            else:
                p.op("act", lambda e, dst=dst, src=src: e.copy(out=dst, in_=src), reads=[bank], writes=[xT[tb]])

    cT = p.sb([128, 8, 2], F32, "cT")
    for j in range(2):
        p.dma("sp", cT[:, :, j], c2[j, :].rearrange("(kt p) -> p kt", p=128), writes=[cT], allow_slow_non_contiguous=True)
    scT = p.sb([128, 8, 2], F32, "scT")
    p.op("act", lambda e: e.activation(out=scT.ap, in_=cT.ap, func=AF.Silu), reads=[cT], writes=[scT])
    modT = [p.sb([128, 72, 2], F32, "modT%d" % l) for l in range(DEPTH)]
    bmT = [p.sb([128, 72], F32, "bmT%d" % l) for l in range(DEPTH)]
    ngT = [p.sb([128, 24], F32, "ngT%d" % l) for l in range(DEPTH)]
    fngT = p.sb([128, 8], F32, "fngT")
    p.dma("sp", fngT.ap, final_norm_g.rearrange("(kt p) -> p kt", p=128), writes=[fngT], allow_slow_non_contiguous=True)
    B.arena_reset()
    wmod_st = [B.al([128, 8, 512], F32, "wmod_st%d" % i) for i in range(2)]
    for l in range(DEPTH):
        p.dma("sp", bmT[l].ap, b_mod[l, :].rearrange("(o p) -> p o", p=128), writes=[bmT[l]], allow_slow_non_contiguous=True)
        p.dma("sp", ngT[l].ap, norm_g[l].rearrange("m (kt p) -> p (m kt)", p=128), writes=[ngT[l]], allow_slow_non_contiguous=True)
        wv = w_mod[l].rearrange("(kt p) c -> p kt c", p=128)
        for ch in range(18):
            st = wmod_st[ch % 2]
            p.dma("sp" if ch % 2 == 0 else "act", st.ap, wv[:, :, ch * 512:(ch + 1) * 512], writes=[st])
            bank = PS[ch % 8]
            for ob in range(4):
                for kt in range(8):
                    B.mm(bank, bank[:, ob * 2:ob * 2 + 2], st[:, kt, ob * 128:(ob + 1) * 128], scT[:, kt, :],
                         kt == 0, kt == 7, [st, scT])
            o0 = ch * 4
            dst = modT[l][:, o0:o0 + 4, :]
            src = bank[:, 0:8].rearrange("p (o j) -> p o j", j=2)
            bias = bmT[l][:, o0:o0 + 4].unsqueeze(2).to_broadcast([128, 4, 2])
            p.op("dve", lambda e, dst=dst, src=src, bias=bias: e.tensor_tensor(out=dst, in0=src, in1=bias, op=ALU.add),
                 reads=[bank, bmT[l]], writes=[modT[l]])
    gmT = [p.sb([128, 24, 2], F32, "gmT%d" % l) for l in range(DEPTH)]
    gtT = [p.sb([128, 24, 2], F32, "gtT%d" % l) for l in range(DEPTH)]
    for l in range(DEPTH):
        for i in range(3):
            sc = modT[l][:, (3 * i + 1) * 8:(3 * i + 2) * 8, :]
            g = ngT[l][:, i * 8:(i + 1) * 8].unsqueeze(2).to_broadcast([128, 8, 2])
            dst = gmT[l][:, i * 8:(i + 1) * 8, :]
            p.op("dve", lambda e, dst=dst, sc=sc, g=g: e.scalar_tensor_tensor(out=dst, in0=sc, scalar=1.0, in1=g, op0=ALU.add, op1=ALU.mult),
                 reads=[modT[l], ngT[l]], writes=[gmT[l]])
            gt = modT[l][:, (3 * i + 2) * 8:(3 * i + 3) * 8, :]
            dst2 = gtT[l][:, i * 8:(i + 1) * 8, :]
            fac = 1.0 if i == 1 else 0.5
            p.op("dve", lambda e, dst2=dst2, gt=gt, fac=fac: e.tensor_scalar(out=dst2, in0=gt, scalar1=fac, scalar2=None, op0=ALU.mult),
                 reads=[modT[l]], writes=[gtT[l]])
    B.dump("modT0", modT[0], [128, 72, 2])
    B.dump("gmT0", gmT[0], [128, 24, 2])

    S = {}

    def norm_setup():
        S["sq_scr"] = [B.al([128, 512], F32, "sq_scr%d" % i) for i in range(2)]
        S["t_scr"] = [B.al([128, 512], F32, "t_scr%d" % i) for i in range(2)]

    def rms_stats(tb, bank):
        sq_scr = S["sq_scr"]
        for ft in range(8):
            s = sq_scr[ft % 2]
            src = xT[tb][:, ft, :]
            p.op("act", lambda e, s=s, src=src: e.activation(out=s.ap, in_=src, func=AF.Square), reads=[xT[tb]], writes=[s])
            B.mm(bank, bank.ap, B.ones.ap, s.ap, ft == 0, ft == 7, [B.ones, s], inc=True)
        r = rstd[tb]
        p.op("act", lambda e: e.activation(out=r.ap, in_=bank.ap, func=AF.Sqrt, bias=B.eps6[:, 0:1], scale=1.0 / D),
             reads=[bank, B.eps6], writes=[r])
        p.op("dve", lambda e: e.reciprocal(out=r.ap, in_=r.ap), reads=[r], writes=[r])

    def adaln(l, i, tb, h, bank, hap=None):
        hap = h.ap if hap is None else hap
        j = 0 if tb < 2 else 1
        rms_stats(tb, bank)
        t_scr = S["t_scr"]
        for ft in range(8):
            t = t_scr[ft % 2]
            src = xT[tb][:, ft, :]
            p.op("dve", lambda e, t=t, src=src: e.tensor_tensor(out=t.ap, in0=src, in1=rstd[tb].ap, op=ALU.mult),
                 reads=[xT[tb], rstd[tb]], writes=[t])
            gm = gmT[l][:, i * 8 + ft, j:j + 1]
            sh = modT[l][:, (3 * i) * 8 + ft, j:j + 1]
            dst = hap[:, ft, :]
            p.op("act", lambda e, dst=dst, t=t, gm=gm, sh=sh: e.activation(out=dst, in_=t.ap, func=AF.Identity, bias=sh, scale=gm),
                 reads=[t, gmT[l], modT[l]], writes=[h])

    cnt = {"w": 0, "o": 0}

    def ffn(l, f, i):
        B.arena_reset()
        norm_setup()
        hT = [B.al([128, 8, 512], BF16, "hT%d" % k) for k in range(3)]
        wg_st = [B.al([128, 8, 256], BF16, "wg_st%d" % k) for k in range(2)]
        wu_st = [B.al([128, 8, 256], BF16, "wu_st%d" % k) for k in range(2)]
        wo_st = [B.al([128, 22, 128], BF16, "wo_st%d" % k) for k in range(2)]
        aT = [B.al([128, 22, 512], BF16, "aT%d" % k) for k in range(3)]
        sg_scr = [B.al([128, 512], F32, "sg_scr%d" % k) for k in range(2)]
        win = ffn_w_in[l, f].rearrange("(kt p) c -> p kt c", p=128)
        wout = ffn_w_out[l, f].rearrange("(ft p) c -> p ft c", p=128)
        for tb in range(3):
            adaln(l, i, tb, hT[tb], PS[6])
        n = 0
        for ch in range(11):
            k = cnt["w"] % 2
            cnt["w"] += 1
            wg, wu = wg_st[k], wu_st[k]
            p.dma("pool", wg.ap, win[:, :, ch * 256:(ch + 1) * 256], writes=[wg])
            p.dma("pool", wu.ap, win[:, :, DFF + ch * 256:DFF + (ch + 1) * 256], writes=[wu])
            for half in range(2):
                fo = ch * 2 + half
                for tb in range(3):
                    h = hT[tb]
                    bg, bu = PS[(n % 2) * 2], PS[(n % 2) * 2 + 1]
                    for kt in range(8):
                        B.mm(bg, bg.ap, wg[:, kt, half * 128:(half + 1) * 128], h[:, kt, :], kt == 0, kt == 7, [wg, h])
                    for kt in range(8):
                        B.mm(bu, bu.ap, wu[:, kt, half * 128:(half + 1) * 128], h[:, kt, :], kt == 0, kt == 7, [wu, h])
                    s = sg_scr[n % 2]
                    n += 1
                    p.op("act", lambda e, s=s, bg=bg: e.activation(out=s.ap, in_=bg.ap, func=AF.Silu), reads=[bg], writes=[s])
                    dst = aT[tb][:, fo, :]
                    p.op("dve", lambda e, dst=dst, s=s, bu=bu: e.tensor_tensor(out=dst, in0=bu.ap, in1=s.ap, op=ALU.mult),
                         reads=[bu, s], writes=[aT[tb]])
        n = 0
        for ob in range(8):
            k = cnt["o"] % 2
            cnt["o"] += 1
            wo = wo_st[k]
            p.dma("pool", wo.ap, wout[:, :, ob * 128:(ob + 1) * 128], writes=[wo])
            for tb in range(3):
                j = 0 if tb < 2 else 1
                bank = PS[4 + n % 2]
                n += 1
                for ft in range(22):
                    B.mm(bank, bank.ap, wo[:, ft, :], aT[tb][:, ft, :], ft == 0, ft == 21, [wo, aT[tb]])
                gt = gtT[l][:, i * 8 + ob, j:j + 1]
                dst = xT[tb][:, ob, :]
                p.op("dve", lambda e, dst=dst, bank=bank, gt=gt: e.scalar_tensor_tensor(out=dst, in0=bank.ap, scalar=gt, in1=dst, op0=ALU.mult, op1=ALU.add),
                     reads=[bank, gtT[l], xT[tb]], writes=[xT[tb]])

    SEQS = [
        dict(name="a", T=TP, j=1, pi=0, hoff=0, xs=[(2, 0)]),
        dict(name="b", T=TP, j=1, pi=1, hoff=256, xs=[(2, 256)]),
        dict(name="s", T=TS, j=0, pi=None, hoff=0, xs=[(0, 0), (1, 0)]),
    ]
    wcnt = {"n": 0}

    def mixing(l):
        B.arena_reset()
        hT_ = B.al([128, 8, TS], BF16, "hT_")
        wsm = [B.al([128, 8, 64], BF16, "wsm%d" % k) for k in range(4)]
        ring = {"big": None, "nb": 0, "ns": 0}
        yT = B.al([128, 8, TS], BF16, "yT")
        ytmp = B.al([64, 4, 128], BF16, "ytmp")

        def yput(m, t0):
            ev = ytmp.ap.rearrange("p (t hh) n -> p hh t n", hh=2)
            p.dma("sp", yT[0:64, m * 2:m * 2 + 2, t0:t0 + 128], ev[:, 0, :, :], reads=[ytmp], writes=[yT])
            p.dma("sp", yT[64:128, m * 2:m * 2 + 2, t0:t0 + 128], ev[:, 1, :, :], reads=[ytmp], writes=[yT])

        mark = B.arena_off
        winv = w_in_bf[l].rearrange("(kt p) c -> p kt c", p=128)
        KMIX = int(os.environ.get("KMIX", "9"))

        def wbig():
            w = ring["big"][ring["nb"] % 2]
            ring["nb"] += 1
            return w

        def wload(col0, M):
            if M <= 64:
                w = wsm[ring["ns"] % 4]
                ring["ns"] += 1
            else:
                w = wbig()
            p.dma("sp", w[:, :, 0:M], winv[:, :, col0:col0 + M], reads=[wbfT], writes=[w], wpool=True)
            return w

        def proj(col0, M, sq, evac, ta=0, tb_=None):
            T = sq["T"]
            tb_ = T if tb_ is None else tb_
            w = wload(col0, M)
            for s0 in range(ta, tb_, 512):
                n = min(512, tb_ - s0)
                bank = nb()
                for kt in range(8):
                    B.mm(bank, bank[0:M, 0:n], w[:, kt, 0:M], hT_[:, kt, sq["hoff"] + s0:sq["hoff"] + s0 + n], kt == 0, kt == 7, [w, hT_])
                evac(bank, bank[0:M, 0:n], s0, n)

        def scan_pass(sq, dirn, dk, dv, lowrank, scalar, prep, S0, yout, ysum_in, post, cst):
            T = sq["T"]
            G = cst["G"]
            tri, mult, msk, nm = cst["tri"], cst["mult"], cst["ms"], cst["nm"]
            NI = cst["NI"]
            nblk = T // 128
            order = range(nblk) if dirn == 0 else range(nblk - 1, -1, -1)
            dkv = dk + dv
            order = list(order)
            pf = cst.get("pf", False)
            nxt_ops = prep(order[0] * 128, 0) if pf else None
            for oi, bi in enumerate(order):
                t0 = bi * 128
                ops = nxt_ops if pf else prep(t0, 0)
                if pf and oi + 1 < len(order):
                    nxt_ops = prep(order[oi + 1] * 128, (oi + 1) % 2)
                ld, q, k, v = ops["ld"], ops["q"], ops["k"], ops["v"]
                z, b = ops.get("z"), ops.get("b")
                bankA = nb()
                for h in range(4):
                    B.tr(bankA, bankA[:, h * dk:(h + 1) * dk], ld[:, h, :], [ld], inc=(h == 3))
                p.op("act", lambda e, bankA=bankA: e.copy(out=G["ldtm"].ap, in_=bankA[:, 0:4 * dk].rearrange("p (h d) -> p h d", h=4)),
                     reads=[bankA], writes=[G["ldtm"]])
                banks = []
                for vi in range(3):
                    bk = nb()
                    banks.append(bk)
                    for h in range(4):
                        B.mm(bk, bk[0:dk, h * 128:(h + 1) * 128], G["ldtm"][:, h, :], tri[:, vi, :], True, True, [G["ldtm"], tri], inc=(h == 3))
                b_in, b_ex, b_rev = banks
                v3 = lambda t: t.ap.rearrange("p h t -> p (h t)")
                p.op("act", lambda e, b_in=b_in: e.activation(out=v3(G["EGin"]), in_=b_in[0:dk, :], func=AF.Exp), reads=[b_in], writes=[G["EGin"]])
                p.op("act", lambda e, b_rev=b_rev: e.activation(out=v3(G["EGrev"]), in_=b_rev[0:dk, :], func=AF.Exp), reads=[b_rev], writes=[G["EGrev"]])
                if lowrank:
                    p.op("act", lambda e, b_ex=b_ex: e.activation(out=v3(G["EGex"]), in_=b_ex[0:dk, :], func=AF.Exp), reads=[b_ex], writes=[G["EGex"]])
                if scalar:
                    p.op("act", lambda e, b_in=b_in: e.copy(out=v3(G["GinB"]), in_=b_in[0:dk, :]), reads=[b_in], writes=[G["GinB"]])
                    p.op("act", lambda e, b_in=b_in: e.mul(out=v3(G["nGinB"]), in_=b_in[0:dk, :], mul=-1.0), reads=[b_in], writes=[G["nGinB"]])
                    p.op("act", lambda e, b_ex=b_ex: e.copy(out=v3(G["GexB"]), in_=b_ex[0:dk, :]), reads=[b_ex], writes=[G["GexB"]])
                else:
                    p.op("act", lambda e, b_in=b_in: e.activation(out=v3(G["EnGin"]), in_=b_in[0:dk, :], func=AF.Exp, scale=-1.0), reads=[b_in], writes=[G["EnGin"]])
                def ew(eng, dst, a, b_):
                    p.op(eng, lambda e: e.tensor_tensor(out=dst.ap, in0=a.ap, in1=b_.ap, op=ALU.mult), reads=[a, b_], writes=[dst])
                ew("pool", G["QtT"], q, G["EGin"])
                ew("dve", G["KdT"], k, G["EGrev"])
                if lowrank:
                    ew("pool", G["ZtT"], z, G["EGex"])
                    ew("dve", G["BdT"], b, G["EGrev"])
                def cp(eng, dst, src):
                    if eng == "act":
                        p.op("act", lambda e: e.copy(out=dst.ap, in_=src.ap), reads=[src], writes=[dst])
                    else:
                        p.op(eng, lambda e: e.tensor_copy(out=dst.ap, in_=src.ap), reads=[src], writes=[dst])
                if not scalar:
                    ew("pool", G["KrT"], k, G["EnGin"])
                    cp("pool", G["QLb"], G["QtT"])
                    if lowrank:
                        ew("pool", G["BrT"], b, G["EnGin"])
                        cp("act", G["ZLb"], G["ZtT"])
                    QL, KR, ZL, BR = G["QLb"], G["KrT"], G.get("ZLb"), G.get("BrT")
                else:
                    cp("pool", G["QLb"], q)
                    cp("act", G["KRb"], k)
                    if lowrank:
                        cp("pool", G["ZLb"], z)
                        cp("act", G["BRb"], b)
                    QL, KR, ZL, BR = G["QLb"], G["KRb"], G.get("ZLb"), G.get("BRb")
                bankV = nb()
                for h in range(4):
                    B.tr(bankV, bankV[:, h * dv:(h + 1) * dv], v[:, h, :], [v], inc=(h == 3))
                p.op("act", lambda e, bankV=bankV: e.copy(out=G["Vtm"].ap, in_=bankV[:, 0:4 * dv].rearrange("p (h d) -> p h d", h=4)), reads=[bankV], writes=[G["Vtm"]])
                bankK = nb()
                for h in range(4):
                    B.tr(bankK, bankK[:, h * dk:(h + 1) * dk], G["KdT"][:, h, :], [G["KdT"]], inc=(h == 3))
                p.op("dve", lambda e, bankK=bankK: e.tensor_copy(out=G["Kdtm"].ap, in_=bankK[:, 0:4 * dk].rearrange("p (h d) -> p h d", h=4)), reads=[bankK], writes=[G["Kdtm"]])
                if lowrank:
                    bankB = nb()
                    for h in range(4):
                        B.tr(bankB, bankB[:, h * dk:(h + 1) * dk], G["BdT"][:, h, :], [G["BdT"]], inc=(h == 3))
                    p.op("act", lambda e, bankB=bankB: e.copy(out=G["Bdtm"].ap, in_=bankB[:, 0:4 * dk].rearrange("p (h d) -> p h d", h=4)), reads=[bankB], writes=[G["Bdtm"]])
                    bankZ = nb()
                    for h in range(4):
                        B.tr(bankZ, bankZ[:, h * dk:(h + 1) * dk], G["ZtT"][:, h, :], [G["ZtT"]], inc=(h == 3))
                    p.op("dve", lambda e, bankZ=bankZ: e.tensor_copy(out=G["RHS"][:, :, 0:dk], in_=bankZ[:, 0:4 * dk].rearrange("p (h d) -> p h d", h=4)), reads=[bankZ], writes=[G["RHS"]])
                def head_gen(h, KR=KR, QL=QL, ZL=ZL, BR=BR, t0=t0):
                    sl = h % NI
                    M5 = G["mats%d" % sl]
                    LbT_t = G.get("LbT%d" % sl)
                    NT0 = G.get("NT0_%d" % sl)
                    bankM = nb()
                    B.mm(bankM, bankM[:, 0:128], KR[:, h, :], QL[:, h, :], True, True, [KR, QL], inc=not lowrank)
                    if lowrank:
                        B.mm(bankM, bankM[:, 128:256], BR[:, h, :], QL[:, h, :], True, True, [BR, QL], inc=False)
                        B.mm(bankM, bankM[:, 256:384], KR[:, h, :], ZL[:, h, :], True, True, [KR, ZL], inc=False)
                        B.mm(bankM, bankM[:, 384:512], BR[:, h, :], ZL[:, h, :], True, True, [BR, ZL], inc=True)
                        bankL = nb()
                        B.mm(bankL, bankL[:, 0:128], ZL[:, h, :], BR[:, h, :], True, True, [ZL, BR], inc=True)
                    if scalar:
                        bankE = nb()
                        c64 = G["c64"]
                        combos = [(G["nGinB"], G["GinB"], 0), (G["nGinB"], G["GexB"], 1)]
                        for ci, (La, Rb, ni) in enumerate(combos):
                            o = bankE[:, ci * 128:(ci + 1) * 128]
                            B.mm(bankE, o, La[:, h, :], c64.ap, True, False, [La, c64], inc=False)
                            B.mm(bankE, o, c64.ap, Rb[:, h, :], False, False, [Rb, c64], inc=False)
                            B.mm(bankE, o, B.ident.ap, nm[:, ni, :], False, True, [B.ident, nm], inc=False)
                        o = bankE[:, 256:384]
                        B.mm(bankE, o, G["GexB"][:, h, :], c64.ap, True, False, [G["GexB"], c64], inc=False)
                        B.mm(bankE, o, c64.ap, G["nGinB"][:, h, :], False, False, [G["nGinB"], c64], inc=False)
                        B.mm(bankE, o, B.ident.ap, nm[:, 2, :], False, True, [B.ident, nm], inc=True)
                        Em = G["Emat%d" % sl]
                        p.op("act", lambda e, bankE=bankE, Em=Em: e.activation(out=Em.ap.rearrange("p a t -> p (a t)"), in_=bankE[:, 0:384], func=AF.Exp), reads=[bankE], writes=[Em])
                        nmm = 2 if lowrank else 1
                        p.op("dve", lambda e, bankM=bankM, Em=Em, nmm=nmm: e.tensor_tensor(out=M5[:, 0:nmm, :], in0=bankM[:, 0:128 * nmm].rearrange("p (a t) -> p a t", a=nmm),
                                                                                 in1=Em[:, 0:1, :].to_broadcast([128, nmm, 128]), op=ALU.mult), reads=[bankM, Em], writes=[M5])
                        if lowrank:
                            p.op("dve", lambda e, bankM=bankM, Em=Em: e.tensor_tensor(out=M5[:, 2, :], in0=bankM[:, 256:384], in1=Em[:, 1, :], op=ALU.mult), reads=[bankM, Em], writes=[M5])
                            p.op("dve", lambda e, bankM=bankM, Em=Em: e.tensor_tensor(out=LbT_t.ap, in0=bankM[:, 384:512], in1=Em[:, 1, :], op=ALU.mult), reads=[bankM, Em], writes=[LbT_t])
                            p.op("dve", lambda e, bankL=bankL, Em=Em: e.tensor_tensor(out=NT0.ap, in0=bankL[:, 0:128], in1=Em[:, 2, :], op=ALU.mult), reads=[bankL, Em], writes=[NT0])
                    else:
                        nmm = 3 if lowrank else 1
                        p.op("dve", lambda e, bankM=bankM, nmm=nmm: e.tensor_tensor(out=M5[:, 0:nmm, :], in0=bankM[:, 0:128 * nmm].rearrange("p (a t) -> p a t", a=nmm),
                                                                             in1=mult[:, 0:nmm, :], op=ALU.mult), reads=[bankM, mult], writes=[M5])
                        if lowrank:
                            p.op("dve", lambda e, bankM=bankM: e.tensor_tensor(out=LbT_t.ap, in0=bankM[:, 384:512], in1=mult[:, 3, :], op=ALU.mult), reads=[bankM, mult], writes=[LbT_t])
                            p.op("dve", lambda e, bankL=bankL: e.tensor_tensor(out=NT0.ap, in0=bankL[:, 0:128], in1=msk.ap, op=ALU.mult), reads=[bankL, msk], writes=[NT0])
                    MkT = M5[:, 0, :]
                    MbT, LkT = (M5[:, 1, :], M5[:, 2, :]) if lowrank else (None, None)
                    LbT = LbT_t.ap if lowrank else None
                    yield
                    if lowrank:
                        Ncur, NTcur = None, NT0
                        R = G["R0_%d" % sl]
                        p.op("pool", lambda e, R=R, LbT=LbT: e.tensor_tensor(out=R.ap, in0=LbT, in1=B.ident.ap, op=ALU.add), reads=[LbT_t, B.ident], writes=[R])
                        N_ap, N_t = LbT, LbT_t
                        for lv in range(5):
                            if lv % 2 == 0:
                                Pn_ap, Pn_t, PTn = G["P0_%d" % sl].ap, G["P0_%d" % sl], G["PT0_%d" % sl]
                            else:
                                Pn_ap, Pn_t, PTn = LbT_t.ap, LbT_t, NT0
                            Rn = G["R%d_%d" % ((lv + 1) % 2, sl)]
                            bp = nb()
                            last = lv == 4
                            if not last:
                                B.mm(bp, bp[:, 0:128], NTcur.ap, N_ap, True, True, [NTcur, N_t], inc=False)
                            B.mm(bp, bp[:, 128:256], N_ap, NTcur.ap, True, True, [NTcur, N_t], inc=True)
                            if not last:
                                p.op("act", lambda e, bp=bp, Pn_ap=Pn_ap: e.copy(out=Pn_ap, in_=bp[:, 0:128]), reads=[bp], writes=[Pn_t])
                            p.op("act", lambda e, bp=bp, PTn=PTn: e.copy(out=PTn.ap, in_=bp[:, 128:256]), reads=[bp], writes=[PTn])
                            yield
                            br = nb()
                            B.mm(br, br[:, 0:128], PTn.ap, R.ap, True, True, [PTn, R], inc=True)
                            p.op("dve", lambda e, br=br, R=R, Rn=Rn: e.tensor_tensor(out=Rn.ap, in0=br[:, 0:128], in1=R.ap, op=ALU.add), reads=[br, R], writes=[Rn])
                            R = Rn
                            N_ap, N_t, NTcur = Pn_ap, Pn_t, PTn
                            yield
                        bl = nb()
                        B.mm(bl, bl[:, 0:dv], LkT, G["Vtm"][:, h, :], True, True, [M5, G["Vtm"]], inc=True)
                        p.op("act", lambda e, bl=bl, h=h: e.copy(out=G["RHS"][:, h, dk:dkv], in_=bl[:, 0:dv]), reads=[bl], writes=[G["RHS"]])
                        yield
                        bw = nb()
                        B.mm(bw, bw[:, 0:dkv], R.ap, G["RHS"][:, h, :], True, True, [R, G["RHS"]], inc=True)
                        WU = G["WU%d" % sl]
                        p.op("act", lambda e, bw=bw, WU=WU: e.copy(out=WU.ap, in_=bw[:, 0:dkv]), reads=[bw], writes=[WU])
                        yield
                        bq = nb()
                        B.mm(bq, bq[0:dk, 0:128], WU[:, 0:dk], MbT, True, True, [WU, M5], inc=True)
                        p.op("dve", lambda e, bq=bq, h=h: e.tensor_tensor(out=G["QeffT"][:, h, :], in0=bq[0:dk, 0:128], in1=G["QtT"][:, h, :], op=ALU.add),
                             reads=[bq, G["QtT"]], writes=[G["QeffT"]])
                        by = nb()
                        B.mm(by, by[0:dv, 0:128], WU[:, dk:dkv], MbT, True, False, [WU, M5], inc=False)
                        B.mm(by, by[0:dv, 0:128], G["Vtm"][:, h, :], MkT, False, True, [G["Vtm"], M5], inc=True)
                        p.op("act", lambda e, by=by, h=h: e.copy(out=G["Y0T"][:, h, :], in_=by[0:dv, 0:128]), reads=[by], writes=[G["Y0T"]])
                        for c in range(2):
                            r0 = 64 * c
                            gcol = G["EGin"][:, h, (r0 + 63 if dirn == 0 else r0):(r0 + 64 if dirn == 0 else r0 + 1)]
                            ba = nb()
                            B.mm(ba, ba[0:dk, 0:dk], WU[r0:r0 + 64, 0:dk], G["Bdtm"][r0:r0 + 64, h, :], True, True, [WU, G["Bdtm"]], inc=True)
                            p.op("dve", lambda e, ba=ba, gcol=gcol, h=h, c=c: e.scalar_tensor_tensor(out=G["AcT"][:, h * 2 + c, :], in0=B.ident[0:dk, 0:dk], scalar=gcol, in1=ba[0:dk, 0:dk],
                                                                                      op0=ALU.mult, op1=ALU.add), reads=[ba, G["EGin"], B.ident], writes=[G["AcT"]])
                            bb = nb()
                            B.mm(bb, bb[0:dk, 0:dv], G["Bdtm"][r0:r0 + 64, h, :], WU[r0:r0 + 64, dk:dkv], True, False, [WU, G["Bdtm"]], inc=False)
                            B.mm(bb, bb[0:dk, 0:dv], G["Kdtm"][r0:r0 + 64, h, :], G["Vtm"][r0:r0 + 64, h, :], False, True, [G["Kdtm"], G["Vtm"]], inc=True)
                            p.op("act", lambda e, bb=bb, h=h, c=c: e.copy(out=G["Bc"][:, h * 2 + c, :], in_=bb[0:dk, 0:dv]), reads=[bb], writes=[G["Bc"]])
                    else:
                        by = nb()
                        B.mm(by, by[0:dv, 0:128], G["Vtm"][:, h, :], MkT, True, True, [G["Vtm"], M5], inc=True)
                        p.op("act", lambda e, by=by, h=h: e.copy(out=G["Y0T"][:, h, :], in_=by[0:dv, 0:128]), reads=[by], writes=[G["Y0T"]])
                        for c in range(2):
                            r0 = 64 * c
                            bb = nb()
                            B.mm(bb, bb[0:dk, 0:dv], G["Kdtm"][r0:r0 + 64, h, :], G["Vtm"][r0:r0 + 64, h, :], True, True, [G["Kdtm"], G["Vtm"]], inc=True)
                            p.op("act", lambda e, bb=bb, h=h, c=c: e.copy(out=G["Bc"][:, h * 2 + c, :], in_=bb[0:dk, 0:dv]), reads=[bb], writes=[G["Bc"]])
                gens_all = [head_gen(h) for h in range(4)]
                for g0 in range(0, 4, NI):
                    live = gens_all[g0:g0 + NI]
                    while live:
                        for g_ in list(live):
                            try:
                                next(g_)
                            except StopIteration:
                                live.remove(g_)
                QE = G["QeffT"] if lowrank else G["QtT"]
                for c in ((0, 1) if dirn == 0 else (1, 0)):
                    r0 = 64 * c
                    bY = nb()
                    for h in range(4):
                        B.mm(bY, bY[0:dv, h * 64:(h + 1) * 64], S0[:, h, :], QE[:, h, r0:r0 + 64], True, True, [S0, QE], inc=(h == 3))
                    ysrc = bY[0:dv, 0:256].rearrange("p (h t) -> p h t", h=4)
                    if ysum_in is None:
                        p.op("dve", lambda e, ysrc=ysrc, r0=r0, t0=t0: e.tensor_tensor(out=yout[:, :, t0 + r0:t0 + r0 + 64], in0=ysrc, in1=G["Y0T"][:, :, r0:r0 + 64], op=ALU.add),
                             reads=[bY, G["Y0T"]], writes=[yout])
                    else:
                        p.op("dve", lambda e, ysrc=ysrc, r0=r0: e.tensor_tensor(out=G["ysum"][:, :, r0:r0 + 64], in0=ysrc, in1=G["Y0T"][:, :, r0:r0 + 64], op=ALU.add),
                             reads=[bY, G["Y0T"]], writes=[G["ysum"]])
                        p.op("pool", lambda e, r0=r0, t0=t0: e.tensor_tensor(out=G["ysum"][:, :, r0:r0 + 64], in0=G["ysum"][:, :, r0:r0 + 64], in1=ysum_in[:, :, t0 + r0:t0 + r0 + 64], op=ALU.add),
                             reads=[G["ysum"], ysum_in], writes=[G["ysum"]])
                    if lowrank:
                        bS = nb()
                        for h in range(4):
                            B.mm(bS, bS[0:dk, h * dv:(h + 1) * dv], G["AcT"][:, h * 2 + c, :], S0[:, h, :], True, True, [G["AcT"], S0], inc=(h == 3))
                        p.op("dve", lambda e, bS=bS, c=c: e.tensor_tensor(out=S0.ap, in0=bS[0:dk, 0:4 * dv].rearrange("p (h d) -> p h d", h=4),
                                                                        in1=G["Bc"].ap.rearrange("p (h c) d -> p h c d", c=2)[:, :, c, :], op=ALU.add),
                             reads=[bS, G["Bc"], S0], writes=[S0])
                    else:
                        for h in range(4):
                            gcol = G["EGin"][:, h, (r0 + 63 if dirn == 0 else r0):(r0 + 64 if dirn == 0 else r0 + 1)]
                            p.op("dve", lambda e, h=h, c=c, gcol=gcol: e.scalar_tensor_tensor(out=S0[:, h, :], in0=S0[:, h, :], scalar=gcol, in1=G["Bc"][:, h * 2 + c, :], op0=ALU.mult, op1=ALU.add),
                                 reads=[S0, G["EGin"], G["Bc"]], writes=[S0])
                if ysum_in is not None:
                    post(t0, G["ysum"], ops)

        def scan_alloc(dk, dv, lowrank, scalar):
            NI_ = 4
            G = {}
            A = lambda nm_, shp, dt_=F32: G.__setitem__(nm_, B.al(shp, dt_, nm_))
            A("ldtm", [128, 4, dk])
            for nm_ in ("EGin", "EGrev", "QtT", "KdT"):
                A(nm_, [dk, 4, 128])
            if lowrank:
                for nm_ in ("EGex", "ZtT", "BdT"):
                    A(nm_, [dk, 4, 128])
            if scalar:
                for nm_ in ("GinB", "nGinB", "GexB"):
                    A(nm_, [dk, 4, 128])
                for sl_ in range(NI_):
                    A("Emat%d" % sl_, [128, 3, 128])
                A("c64", [dk, 128])
                A("KRb", [dk, 4, 128], BF16)
                if lowrank:
                    A("BRb", [dk, 4, 128], BF16)
                p.op("pool", lambda e: e.memset(G["c64"].ap, 1.0 / dk), writes=[G["c64"]])
            else:
                A("EnGin", [dk, 4, 128])
                A("KrT", [dk, 4, 128], BF16)
                if lowrank:
                    A("BrT", [dk, 4, 128], BF16)
            A("QLb", [dk, 4, 128], BF16)
            if lowrank:
                A("ZLb", [dk, 4, 128], BF16)
            A("Vtm", [128, 4, dv], BF16)
            A("Kdtm", [128, 4, dk], BF16)
            for sl_ in range(NI_):
                A("mats%d" % sl_, [128, 3 if lowrank else 1, 128] if not lowrank else [128, 3, 128], BF16) if lowrank else A("mats%d" % sl_, [128, 2, 128], BF16)
            A("Y0T", [dv, 4, 128])
            A("Bc", [dk, 8, dv])
            A("ysum", [dv, 4, 128])
            if lowrank:
                A("Bdtm", [128, 4, dk], BF16)
                A("RHS", [128, 4, dk + dv])
                for sl_ in range(NI_):
                    for nm_ in ("NT0", "R0", "R1", "P0", "PT0"):
                        A("%s_%d" % (nm_, sl_), [128, 128])
                    A("WU%d" % sl_, [128, dk + dv], BF16)
                    A("LbT%d" % sl_, [128, 128])
                A("QeffT", [dk, 4, 128])
                A("AcT", [dk, 8, dk])
            cst = dict(G=G, NI=NI_, tri=B.al([128, 3, 128], F32, "tri"),
                       mult=None if scalar else B.al([128, 4, 128], F32, "mult"),
                       ms=None if scalar else B.al([128, 128], F32, "ms"),
                       nm=B.al([128, 3, 128], F32, "nm") if scalar else None)
            return cst

        def load_dir_consts(cst, dirn):
            p.dma("sp", cst["tri"].ap, c_tri[dirn], writes=[cst["tri"]])
            if cst["mult"] is not None:
                p.dma("sp", cst["mult"].ap, c_mult[dirn], writes=[cst["mult"]])
                p.dma("sp", cst["ms"].ap, c_ms[dirn], writes=[cst["ms"]])
            if cst["nm"] is not None:
                p.dma("sp", cst["nm"].ap, c_nm[dirn], writes=[cst["nm"]])

        def gla(sq):
            T = sq["T"]
            dk, dv = 32, 64
            cst = scan_alloc(dk, dv, False, False)
            cst["pf"] = True
            yf = B.al([dv, 4, T], BF16, "yf")
            S0 = B.al([dk, 4, dv], F32, "S0")
            Pq_ = [B.al([dk, 4, 128], F32, "Pq%d" % k_) for k_ in range(2)]
            Pk_ = [B.al([dk, 4, 128], F32, "Pk%d" % k_) for k_ in range(2)]
            Pv_ = [B.al([dv, 4, 128], F32, "Pv%d" % k_) for k_ in range(2)]
            Pr_ = [B.al([dv, 4, 128], F32, "Pr%d" % k_) for k_ in range(2)]
            ad = B.al([16, 128], F32, "ad")
            ee = B.al([dk, 4, 128], F32, "ee")
            ld_ = [B.al([dk, 4, 128], F32, "ld%d" % k_) for k_ in range(2)]
            aup = B.al([16, 128], F32, "aup")
            nab = B.al([dk, 4], F32, "nab")
            gng = B.al([dv, 1], F32, "gng")
            sqs = B.al([dv, 512], F32, "sqs")
            rr_ = B.al([dv, 512], F32, "rr_")
            p.dma("sp", gng.ap, gla_norm_g[l:l + 1, :].rearrange("o d -> d o"), writes=[gng], allow_slow_non_contiguous=True)
            for dirn in range(2):
                load_dir_consts(cst, dirn)
                p.dma("sp", aup.ap, gla_alpha_up[l, dirn], writes=[aup])
                p.dma("sp", nab.ap, gla_alpha_b[l, dirn].rearrange("(h d) -> d h", d=dk), writes=[nab], allow_slow_non_contiguous=True)
                p.op("dve", lambda e: e.tensor_scalar(out=nab.ap, in0=nab.ap, scalar1=-1.0, scalar2=None, op0=ALU.mult), reads=[nab], writes=[nab])
                if sq["pi"] is None:
                    p.dma("sp", S0.ap, state_gla[l, dirn].rearrange("h k v -> k h v"), writes=[S0])
                else:
                    p.op("pool", lambda e: e.memset(S0.ap, 0.0), writes=[S0])

                def prep(t0, sl, dirn=dirn):
                    Pq, Pk, Pv, Pr, ld = Pq_[sl], Pk_[sl], Pv_[sl], Pr_[sl], ld_[sl]
                    for h in range(4):
                        proj(B0 + 32 * h, 32, sq, lambda bank, ap, s0, n, h=h: p.op("act", lambda e: e.mul(out=Pq[:, h, :], in_=ap, mul=float(32 ** -0.5)), reads=[bank], writes=[Pq]), t0, t0 + 128)
                        proj(B0 + 128 + 32 * h, 32, sq, lambda bank, ap, s0, n, h=h: p.op("act", lambda e: e.copy(out=Pk[:, h, :], in_=ap), reads=[bank], writes=[Pk]), t0, t0 + 128)
                        proj(B0 + 256 + 64 * h, 64, sq, lambda bank, ap, s0, n, h=h: p.op("dve", lambda e: e.tensor_copy(out=Pv[:, h, :], in_=ap), reads=[bank], writes=[Pv]), t0, t0 + 128)
                        if dirn == 1:
                            proj(B0 + 544 + 64 * h, 64, sq, lambda bank, ap, s0, n, h=h: p.op("act", lambda e: e.activation(out=Pr[:, h, :], in_=ap, func=AF.Silu), reads=[bank], writes=[Pr]), t0, t0 + 128)
                    proj(B0 + 512 + 16 * dirn, 16, sq, lambda bank, ap, s0, n: p.op("act", lambda e: e.copy(out=ad.ap, in_=ap), reads=[bank], writes=[ad]), t0, t0 + 128)
                    bk = nb()
                    for h in range(4):
                        B.mm(bk, bk[0:dk, h * 128:(h + 1) * 128], aup[:, h * dk:(h + 1) * dk], ad.ap, True, True, [aup, ad], inc=(h == 3))
                    for h in range(4):
                        p.op("act", lambda e, h=h, bk=bk: e.activation(out=ee[:, h, :], in_=bk[0:dk, h * 128:(h + 1) * 128], func=AF.Exp, bias=nab[:, h:h + 1], scale=-1.0),
                             reads=[bk, nab], writes=[ee])
                    p.op("act", lambda e: e.activation(out=ld.ap, in_=ee.ap, func=AF.Ln, bias=B.one1[0:dk, 0:1], scale=1.0), reads=[ee, B.one1], writes=[ld])
                    p.op("pool", lambda e: e.tensor_scalar(out=ld.ap, in0=ld.ap, scalar1=-1.0 / 16.0, scalar2=None, op0=ALU.mult), reads=[ld], writes=[ld])
                    return dict(ld=ld, q=Pq, k=Pk, v=Pv, r=Pr)

                def post(t0, ys, ops):
                    yv = ys.ap.rearrange("p h t -> p (h t)")
                    p.op("act", lambda e: e.activation(out=sqs.ap, in_=yv, func=AF.Square), reads=[ys], writes=[sqs])
                    bk = nb()
                    B.mm(bk, bk[0:dv, :], B.ones[0:dv, 0:dv], sqs.ap, True, True, [B.ones, sqs], inc=True)
                    p.op("act", lambda e, bk=bk: e.activation(out=rr_.ap, in_=bk[0:dv, :], func=AF.Sqrt, bias=B.eps6[0:dv, 0:1], scale=1.0 / dv), reads=[bk, B.eps6], writes=[rr_])
                    p.op("dve", lambda e: e.reciprocal(out=rr_.ap, in_=rr_.ap), reads=[rr_], writes=[rr_])
                    p.op("dve", lambda e: e.scalar_tensor_tensor(out=sqs.ap, in0=yv, scalar=gng[:, 0:1], in1=rr_.ap, op0=ALU.mult, op1=ALU.mult), reads=[ys, gng, rr_], writes=[sqs])
                    p.op("pool", lambda e: e.tensor_tensor(out=ytmp.ap, in0=sqs.ap.rearrange("p (h t) -> p h t", h=4), in1=ops["r"].ap, op=ALU.mult),
                         reads=[sqs, ops["r"]], writes=[ytmp])
                    yput(1, t0)

                scan_pass(sq, dirn, dk, dv, False, False, prep, S0, yf, None if dirn == 0 else yf, post, cst)
                if sq["pi"] is not None:
                    p.dma("sp", st_gla[sq["pi"], l, dirn].rearrange("h k v -> k h v"), S0.ap, reads=[S0])

        def rwkv(sq):
            T = sq["T"]
            dk, dv = 64, 64
            cst = scan_alloc(dk, dv, True, False)
            yf = B.al([dv, 4, T], BF16, "yf")
            S0 = B.al([dk, 4, dv], F32, "S0")
            raw = [B.al([64, 130], F32, "raw%d" % k_) for k_ in range(1)]
            lt_ = B.al([64, 128], F32, "lt_")
            Pr_ = [B.al([64, 4, 128], F32, "Pr%d" % k_) for k_ in range(1)]
            Pk_ = [B.al([64, 4, 128], F32, "Pk%d" % k_) for k_ in range(1)]
            Pv_ = [B.al([64, 4, 128], F32, "Pv%d" % k_) for k_ in range(1)]
            kk_ = [B.al([64, 4, 128], F32, "kk%d" % k_) for k_ in range(1)]
            ld_ = [B.al([64, 4, 128], F32, "ld%d" % k_) for k_ in range(1)]
            aa = B.al([64, 4, 128], F32, "aa")
            af_ = [B.al([64, 4, 128], F32, "af%d" % k_) for k_ in range(1)]
            bop_ = [B.al([64, 4, 128], F32, "bop%d" % k_) for k_ in range(1)]
            gT_ = [B.al([64, 4, 128], F32, "gT%d" % k_) for k_ in range(1)]
            wd = B.al([64, 128], F32, "wd")
            adt = B.al([64, 128], F32, "adt")
            adf = B.al([64, 128], F32, "adf")
            sgd = [B.al([64, 128], F32, "sgd%d" % k_) for k_ in range(3)]
            sqs = B.al([64, 512], F32, "sqs")
            rr_ = B.al([64, 512], F32, "rr_")
            wup = B.al([64, 256], F32, "wup")
            aup = B.al([64, 256], F32, "aup")
            aupf = B.al([64, 256], F32, "aupf")
            gup = [B.al([64, 256], F32, "gup%d" % k_) for k_ in range(3)]
            hm = B.al([64, 19], F32, "hm")
            omm = B.al([64, 19], F32, "omm")
            prm = B.al([64, 8, 4], F32, "prm")
            epsg = B.al([64, 1], F32, "epsg")
            stt_ap = sqs.ap[:, 0:256].rearrange("p (h d) -> p h d", h=4)
            p.op("pool", lambda e: e.memset(hm.ap, 0.0), writes=[hm])
            p.dma("sp", hm[:, 0:18], rwkv_mu[l, 0:1152].rearrange("(g d) -> d g", d=64), writes=[hm], allow_slow_non_contiguous=True)
            p.dma("sp", hm[0:32, 18:19], rwkv_mu[l, 1152:1184].rearrange("(g d) -> d g", d=32), writes=[hm], allow_slow_non_contiguous=True)
            p.op("dve", lambda e: e.tensor_scalar(out=omm.ap, in0=hm.ap, scalar1=-1.0, scalar2=1.0, op0=ALU.mult, op1=ALU.add), reads=[hm], writes=[omm])
            p.op("dve", lambda e: e.tensor_scalar(out=hm.ap, in0=hm.ap, scalar1=0.5, scalar2=None, op0=ALU.mult), reads=[hm], writes=[hm])
            hd = lambda apx: apx.rearrange("(h d) -> d h", d=64)
            p.dma("sp", prm[:, 0, :], hd(rwkv_k_k[l]), writes=[prm], allow_slow_non_contiguous=True)
            p.dma("sp", prm[:, 1, :], hd(rwkv_k_a[l]), writes=[prm], allow_slow_non_contiguous=True)
            p.dma("sp", prm[:, 2, :], rwkv_r_k[l].rearrange("h d -> d h"), writes=[prm], allow_slow_non_contiguous=True)
            p.dma("sp", prm[:, 5, :], hd(rwkv_a0[l, 0]), writes=[prm], allow_slow_non_contiguous=True)
            p.dma("sp", prm[:, 6, :], hd(rwkv_ln_g[l]), writes=[prm], allow_slow_non_contiguous=True)
            p.dma("sp", prm[:, 7, :], hd(rwkv_ln_b[l]), writes=[prm], allow_slow_non_contiguous=True)
            p.op("pool", lambda e: e.memset(epsg.ap, 64e-5), writes=[epsg])
            p.dma("sp", aupf.ap, rwkv_a_up[l, 0], writes=[aupf])
            p.dma("sp", gup[0].ap, rwkv_g_up[l, 0:64, :], writes=[gup[0]])
            p.dma("sp", gup[1].ap, rwkv_g_up[l, 64:128, :], writes=[gup[1]])
            p.dma("sp", gup[2][0:32, :], rwkv_g_up[l, 128:160, :], writes=[gup[2]])
            rawc = {"n": 0}
            bc = lambda col: prm[:, col, :].unsqueeze(2).to_broadcast([64, 4, 128])

            def lerp_group(col0, M, g, dst_ap, dst_t, t0):
                r = raw[0]
                rawc["n"] += 1
                ta, tb_ = max(t0 - 1, 0), min(t0 + 129, T)
                o = ta - (t0 - 1)
                if o > 0 or tb_ < t0 + 129:
                    p.op("pool", lambda e, r=r: e.memset(r.ap, 0.0), writes=[r])
                proj(col0, M, sq, lambda bank, ap, s0, n, r=r, o=o: p.op("act", lambda e: e.copy(out=r[0:M, o:o + n], in_=ap), reads=[bank], writes=[r]), ta, tb_)
                p.op("pool", lambda e, r=r: e.tensor_tensor(out=lt_[0:M, :], in0=r[0:M, 0:128], in1=r[0:M, 2:130], op=ALU.add), reads=[r], writes=[lt_])
                p.op("dve", lambda e: e.tensor_scalar(out=lt_[0:M, :], in0=lt_[0:M, :], scalar1=hm[0:M, g:g + 1], scalar2=None, op0=ALU.mult), reads=[lt_, hm], writes=[lt_])
                p.op("dve", lambda e, r=r: e.scalar_tensor_tensor(out=dst_ap, in0=r[0:M, 1:129], scalar=omm[0:M, g:g + 1], in1=lt_[0:M, :], op0=ALU.mult, op1=ALU.add), reads=[r, omm, lt_], writes=[dst_t])

            def l2n(x):
                xv = x.ap.rearrange("p h t -> p (h t)")
                p.op("act", lambda e: e.activation(out=sqs.ap, in_=xv, func=AF.Square), reads=[x], writes=[sqs])
                bk = nb()
                B.mm(bk, bk[0:64, :], B.ones[0:64, 0:64], sqs.ap, True, True, [B.ones, sqs], inc=True)
                p.op("act", lambda e, bk=bk: e.activation(out=rr_.ap, in_=bk[0:64, :], func=AF.Sqrt, bias=B.eps6[0:64, 0:1], scale=1.0), reads=[bk, B.eps6], writes=[rr_])
                p.op("dve", lambda e: e.reciprocal(out=rr_.ap, in_=rr_.ap), reads=[rr_], writes=[rr_])
                p.op("dve", lambda e: e.tensor_tensor(out=xv, in0=xv, in1=rr_.ap, op=ALU.mult), reads=[x, rr_], writes=[x])

            def lora_sig(dst, src, wmat, bias_col):
                bk = nb()
                for h in range(4):
                    B.mm(bk, bk[0:64, h * 128:(h + 1) * 128], wmat[:, h * 64:(h + 1) * 64], src.ap, True, True, [wmat, src], inc=(h == 3))
                for h in range(4):
                    p.op("act", lambda e, h=h, bk=bk: e.activation(out=dst[:, h, :], in_=bk[0:64, h * 128:(h + 1) * 128], func=AF.Sigmoid, bias=prm[:, bias_col, h:h + 1], scale=1.0),
                         reads=[bk, prm], writes=[dst])

            for dirn in range(2):
                load_dir_consts(cst, dirn)
                p.dma("sp", wup.ap, rwkv_w_up[l, dirn], writes=[wup])
                p.dma("sp", aup.ap, rwkv_a_up[l, dirn], writes=[aup])
                p.dma("sp", prm[:, 3, :], hd(rwkv_w0[l, dirn]), writes=[prm], allow_slow_non_contiguous=True)
                p.dma("sp", prm[:, 4, :], hd(rwkv_a0[l, dirn]), writes=[prm], allow_slow_non_contiguous=True)
                if sq["pi"] is None:
                    p.dma("sp", stt_ap, state_rwkv[l, dirn].rearrange("h v k -> v h k"), writes=[sqs])
                    bk = nb()
                    for h in range(4):
                        B.tr(bk, bk[0:64, h * 64:(h + 1) * 64], stt_ap[:, h, :], [sqs], inc=(h == 3))
                    p.op("act", lambda e, bk=bk: e.copy(out=S0.ap, in_=bk[0:64, 0:256].rearrange("p (h d) -> p h d", h=4)), reads=[bk], writes=[S0])
                else:
                    p.op("pool", lambda e: e.memset(S0.ap, 0.0), writes=[S0])

                def prep(t0, sl, dirn=dirn):
                    Pr, Pk, Pv, kk, ld, af, bop, gT = Pr_[0], Pk_[0], Pv_[0], kk_[0], ld_[0], af_[0], bop_[0], gT_[0]
                    for h in range(4):
                        lerp_group(A0 + 64 * h, 64, h, Pr[:, h, :], Pr, t0)
                        lerp_group(A0 + 256 + 64 * h, 64, 4 + h, Pk[:, h, :], Pk, t0)
                        lerp_group(A0 + 512 + 64 * h, 64, 8 + h, Pv[:, h, :], Pv, t0)
                    lerp_group(A0 + 768 + 64 * dirn, 64, 12 + dirn, wd.ap, wd, t0)
                    lerp_group(A0 + 896 + 64 * dirn, 64, 14 + dirn, adt.ap, adt, t0)
                    p.op("dve", lambda e: e.tensor_tensor(out=kk.ap, in0=Pk.ap, in1=bc(0), op=ALU.mult), reads=[Pk, prm], writes=[kk])
                    l2n(kk)
                    p.op("act", lambda e: e.activation(out=wd.ap, in_=wd.ap, func=AF.Tanh), reads=[wd], writes=[wd])
                    lora_sig(ld, wd, wup, 3)
                    p.op("pool", lambda e: e.tensor_scalar(out=ld.ap, in0=ld.ap, scalar1=-float(np.exp(-0.5)), scalar2=None, op0=ALU.mult), reads=[ld], writes=[ld])
                    lora_sig(aa, adt, aup, 4)
                    if dirn == 1:
                        lerp_group(A0 + 896, 64, 14, adf.ap, adf, t0)
                        lora_sig(af, adf, aupf, 5)
                        for gi, (c0_, M_) in enumerate(((1024, 64), (1088, 64), (1152, 32))):
                            lerp_group(A0 + c0_, M_, 16 + gi, sgd[gi][0:M_, :], sgd[gi], t0)
                            p.op("act", lambda e, gi=gi, M_=M_: e.activation(out=sgd[gi][0:M_, :], in_=sgd[gi][0:M_, :], func=AF.Sigmoid), reads=[sgd[gi]], writes=[sgd[gi]])
                        bk = nb()
                        for h in range(4):
                            for gi, M_ in enumerate((64, 64, 32)):
                                B.mm(bk, bk[0:64, h * 128:(h + 1) * 128], gup[gi][0:M_, h * 64:(h + 1) * 64], sgd[gi][0:M_, :], gi == 0, gi == 2, [gup[gi], sgd[gi]], inc=(h == 3 and gi == 2))
                        p.op("act", lambda e, bk=bk: e.copy(out=gT.ap.rearrange("p h t -> p (h t)"), in_=bk[0:64, :]), reads=[bk], writes=[gT])
                        p.op("pool", lambda e: e.tensor_tensor(out=af.ap, in0=af.ap, in1=aa.ap, op=ALU.add), reads=[af, aa], writes=[af])
                        p.op("dve", lambda e: e.scalar_tensor_tensor(out=af.ap, in0=af.ap, scalar=-2.0, in1=bc(1), op0=ALU.add, op1=ALU.mult), reads=[af, prm], writes=[af])
                        p.op("dve", lambda e: e.scalar_tensor_tensor(out=af.ap, in0=af.ap, scalar=2.0, in1=Pk.ap, op0=ALU.add, op1=ALU.mult), reads=[af, Pk], writes=[af])
                        p.op("pool", lambda e: e.tensor_tensor(out=af.ap, in0=af.ap, in1=Pr.ap, op=ALU.mult), reads=[af, Pr], writes=[af])
                        p.op("pool", lambda e: e.tensor_tensor(out=af.ap, in0=af.ap, in1=bc(2), op=ALU.mult), reads=[af, prm], writes=[af])
                        bk2 = nb()
                        B.mm(bk2, bk2[0:64, :], B.ones[0:64, 0:64], af.ap.rearrange("p h t -> p (h t)"), True, True, [B.ones, af], inc=True)
                        p.op("dve", lambda e, bk2=bk2: e.tensor_tensor(out=af.ap.rearrange("p h t -> p (h t)"), in0=bk2[0:64, :], in1=Pv.ap.rearrange("p h t -> p (h t)"), op=ALU.mult), reads=[bk2, Pv], writes=[af])
                    p.op("pool", lambda e: e.tensor_tensor(out=bop.ap, in0=kk.ap, in1=aa.ap, op=ALU.mult), reads=[kk, aa], writes=[bop])
                    p.op("pool", lambda e: e.tensor_scalar(out=kk.ap, in0=kk.ap, scalar1=-1.0, scalar2=None, op0=ALU.mult), reads=[kk], writes=[kk])
                    p.op("dve", lambda e: e.scalar_tensor_tensor(out=aa.ap, in0=aa.ap, scalar=-1.0, in1=bc(1), op0=ALU.add, op1=ALU.mult), reads=[aa, prm], writes=[aa])
                    p.op("dve", lambda e: e.scalar_tensor_tensor(out=Pk.ap, in0=aa.ap, scalar=1.0, in1=Pk.ap, op0=ALU.add, op1=ALU.mult), reads=[aa, Pk], writes=[Pk])
                    return dict(ld=ld, q=Pr, k=Pk, v=Pv, z=kk, b=bop, bonus=af, g=gT)

                def post(t0, ys, ops):
                    yv = ys.ap.rearrange("p h t -> p (h t)")
                    bk = nb()
                    B.mm(bk, bk[0:64, :], B.ones[0:64, 0:64], yv, True, True, [B.ones, ys], inc=True)
                    p.op("dve", lambda e, bk=bk: e.scalar_tensor_tensor(out=yv, in0=bk[0:64, :], scalar=-1.0 / 64, in1=yv, op0=ALU.mult, op1=ALU.add), reads=[bk, ys], writes=[ys])
                    p.op("act", lambda e: e.activation(out=sqs.ap, in_=yv, func=AF.Square), reads=[ys], writes=[sqs])
                    bk2 = nb()
                    B.mm(bk2, bk2[0:64, :], B.ones[0:64, 0:64], sqs.ap, True, True, [B.ones, sqs], inc=True)
                    p.op("act", lambda e, bk2=bk2: e.activation(out=rr_.ap, in_=bk2[0:64, :], func=AF.Sqrt, bias=epsg[:, 0:1], scale=1.0 / 64), reads=[bk2, epsg], writes=[rr_])
                    p.op("dve", lambda e: e.reciprocal(out=rr_.ap, in_=rr_.ap), reads=[rr_], writes=[rr_])
                    p.op("dve", lambda e: e.tensor_tensor(out=yv, in0=yv, in1=rr_.ap, op=ALU.mult), reads=[ys, rr_], writes=[ys])
                    p.op("pool", lambda e: e.tensor_tensor(out=ys.ap, in0=ys.ap, in1=bc(6), op=ALU.mult), reads=[ys, prm], writes=[ys])
                    p.op("pool", lambda e: e.tensor_tensor(out=ys.ap, in0=ys.ap, in1=bc(7), op=ALU.add), reads=[ys, prm], writes=[ys])
                    p.op("pool", lambda e: e.tensor_tensor(out=ys.ap, in0=ys.ap, in1=ops["bonus"].ap, op=ALU.add), reads=[ys, ops["bonus"]], writes=[ys])
                    p.op("pool", lambda e: e.tensor_tensor(out=ytmp.ap, in0=ys.ap, in1=ops["g"].ap, op=ALU.mult), reads=[ys, ops["g"]], writes=[ytmp])
                    yput(0, t0)

                scan_pass(sq, dirn, dk, dv, True, False, prep, S0, yf, None if dirn == 0 else yf, post, cst)
                if sq["pi"] is not None:
                    bk = nb()
                    for h in range(4):
                        B.tr(bk, bk[0:64, h * 64:(h + 1) * 64], S0[:, h, :], [S0], inc=(h == 3))
                    p.op("act", lambda e, bk=bk: e.copy(out=stt_ap, in_=bk[0:64, 0:256].rearrange("p (h d) -> p h d", h=4)), reads=[bk], writes=[sqs])
                    p.dma("sp", st_rwkv[sq["pi"], l, dirn].rearrange("h v k -> v h k"), stt_ap, reads=[sqs])

        def delta(sq):
            T = sq["T"]
            dk, dv = 64, 64
            cst = scan_alloc(dk, dv, True, True)
            yf = B.al([dv, 4, T], BF16, "yf")
            S0 = B.al([dk, 4, dv], F32, "S0")
            raw = [B.al([64, 130], F32, "raw%d" % k_) for k_ in range(2)]
            cq_ = [B.al([64, 4, 128], F32, "cq%d" % k_) for k_ in range(1)]
            ck_ = [B.al([64, 4, 128], F32, "ck%d" % k_) for k_ in range(1)]
            cv_ = [B.al([64, 4, 128], F32, "cv%d" % k_) for k_ in range(1)]
            sqs = B.al([64, 512], F32, "sqs")
            rr_ = B.al([64, 512], F32, "rr_")
            X16 = B.al([16, 128], F32, "X16")
            betaB = B.al([64, 4, 128], F32, "betaB")
            ld_ = [B.al([64, 4, 128], F32, "ld%d" % k_) for k_ in range(1)]
            nbt = B.al([64, 4, 128], F32, "nbt")
            bop_ = [B.al([64, 4, 128], F32, "bop%d" % k_) for k_ in range(1)]
            Pg_ = [B.al([64, 4, 128], F32, "Pg%d" % k_) for k_ in range(1)]
            sel = B.al([16, 16, 64], F32, "sel")
            wc = B.al([64, 3, 12], F32, "wc")
            pa = B.al([64, 8], F32, "pa")
            pb = B.al([64, 8], F32, "pb")
            dng = B.al([64, 1], F32, "dng")
            p.dma("sp", sel.ap, c_sel, writes=[sel])
            p.dma("sp", wc.ap, delta_conv[l].rearrange("k (g d) -> d k g", d=64), writes=[wc], allow_slow_non_contiguous=True)
            p.dma("sp", pa.ap, delta_a_log[l].rearrange("d h -> (d h)").partition_broadcast(64), writes=[pa])
            p.dma("sp", pb.ap, delta_dt_bias[l].rearrange("d h -> (d h)").partition_broadcast(64), writes=[pb])
            p.dma("sp", dng.ap, delta_norm_g[l:l + 1, :].rearrange("o d -> d o"), writes=[dng], allow_slow_non_contiguous=True)
            p.op("act", lambda e: e.activation(out=pa.ap, in_=pa.ap, func=AF.Exp), reads=[pa], writes=[pa])
            p.op("dve", lambda e: e.tensor_scalar(out=pa.ap, in0=pa.ap, scalar1=-1.0, scalar2=None, op0=ALU.mult), reads=[pa], writes=[pa])
            rawc = {"n": 0}

            def conv_group(col0, g, dst_ap, dst_t, t0):
                r = raw[rawc["n"] % 2]
                rawc["n"] += 1
                ta, tb_ = max(t0 - 1, 0), min(t0 + 129, T)
                o = ta - (t0 - 1)
                if o > 0 or tb_ < t0 + 129:
                    p.op("pool", lambda e, r=r: e.memset(r.ap, 0.0), writes=[r])
                proj(col0, 64, sq, lambda bank, ap, s0, n, r=r, o=o: p.op("act", lambda e: e.copy(out=r[:, o:o + n], in_=ap), reads=[bank], writes=[r]), ta, tb_)
                p.op("dve", lambda e, r=r: e.tensor_scalar(out=dst_ap, in0=r[:, 0:128], scalar1=wc[:, 0, g:g + 1], scalar2=None, op0=ALU.mult), reads=[r, wc], writes=[dst_t])
                p.op("dve", lambda e, r=r: e.scalar_tensor_tensor(out=dst_ap, in0=r[:, 1:129], scalar=wc[:, 1, g:g + 1], in1=dst_ap, op0=ALU.mult, op1=ALU.add), reads=[r, wc, dst_t], writes=[dst_t])
                p.op("dve", lambda e, r=r: e.scalar_tensor_tensor(out=dst_ap, in0=r[:, 2:130], scalar=wc[:, 2, g:g + 1], in1=dst_ap, op0=ALU.mult, op1=ALU.add), reads=[r, wc, dst_t], writes=[dst_t])

            def l2n(x, scale):
                xv = x.ap.rearrange("p h t -> p (h t)")
                p.op("act", lambda e: e.activation(out=sqs.ap, in_=xv, func=AF.Square), reads=[x], writes=[sqs])
                bk = nb()
                B.mm(bk, bk[0:64, :], B.ones[0:64, 0:64], sqs.ap, True, True, [B.ones, sqs], inc=True)
                p.op("act", lambda e, bk=bk: e.activation(out=rr_.ap, in_=bk[0:64, :], func=AF.Sqrt, bias=B.eps6[0:64, 0:1], scale=1.0), reads=[bk, B.eps6], writes=[rr_])
                p.op("dve", lambda e: e.reciprocal(out=rr_.ap, in_=rr_.ap), reads=[rr_], writes=[rr_])
                p.op("dve", lambda e: e.scalar_tensor_tensor(out=xv, in0=xv, scalar=float(scale), in1=rr_.ap, op0=ALU.mult, op1=ALU.mult), reads=[x, rr_], writes=[x])

            for dirn in range(2):
                load_dir_consts(cst, dirn)
                if sq["pi"] is None:
                    p.dma("sp", S0.ap, state_delta[l, dirn].rearrange("h k v -> k h v"), writes=[S0])
                else:
                    p.op("pool", lambda e: e.memset(S0.ap, 0.0), writes=[S0])

                def prep(t0, sl, dirn=dirn):
                    cq, ck, cv, ld, bop, Pg = cq_[0], ck_[0], cv_[0], ld_[0], bop_[0], Pg_[0]
                    for h in range(4):
                        conv_group(C0 + 64 * h, h, cq[:, h, :], cq, t0)
                        conv_group(C0 + 256 + 64 * h, 4 + h, ck[:, h, :], ck, t0)
                        conv_group(C0 + 512 + 64 * h, 8 + h, cv[:, h, :], cv, t0)
                        if dirn == 1:
                            proj(C0 + 768 + 64 * h, 64, sq, lambda bank, ap, s0, n, h=h: p.op("act", lambda e: e.activation(out=Pg[:, h, :], in_=ap, func=AF.Silu), reads=[bank], writes=[Pg]), t0, t0 + 128)
                    for x in (cq, ck, cv):
                        p.op("act", lambda e, x=x: e.activation(out=x.ap, in_=x.ap, func=AF.Silu), reads=[x], writes=[x])
                    l2n(cq, 0.125)
                    l2n(ck, 1.0)
                    proj(C0 + 1024, 16, sq, lambda bank, ap, s0, n: p.op("act", lambda e: e.copy(out=X16.ap, in_=ap), reads=[bank], writes=[X16]), t0, t0 + 128)
                    bkb, bka = nb(), nb()
                    for h in range(4):
                        B.mm(bkb, bkb[0:64, h * 128:(h + 1) * 128], sel[:, dirn * 4 + h, :], X16.ap, True, True, [sel, X16], inc=(h == 3))
                    for h in range(4):
                        B.mm(bka, bka[0:64, h * 128:(h + 1) * 128], sel[:, 8 + dirn * 4 + h, :], X16.ap, True, True, [sel, X16], inc=(h == 3))
                    p.op("act", lambda e, bkb=bkb: e.activation(out=betaB.ap.rearrange("p h t -> p (h t)"), in_=bkb[0:64, :], func=AF.Sigmoid), reads=[bkb], writes=[betaB])
                    for h in range(4):
                        p.op("act", lambda e, bka=bka, h=h: e.activation(out=ld[:, h, :], in_=bka[0:64, h * 128:(h + 1) * 128], func=AF.Exp, bias=pb[:, dirn * 4 + h:dirn * 4 + h + 1], scale=1.0),
                             reads=[bka, pb], writes=[ld])
                    p.op("act", lambda e: e.activation(out=ld.ap, in_=ld.ap, func=AF.Ln, bias=B.one1[0:64, 0:1], scale=1.0), reads=[ld, B.one1], writes=[ld])
                    for h in range(4):
                        p.op("dve", lambda e, h=h: e.tensor_scalar(out=ld[:, h, :], in0=ld[:, h, :], scalar1=pa[:, dirn * 4 + h:dirn * 4 + h + 1], scalar2=None, op0=ALU.mult), reads=[ld, pa], writes=[ld])
                    p.op("act", lambda e: e.activation(out=nbt.ap, in_=ld.ap, func=AF.Exp), reads=[ld], writes=[nbt])
                    p.op("dve", lambda e: e.scalar_tensor_tensor(out=nbt.ap, in0=betaB.ap, scalar=-1.0, in1=nbt.ap, op0=ALU.mult, op1=ALU.mult), reads=[betaB, nbt], writes=[nbt])
                    p.op("pool", lambda e: e.tensor_tensor(out=bop.ap, in0=ck.ap, in1=nbt.ap, op=ALU.mult), reads=[ck, nbt], writes=[bop])
                    p.op("pool", lambda e: e.tensor_tensor(out=cv.ap, in0=cv.ap, in1=betaB.ap, op=ALU.mult), reads=[cv, betaB], writes=[cv])
                    if t0 == 0 and dirn == 0:
                        for nm_, tl in (("d_ld", ld), ("d_q", cq), ("d_k", ck), ("d_v", cv), ("d_b", bop)):
                            B.dump(nm_ + "_" + sq["name"], tl, [64, 4, 128])
                    return dict(ld=ld, q=cq, k=ck, v=cv, z=ck, b=bop, r=Pg)

                def post(t0, ys, ops):
                    yv = ys.ap.rearrange("p h t -> p (h t)")
                    p.op("act", lambda e: e.activation(out=sqs.ap, in_=yv, func=AF.Square), reads=[ys], writes=[sqs])
                    bk = nb()
                    B.mm(bk, bk[0:dv, :], B.ones[0:dv, 0:dv], sqs.ap, True, True, [B.ones, sqs], inc=True)
                    p.op("act", lambda e, bk=bk: e.activation(out=rr_.ap, in_=bk[0:dv, :], func=AF.Sqrt, bias=B.eps6[0:dv, 0:1], scale=1.0 / dv), reads=[bk, B.eps6], writes=[rr_])
                    p.op("dve", lambda e: e.reciprocal(out=rr_.ap, in_=rr_.ap), reads=[rr_], writes=[rr_])
                    p.op("dve", lambda e: e.scalar_tensor_tensor(out=sqs.ap, in0=yv, scalar=dng[:, 0:1], in1=rr_.ap, op0=ALU.mult, op1=ALU.mult), reads=[ys, dng, rr_], writes=[sqs])
                    p.op("pool", lambda e: e.tensor_tensor(out=ytmp.ap, in0=sqs.ap.rearrange("p (h t) -> p h t", h=4), in1=ops["r"].ap, op=ALU.mult),
                         reads=[sqs, ops["r"]], writes=[ytmp])
                    yput(2, t0)

                scan_pass(sq, dirn, dk, dv, True, True, prep, S0, yf, None if dirn == 0 else yf, post, cst)
                if dirn == 0:
                    B.dump("d_yf_" + sq["name"], yf, [dv, 4, T])
                if sq["pi"] is not None:
                    p.dma("sp", st_delta[sq["pi"], l, dirn].rearrange("h k v -> k h v"), S0.ap, reads=[S0])

        def attention(sq):
            T = sq["T"]
            rr["n"] = 6
            rr["b"] = 0
            ctx = sq["pi"] is None
            L = T + (256 if ctx else 0)
            koff = 256 if ctx else 0
            nlt = L // 128
            ring["big"] = [B.al([128, 8, 128], BF16, "wbig%d" % k) for k in range(2)]
            qraw = B.al([64, 4, T], F32, "qraw")
            kraw = B.al([64, 2, T], F32, "kraw")
            qb = B.al([64, 4, T], BF16, "qb")
            kb = B.al([64, 2, L], BF16, "kb")
            Vb = B.al([128, nlt, 128], BF16, "Vb")
            scr = [B.al([64, 512], F32, "ascr%d" % k) for k in range(5)]
            Eb = [B.al([128, 512], BF16, "Eb%d" % k) for k in range(2)]
            rs = B.al([64, 512], F32, "rs")
            atmp = B.al([64, 512], BF16, "atmp")
            if ctx:
                cosT = B.al([64, TS], F32, "cosT")
                sinT = B.al([64, TS], F32, "sinT")
                p.dma("sp", cosT.ap, c_cos, writes=[cosT])
                p.dma("sp", sinT.ap, c_sin, writes=[sinT])
                kst = B.al([128, 2, 128], F32, "kst")
                p.dma("sp", kst.ap, cache_k[l].rearrange("(t p) f -> p t f", p=128), writes=[kst])
                p.dma("pool", Vb[:, 0:2, :], cache_v[l].rearrange("(t p) f -> p t f", p=128), writes=[Vb])
                for t in range(2):
                    for n in range(2):
                        bank = nb()
                        B.tr(bank, bank[0:64, 0:128], kst[:, t, n * 64:(n + 1) * 64], [kst])
                        p.op("act", lambda e, bank=bank, t=t, n=n: e.copy(out=kb[:, n, t * 128:(t + 1) * 128], in_=bank[0:64, 0:128]),
                             reads=[bank], writes=[kb])
            else:
                Vf = B.al([128, 2, 128], F32, "Vf")
                kTf = B.al([64, 2, T], F32, "kTf")
                kout = B.al([128, 2, 128], F32, "kout")
            KSUB = int(os.environ.get("KSUB", "9"))
            if KSUB <= 1:
                return
            for h in range(4):
                proj(D0 + 64 * h, 64, sq,
                     lambda bank, ap, s0, n, h=h: p.op("act", lambda e: e.copy(out=qraw[:, h, s0:s0 + n], in_=ap), reads=[bank], writes=[qraw]))
            if KSUB <= 2:
                return
            for n_ in range(2):
                proj(D0 + 256 + 64 * n_, 64, sq,
                     lambda bank, ap, s0, n, n_=n_: p.op("act", lambda e: e.copy(out=kraw[:, n_, s0:s0 + n], in_=ap), reads=[bank], writes=[kraw]))
            if KSUB <= 3:
                return
            w = wload(D0 + 384, 128)
            for s0 in range(0, T, 128):
                bank = nb()
                a = sq["hoff"] + s0
                for kt in range(8):
                    B.mm(bank, bank[:, 0:128], hT_[:, kt, a:a + 128], w[:, kt, :], kt == 0, kt == 7, [w, hT_])
                lt = (koff + s0) // 128
                if ctx:
                    p.op("act", lambda e, bank=bank, lt=lt: e.copy(out=Vb[:, lt, :], in_=bank[:, 0:128]), reads=[bank], writes=[Vb])
                else:
                    p.op("act", lambda e, bank=bank, lt=lt: e.copy(out=Vf[:, lt, :], in_=bank[:, 0:128]), reads=[bank], writes=[Vf])
                    p.op("dve", lambda e, lt=lt: e.tensor_copy(out=Vb[:, lt, :], in_=Vf[:, lt, :]), reads=[Vf], writes=[Vb])
            if not ctx and not os.environ.get("KNOVC"):
                p.dma("sp", vc_out[sq["pi"], l].rearrange("(t p) f -> p t f", p=128), Vf.ap, reads=[Vf])
            KS = int(os.environ.get("KSTOP", "9"))
            if KS <= 1:
                return

            def normrope(raw, nh, gcol, dst, doff, keepf=None):
                for h in range(nh):
                    for c0 in range(0, T, 512):
                        n = min(512, T - c0)
                        x = raw[:, h, c0:c0 + n]
                        s_sq, s_r, s_x, s_1, s_2 = scr[0], scr[1], scr[2], scr[3], scr[4]
                        p.op("act", lambda e, x=x, n=n: e.activation(out=s_sq[:, 0:n], in_=x, func=AF.Square), reads=[raw], writes=[s_sq])
                        bank = nb()
                        B.mm(bank, bank[0:64, 0:n], B.ones[0:64, 0:64], s_sq[:, 0:n], True, True, [B.ones, s_sq])
                        p.op("act", lambda e, bank=bank, n=n: e.activation(out=s_r[:, 0:n], in_=bank[0:64, 0:n], func=AF.Sqrt, bias=B.eps6[0:64, 0:1], scale=1.0 / 64),
                             reads=[bank, B.eps6], writes=[s_r])
                        p.op("dve", lambda e, n=n: e.reciprocal(out=s_r[:, 0:n], in_=s_r[:, 0:n]), reads=[s_r], writes=[s_r])
                        d = dst[:, h, doff + c0:doff + c0 + n]
                        if ctx:
                            p.op("dve", lambda e, x=x, n=n: e.scalar_tensor_tensor(out=s_x[:, 0:n], in0=x, scalar=gcol, in1=s_r[:, 0:n], op0=ALU.mult, op1=ALU.mult),
                                 reads=[raw, s_r, qkng2], writes=[s_x])
                            bank2 = nb()
                            B.mm(bank2, bank2[0:64, 0:n], rot_t.ap, s_x[:, 0:n], True, True, [rot_t, s_x])
                            p.op("dve", lambda e, n=n, c0=c0: e.tensor_tensor(out=s_1[:, 0:n], in0=s_x[:, 0:n], in1=cosT[:, c0:c0 + n], op=ALU.mult),
                                 reads=[s_x, cosT], writes=[s_1])
                            p.op("dve", lambda e, n=n, c0=c0, bank2=bank2: e.tensor_tensor(out=s_2[:, 0:n], in0=bank2[0:64, 0:n], in1=sinT[:, c0:c0 + n], op=ALU.mult),
                                 reads=[bank2, sinT], writes=[s_2])
                            p.op("pool", lambda e, n=n, d=d: e.tensor_tensor(out=d, in0=s_1[:, 0:n], in1=s_2[:, 0:n], op=ALU.add),
                                 reads=[s_1, s_2], writes=[dst])
                        else:
                            p.op("dve", lambda e, x=x, n=n, d=d: e.scalar_tensor_tensor(out=d, in0=x, scalar=gcol, in1=s_r[:, 0:n], op0=ALU.mult, op1=ALU.mult),
                                 reads=[raw, s_r, qkng2], writes=[dst])
                            if keepf is not None:
                                kf = keepf[:, h, c0:c0 + n]
                                p.op("dve", lambda e, x=x, n=n, kf=kf: e.scalar_tensor_tensor(out=kf, in0=x, scalar=gcol, in1=s_r[:, 0:n], op0=ALU.mult, op1=ALU.mult),
                                     reads=[raw, s_r, qkng2], writes=[keepf])

            normrope(qraw, 4, qkng2[:, 2 * l:2 * l + 1], qb, 0)
            normrope(kraw, 2, qkng2[:, 2 * l + 1:2 * l + 2], kb, koff, None if ctx else kTf)
            if KS <= 2:
                return
            if not ctx:
                for t in range(2):
                    bank = nb()
                    for n_ in range(2):
                        B.tr(bank, bank[:, n_ * 64:(n_ + 1) * 64], kTf[:, n_, t * 128:(t + 1) * 128], [kTf], inc=(n_ == 1))
                    p.op("act", lambda e, bank=bank, t=t: e.copy(out=kout[:, t, :], in_=bank[:, 0:128]), reads=[bank], writes=[kout])
                p.dma("sp", kc_out[sq["pi"], l].rearrange("(t p) f -> p t f", p=128), kout.ap, reads=[kout])
            if KS <= 3:
                return
            ei = 0
            for h in range(4):
                nk = h // 2
                for q0 in range(0, T, 512):
                    n = min(512, T - q0)
                    b1, b2 = PS[6], PS[7]
                    for lt in range(nlt):
                        bs = nb()
                        B.mm(bs, bs[:, 0:n], kb[:, nk, lt * 128:(lt + 1) * 128], qb[:, h, q0:q0 + n], True, True, [kb, qb])
                        E = Eb[ei % 2]
                        ei += 1
                        p.op("act", lambda e, E=E, bs=bs, n=n: e.activation(out=E[:, 0:n], in_=bs[:, 0:n], func=AF.Exp), reads=[bs], writes=[E])
                        B.mm(b1, b1[0:64, 0:n], Vb[:, lt, nk * 64:(nk + 1) * 64], E[:, 0:n], lt == 0, lt == nlt - 1, [Vb, E], inc=True)
                        B.mm(b2, b2[0:64, 0:n], ones_b.ap, E[:, 0:n], lt == 0, lt == nlt - 1, [ones_b, E], inc=True)
                    p.op("dve", lambda e, b2=b2, n=n: e.reciprocal(out=rs[:, 0:n], in_=b2[0:64, 0:n]), reads=[b2], writes=[rs])
                    if h % 2 == 0:
                        d = yT[0:64, 6 + h // 2, q0:q0 + n]
                        p.op("dve", lambda e, b1=b1, n=n, d=d: e.tensor_tensor(out=d, in0=b1[0:64, 0:n], in1=rs[:, 0:n], op=ALU.mult),
                             reads=[b1, rs], writes=[yT])
                    else:
                        p.op("dve", lambda e, b1=b1, n=n: e.tensor_tensor(out=atmp[:, 0:n], in0=b1[0:64, 0:n], in1=rs[:, 0:n], op=ALU.mult),
                             reads=[b1, rs], writes=[atmp])
                        p.dma("sp", yT[64:128, 6 + h // 2, q0:q0 + n], atmp[:, 0:n], reads=[atmp], writes=[yT])

        def merge(sq):
            T = sq["T"]
            j = sq["j"]
            ring["big"] = [B.al([128, 8, 128], BF16, "wbig%d" % k) for k in range(2)]
            mergedT = B.al([128, 8, T], BF16, "mergedT")
            acc = B.al([128, T], F32, "acc")
            tmpm = [B.al([128, 512], F32, "tmpm%d" % k) for k in range(2)]
            sgm = [B.al([128, 512], F32, "sgm%d" % k) for k in range(2)]
            wbr = [B.al([128, 2, 128], BF16, "wbr%d" % k) for k in range(2)]
            n_ = 0
            for ob in range(8):
                for m in range(4):
                    wg = wload(G0 + m * 1024 + ob * 128, 128)
                    wb = wbr[(ob * 4 + m) % 2]
                    p.dma("pool", wb.ap, w_branch[l, m].rearrange("(t p) c -> p t c", p=128)[:, :, ob * 128:(ob + 1) * 128], writes=[wb])
                    for s0 in range(0, T, 512):
                        n = min(512, T - s0)
                        bg, bu = nb(), nb()
                        for kt in range(8):
                            B.mm(bg, bg[:, 0:n], wg[:, kt, :], hT_[:, kt, sq["hoff"] + s0:sq["hoff"] + s0 + n], kt == 0, kt == 7, [wg, hT_])
                        for t_ in range(2):
                            B.mm(bu, bu[:, 0:n], wb[:, t_, :], yT[:, m * 2 + t_, s0:s0 + n], t_ == 0, t_ == 1, [wb, yT])
                        sg = sgm[n_ % 2]
                        n_ += 1
                        p.op("act", lambda e, sg=sg, bg=bg, n=n: e.activation(out=sg[:, 0:n], in_=bg[:, 0:n], func=AF.Sigmoid), reads=[bg], writes=[sg])
                        a = acc[:, s0:s0 + n]
                        if m == 0:
                            p.op("dve", lambda e, a=a, sg=sg, bu=bu, n=n: e.tensor_tensor(out=a, in0=bu[:, 0:n], in1=sg[:, 0:n], op=ALU.mult), reads=[bu, sg], writes=[acc])
                        else:
                            tm = tmpm[n_ % 2]
                            p.op("dve", lambda e, tm=tm, sg=sg, bu=bu, n=n: e.tensor_tensor(out=tm[:, 0:n], in0=bu[:, 0:n], in1=sg[:, 0:n], op=ALU.mult), reads=[bu, sg], writes=[tm])
                            if m < 3:
                                p.op("dve", lambda e, a=a, tm=tm, n=n: e.tensor_tensor(out=a, in0=a, in1=tm[:, 0:n], op=ALU.add), reads=[acc, tm], writes=[acc])
                            else:
                                d = mergedT[:, ob, s0:s0 + n]
                                p.op("dve", lambda e, a=a, tm=tm, n=n, d=d: e.tensor_tensor(out=d, in0=a, in1=tm[:, 0:n], op=ALU.add), reads=[acc, tm], writes=[mergedT, acc])
            woutv = w_out[l].rearrange("(kt p) c -> p kt c", p=128)
            for ob in range(8):
                w = wbig()
                p.dma("pool", w.ap, woutv[:, :, ob * 128:(ob + 1) * 128], writes=[w])
                for xi, (tb, c0) in enumerate(sq["xs"]):
                    s0 = xi * 512
                    n = min(512, T - s0)
                    bank = nb()
                    for kt in range(8):
                        B.mm(bank, bank[:, 0:n], w[:, kt, :], mergedT[:, kt, s0:s0 + n], kt == 0, kt == 7, [w, mergedT])
                    gt = gtT[l][:, 8 + ob, j:j + 1]
                    dst = xT[tb][:, ob, c0:c0 + n]
                    p.op("dve", lambda e, dst=dst, bank=bank, gt=gt, n=n: e.scalar_tensor_tensor(out=dst, in0=bank[:, 0:n], scalar=gt, in1=dst, op0=ALU.mult, op1=ALU.add),
                         reads=[bank, gtT[l], xT[tb]], writes=[xT[tb]])

        hdone = {"p": False, "s": False}
        for si, sq in enumerate(SEQS):
            if si >= int(os.environ.get("KSEQ", "9")):
                break
            grp = "s" if sq["pi"] is None else "p"
            p.barrier()
            B.arena_off = mark
            if not hdone[grp]:
                hdone[grp] = True
                norm_setup()
                if grp == "p":
                    adaln(l, 1, 2, hT_, PS[6], hT_[:, :, 0:512])
                else:
                    adaln(l, 1, 0, hT_, PS[6], hT_[:, :, 0:512])
                    adaln(l, 1, 1, hT_, PS[6], hT_[:, :, 512:1024])
                p.barrier()
                B.arena_off = mark
            for m in MIXERS_OFF:
                d = yT[:, m * 2:(m + 1) * 2, 0:sq["T"]]
                p.op("pool", lambda e, d=d: e.memset(d, 0.0), writes=[yT])
            if 0 not in MIXERS_OFF:
                rwkv(sq)
                B.dump("yTa_" + sq["name"], yT, [128, 8, TS])
                p.barrier()
                B.arena_off = mark
            if 1 not in MIXERS_OFF:
                gla(sq)
                B.dump("yT_" + sq["name"], yT, [128, 8, TS])
                p.barrier()
                B.arena_off = mark
            if 2 not in MIXERS_OFF:
                delta(sq)
                B.dump("yTc_" + sq["name"], yT, [128, 8, TS])
                p.barrier()
                B.arena_off = mark
            if 3 not in MIXERS_OFF:
                attention(sq)
                rr["n"] = 8
                p.barrier()
                B.arena_off = mark
            if KMIX > 1:
                merge(sq)

    def final_out():
        B.arena_reset()
        norm_setup()
        t_scr = S["t_scr"]
        for tb in range(3):
            rms_stats(tb, PS[6])
            for ft in range(8):
                t = t_scr[ft % 2]
                src = xT[tb][:, ft, :]
                p.op("dve", lambda e, t=t, src=src, rs_=rstd[tb]: e.tensor_tensor(out=t.ap, in0=src, in1=rs_.ap, op=ALU.mult),
                     reads=[xT[tb], rstd[tb]], writes=[t])
                g = fngT[:, ft:ft + 1]
                dst = xT[tb][:, ft, :]
                p.op("act", lambda e, dst=dst, t=t, g=g: e.activation(out=dst, in_=t.ap, func=AF.Copy, scale=g),
                     reads=[t, fngT], writes=[xT[tb]])
        store_x()

    def store_x():
        stage_tm = [B.al([128, D], F32, "stage_tm%d" % i) for i in range(2)]
        for tt in range(12):
            st = stage_tm[tt % 2]
            tb, tq = divmod(tt, 4)
            for half in range(2):
                bank = PS[(tt * 2 + half) % 8]
                for jj in range(4):
                    ft = half * 4 + jj
                    B.tr(bank, bank[:, jj * 128:(jj + 1) * 128], xT[tb][:, ft, tq * 128:(tq + 1) * 128], [xT[tb]], inc=(jj == 3))
                dst = st[:, half * 512:(half + 1) * 512]
                if half == 0:
                    p.op("dve", lambda e, dst=dst, bank=bank: e.tensor_copy(out=dst, in_=bank.ap), reads=[bank], writes=[st])
                else:
                    p.op("act", lambda e, dst=dst, bank=bank: e.copy(out=dst, in_=bank.ap), reads=[bank], writes=[st])
            p.dma("sp", y_tm[tt * 128:(tt + 1) * 128, :], st.ap, reads=[st])

    if stop_after == "in":
        B.arena_reset()
        store_x()
    else:
        for l in range(DEPTH):
            ffn(l, 0, 0)
            if stop_after == "ffn1" or (stop_after == "l1ffn1" and l == 1):
                break
            mixing(l)
            if stop_after == "mix1" or (stop_after == "l1mix" and l == 1):
                break
            ffn(l, 1, 2)
            if stop_after == "l0":
                break
        if stop_after in ("ffn1", "mix1", "l0", "l1ffn1", "l1mix"):
            B.arena_reset()
            store_x()
        else:
            final_out()

    p.finish()
    p.emit()
    return B


CONSTS = None


def make_consts():
    c = {}
    c["c_ident"] = np.eye(128, dtype=np.float32)
    c["c_ones"] = np.ones((128, 128), dtype=np.float32)
    rot = np.zeros((64, 64), np.float32)
    for base in (0, 32):
        for d in range(16):
            rot[base + d + 16, base + d] = -1.0
            rot[base + d, base + d + 16] = 1.0
    c["c_rot"] = rot
    t = np.arange(TS)
    inv = (1.0 / (np.float32(10000.0) ** (np.arange(0, 32, 2, dtype=np.float32) / np.float32(32)))).astype(np.float32)
    ang = np.zeros((64, TS), np.float32)
    for d in range(64):
        pos = (t // 64) if d < 32 else (t % 64)
        ang[d] = pos.astype(np.float32) * inv[d % 16]
    r_, c_ = np.meshgrid(np.arange(128), np.arange(128), indexing="ij")
    same = (r_ // 64) == (c_ // 64)
    IU = (same & (r_ <= c_)).astype(np.float32)
    SU = (same & (r_ < c_)).astype(np.float32)
    SL = (same & (r_ > c_)).astype(np.float32)
    IL = (same & (r_ >= c_)).astype(np.float32)
    neg = lambda m_: ((1.0 - m_) * -30000.0).astype(np.float32)
    c["c_tri"] = np.ascontiguousarray(np.stack([np.stack([IU, SU, SL], 1), np.stack([IL, SL, SU], 1)], 0))
    c["c_mult"] = np.ascontiguousarray(np.stack([np.stack([IU, IU, SU, SU], 1), np.stack([IL, IL, SL, SL], 1)], 0))
    c["c_ms"] = np.ascontiguousarray(np.stack([SL, SU], 0))
    c["c_nm"] = np.ascontiguousarray(np.stack([np.stack([neg(IU), neg(SU), neg(SL)], 1), np.stack([neg(IL), neg(SL), neg(SU)], 1)], 0))
    sel = np.zeros((16, 16, 64), np.float32)
    for r__ in range(16):
        sel[r__, r__, :] = 1.0
    c["c_sel"] = sel
    c["c_cos"] = np.cos(ang).astype(np.float32)
    c["c_sin"] = np.sin(ang).astype(np.float32)
    return c


def make_in_maps(inputs, names, bf16_inputs=()):
    consts = make_consts()
    maps = []
    f = lambda a: np.ascontiguousarray(np.asarray(a, dtype=np.float32))
    shared = {k: f(inputs[k]) for k in ("norm_g", "w_mod", "b_mod", "ffn_w_in", "ffn_w_out", "final_norm_g",
                                        "w_in", "w_branch", "w_out", "attn_q_norm", "attn_k_norm",
                                        "gla_alpha_up", "gla_alpha_b", "gla_norm_g",
                                        "delta_conv", "delta_a_log", "delta_dt_bias", "delta_norm_g",
                                        "rwkv_mu", "rwkv_w0", "rwkv_w_up", "rwkv_a0", "rwkv_a_up", "rwkv_g_up", "rwkv_k_k", "rwkv_k_a",
                                        "rwkv_r_k", "rwkv_ln_g", "rwkv_ln_b") if k in names}
    xs, xp = f(inputs["x_sample"]), f(inputs["x_prompt"])
    cc, cctx = f(inputs["c"]), f(inputs["c_ctx"])
    ck, cv = f(inputs["cache_attn_k"]), f(inputs["cache_attn_v"])
    for b in range(8):
        m = dict(shared)
        m.update(consts)
        m["x_tm"] = np.ascontiguousarray(np.concatenate([xs[b], xp[2 * b], xp[2 * b + 1]], axis=0))
        m["c2"] = np.ascontiguousarray(np.stack([cc[b], cctx], axis=0))
        m["cache_k"] = np.ascontiguousarray(ck[b].reshape(DEPTH, 256, 128))
        m["cache_v"] = np.ascontiguousarray(cv[b].reshape(DEPTH, 256, 128))
        m["state_gla"] = f(inputs["state_gla"][b])
        m["state_delta"] = f(inputs["state_delta"][b])
        m["state_rwkv"] = f(inputs["state_rwkv"][b])
        if bf16_inputs:
            import ml_dtypes
            for k in bf16_inputs:
                if k in m:
                    key = ("bf", k) if k in shared else None
                    if key and key in _bfcache:
                        m[k] = _bfcache[key]
                    else:
                        m[k] = m[k].astype(ml_dtypes.bfloat16)
                        if key:
                            _bfcache[key] = m[k]
        maps.append({k: v for k, v in m.items() if k in names})
    return maps


_bfcache = {}


def kernel(**inputs):
    B = build_program()
    maps = make_in_maps(inputs, set(B.inp.keys()), B.bf16_inputs)
    res = run_bass_kernel_spmd(B.nc, maps, core_ids=list(range(8)))
    r = res.results
    f32 = lambda a: np.ascontiguousarray(np.asarray(a, dtype=np.float32))
    y_sample = np.stack([f32(r[b]["y_tm"])[:TS] for b in range(8)], axis=0)
    y_prompt = np.stack([f32(r[b // 2]["y_tm"])[TS + (b % 2) * TP:TS + (b % 2 + 1) * TP] for b in range(16)], axis=0)
    kc = np.stack([f32(r[b // 2]["kc_out"])[b % 2].reshape(DEPTH, 256, 2, 64) for b in range(16)], axis=0)
    vc = np.stack([f32(r[b // 2]["vc_out"])[b % 2].reshape(DEPTH, 256, 2, 64) for b in range(16)], axis=0)
    sr = np.stack([f32(r[b // 2]["st_rwkv"])[b % 2] for b in range(16)], axis=0)
    sg = np.stack([f32(r[b // 2]["st_gla"])[b % 2] for b in range(16)], axis=0)
    sd = np.stack([f32(r[b // 2]["st_delta"])[b % 2] for b in range(16)], axis=0)
    return (y_prompt, y_sample, kc, vc, sr, sg, sd)
```

```python
import os
import numpy as np
import concourse.bass as bass
import concourse.mybir as mybir
from concourse.bass_utils import run_bass_kernel_spmd

F32 = mybir.dt.float32
F32R = mybir.dt.float32r
BF16 = mybir.dt.bfloat16
AF = mybir.ActivationFunctionType
ALU = mybir.AluOpType
AX = mybir.AxisListType

ENGS = ("pe", "act", "dve", "pool", "sp")

D = 1024
NT = 1536
TS = 1024
TP = 256
DFF = 2816
NIN = 7632
DEPTH = 2
import os
SIMMODE = bool(os.environ.get('SIMMODE'))
EMBED_WAIT = os.environ.get('KEMBED', '1') == '1'
POOL2DVE = os.environ.get('KP2D', '1') == '1'
MIXERS_OFF = tuple(int(c) for c in os.environ.get('KOFF', ''))
A0, B0, C0, D0, G0 = 0, 1184, 1984, 3024, 3536


class Tile:
    __slots__ = ("ap", "w", "r", "name", "excl")

    def __init__(self, ap, name="", excl=False):
        self.ap = ap
        self.w = {}
        self.r = {}
        self.name = name
        self.excl = excl

    def __getitem__(self, k):
        return self.ap[k]


class Prog:
    def __init__(self, nc, n_dma_sems=96, n_q_sems=0):
        self.nc = nc
        self.ops = {e: [] for e in ENGS}
        self.cnt = {e: 0 for e in ENGS}
        self.sems = {e: nc.alloc_semaphore("s_" + e) for e in ENGS}
        self.dsems = [nc.alloc_semaphore("d%d" % i) for i in range(n_dma_sems)]
        self.dcnt = [0] * n_dma_sems
        self.dnext = 0
        self.wnext = 0
        self.qsems = [nc.alloc_semaphore("q%d" % i) for i in range(n_q_sems)]
        self.qgen = [0] * n_q_sems
        self.qnext = 0
        self.known = {e: {} for e in ENGS}
        self.snap = {e: {} for e in ENGS}
        self.ninst = 0
        self.nwaits = 0
        self._nm = 0
        self.debug = bool(os.environ.get("KDEBUG"))
        self.names = {}

    def _name(self, name):
        self._nm += 1
        return "%s_%d" % (name or "t", self._nm)

    def sb(self, shape, dtype=F32, name=None):
        t = self.nc.alloc_sbuf_tensor(self._name(name), list(shape), dtype)
        return Tile(t[:], name or "sb")

    def ps(self, shape, dtype=F32, name=None):
        t = self.nc.alloc_psum_tensor(self._name(name), list(shape), dtype)
        return Tile(t[:], name or "ps", excl=True)

    def _collect(self, eng, reads, writes):
        deps = {}
        for t in reads:
            for k, v in t.w.items():
                if deps.get(k, 0) < v:
                    deps[k] = v
            if t.excl:
                for k, v in t.r.items():
                    if k != eng and deps.get(k, 0) < v:
                        deps[k] = v
        for t in writes:
            for d in (t.w, t.r):
                for k, v in d.items():
                    if deps.get(k, 0) < v:
                        deps[k] = v
        kn = self.known[eng]
        waits = []
        for k, v in sorted(deps.items(), key=lambda kv: -kv[1] if isinstance(kv[0], str) else 0):
            if eng == "pe" and k == "pe":
                continue
            if kn.get(k, 0) < v:
                kn[k] = v
                waits.append((k, v))
                if isinstance(k, str) and k != eng:
                    snap = self.snap[k].get(v)
                    if snap:
                        for k2, v2 in snap.items():
                            if kn.get(k2, 0) < v2:
                                kn[k2] = v2
        return waits

    def _mark(self, tok, reads, writes):
        k, v = tok
        for t in writes:
            t.w = {k: v}
            t.r = {}
        for t in reads:
            if t.r.get(k, 0) < v:
                t.r[k] = v

    def op(self, eng, fn, reads=(), writes=(), inc=True):
        if eng == "pool" and POOL2DVE:
            eng = "dve"
        if self.debug:
            import sys as _s
            f = _s._getframe(1)
            if f.f_code.co_name in ("mm", "tr"):
                f = f.f_back
            fn._tag = "%s:%d r=%s w=%s" % (f.f_code.co_name, f.f_lineno, [t.name for t in reads], [t.name for t in writes])
        waits = self._collect(eng, reads, writes)
        if inc:
            self.cnt[eng] += 1
            tok = (eng, self.cnt[eng])
            self.ops[eng].append((waits, fn, (eng, 1)))
            self.snap[eng][self.cnt[eng]] = {k: v for k, v in self.known[eng].items() if isinstance(k, str)}
        else:
            assert eng == "pe"
            tok = (eng, self.cnt[eng] + 1)
            self.ops[eng].append((waits, fn, None))
        self._mark(tok, reads, writes)
        return tok

    def dma(self, eng, out, in_, reads=(), writes=(), **kw):
        if eng == "pool" and SIMMODE:
            eng = "sp"
        waits = self._collect(eng, reads, writes)
        if kw.pop("wpool", False):
            i = self.wnext
            self.wnext = (self.wnext + 1) % 32
        else:
            i = 32 + self.dnext
            self.dnext = (self.dnext + 1) % (len(self.dsems) - 32)
        self.dcnt[i] += 16
        tok = (("d", i), self.dcnt[i])
        self.ops[eng].append((waits, lambda e: e.dma_start(out=out, in_=in_, **kw), (("d", i), 16)))
        self._mark(tok, reads, writes)
        return tok

    def _sem(self, k):
        if isinstance(k, str):
            return self.sems[k]
        return self.qsems[k[1]] if k[0] == "q" else self.dsems[k[1]]

    def _dma_targets(self):
        t = [(("d", i), c) for i, c in enumerate(self.dcnt) if c]
        t += [(("q", i, g), 16) for i, g in enumerate(self.qgen) if g]
        return t

    def barrier(self):
        targets = self._dma_targets()
        targets += [(e, self.cnt[e]) for e in ENGS if self.cnt[e]]
        for e in ENGS:
            kn = self.known[e]
            waits = []
            for k, v in targets:
                if k == e:
                    continue
                if kn.get(k, 0) < v:
                    kn[k] = v
                    waits.append((k, v))
            if waits:
                self.ops[e].append((waits, None, None))

    def finish(self):
        waits = self._dma_targets()
        for e in ENGS:
            if e != "sp" and self.cnt[e]:
                waits.append((e, self.cnt[e]))
        self.ops["sp"].append((waits, None, None))

    def check(self):
        val = {}
        pos = {e: 0 for e in ENGS}
        progress = True
        while progress:
            progress = False
            for e in ENGS:
                q = self.ops[e]
                while pos[e] < len(q):
                    waits, fn, inc = q[pos[e]]
                    if any(val.get(k, 0) < v for k, v in waits):
                        break
                    if inc is not None:
                        val[inc[0]] = val.get(inc[0], 0) + inc[1]
                    pos[e] += 1
                    progress = True
        bad = {e: (pos[e], len(self.ops[e])) for e in ENGS if pos[e] < len(self.ops[e])}
        for e, (i, n) in bad.items():
            waits = self.ops[e][i][0]
            print("DEADLOCK", e, "op", i, "of", n, [(k, v, val.get(k, 0)) for k, v in waits if val.get(k, 0) < v],
                  self.ops[e][i][3] if len(self.ops[e][i]) > 3 else "")
        return not bad

    def emit(self):
        nc = self.nc
        prog = self
        assert self.check(), "deadlock in schedule"

        def replay(name, e):
            for waits, fn, inc in prog.ops[name]:
                emb = None
                if fn is not None and waits and EMBED_WAIT:
                    emb = waits[-1]
                    waits = waits[:-1]
                for k, v in waits:
                    e.wait_ge(prog._sem(k), v)
                    prog.nwaits += 1
                if fn is not None:
                    ins = fn(e)
                    if emb is not None:
                        ins._wait_ge(prog._sem(emb[0]), emb[1])
                    if prog.debug:
                        try:
                            prog.names[ins.ins.name] = getattr(fn, "_tag", "")
                        except Exception:
                            pass
                    if inc is not None:
                        ins.then_inc(prog._sem(inc[0]), inc[1])
                    prog.ninst += 1

        with nc.Block() as block:
            @block.tensor
            def _(e):
                replay("pe", e)

            @block.scalar
            def _(e):
                replay("act", e)

            @block.vector
            def _(e):
                replay("dve", e)

            @block.gpsimd
            def _(e):
                replay("pool", e)

            @block.sync
            def _(e):
                replay("sp", e)


class Builder:
    def __init__(self, stop_after=None, dumps=()):
        self.stop_after = stop_after
        self.want_dumps = set(dumps)
        self.dump_list = []
        nc = bass.Bass("TRN2", target_bir_lowering=False)
        self.nc = nc
        self.p = Prog(nc)
        self.inp = {}
        self.out = {}
        self.bf16_inputs = set()
        self._evac_rr = 0

    def arena_init(self, cols):
        self.arena = self.nc.alloc_sbuf_tensor("arena", [128, cols], F32)[:]
        self.arena_cols = cols
        self.arena_off = 0

    def arena_reset(self):
        self.p.barrier()
        self.arena_off = 0

    def al(self, shape, dtype=F32, name="a"):
        n = int(np.prod(shape[1:]))
        if dtype == BF16:
            assert n % 2 == 0
            n //= 2
        assert self.arena_off + n <= self.arena_cols, (name, self.arena_off, n, self.arena_cols)
        ap = self.arena[0:shape[0], self.arena_off:self.arena_off + n]
        self.arena_off += n
        if dtype != F32:
            ap = ap.bitcast(dtype)
        if len(shape) == 3:
            ap = ap.rearrange("p (a b) -> p a b", a=shape[1])
        return Tile(ap, name)

    def din(self, name, shape, cast=False):
        dt_ = BF16 if (cast and SIMMODE) else F32
        if cast and SIMMODE:
            self.bf16_inputs.add(name)
        ap = self.nc.dram_tensor(name, list(shape), dt_, kind="ExternalInput").ap()
        self.inp[name] = ap
        return ap

    def dout(self, name, shape):
        ap = self.nc.dram_tensor(name, list(shape), F32, kind="ExternalOutput").ap()
        self.out[name] = ap
        return ap

    def dump(self, name, tile, shape):
        if name not in self.want_dumps:
            return
        if SIMMODE and tile.ap.dtype == BF16:
            o = self.nc.dram_tensor("dbg_" + name, list(shape), BF16, kind="ExternalOutput").ap()
            self.out["dbg_" + name] = o
        else:
            o = self.dout("dbg_" + name, shape)
        self.p.dma("pool", o, tile.ap, reads=[tile])

    def mm(self, out_t, out_ap, lhsT, rhs, start, stop, reads, inc=None):
        self.p.op("pe", lambda e: e.matmul(out_ap, lhsT, rhs, start=start, stop=stop),
                  reads=reads, writes=[out_t], inc=(stop if inc is None else inc))

    def tr(self, out_t, out_ap, in_ap, reads, inc=True):
        ident = self.ident
        kp = in_ap.shape[0]
        self.p.op("pe", lambda e: e.transpose(out_ap, in_ap, ident[0:kp, 0:kp]), reads=list(reads) + [ident],
                  writes=[out_t], inc=inc)


def build_program(stop_after=None, dumps=()):
    B = Builder(stop_after, dumps)
    nc, p = B.nc, B.p

    x_tm = B.din("x_tm", [NT, D])
    c2 = B.din("c2", [2, D])
    norm_g = B.din("norm_g", [DEPTH, 3, D])
    w_mod = B.din("w_mod", [DEPTH, D, 9 * D])
    b_mod = B.din("b_mod", [DEPTH, 9 * D])
    ffn_w_in = B.din("ffn_w_in", [DEPTH, 2, D, 2 * DFF], cast=True)
    ffn_w_out = B.din("ffn_w_out", [DEPTH, 2, DFF, D], cast=True)
    final_norm_g = B.din("final_norm_g", [D])
    c_ident = B.din("c_ident", [128, 128])
    c_ones = B.din("c_ones", [128, 128])
    y_tm = B.dout("y_tm", [NT, D])
    w_in = B.din("w_in", [DEPTH, D, NIN], cast=True)
    w_branch = B.din("w_branch", [DEPTH, 4, 256, D], cast=True)
    w_out = B.din("w_out", [DEPTH, D, D], cast=True)
    attn_q_norm = B.din("attn_q_norm", [DEPTH, 64])
    attn_k_norm = B.din("attn_k_norm", [DEPTH, 64])
    cache_k = B.din("cache_k", [DEPTH, 256, 128])
    cache_v = B.din("cache_v", [DEPTH, 256, 128], cast=True)
    c_rot = B.din("c_rot", [64, 64])
    c_cos = B.din("c_cos", [64, TS])
    c_sin = B.din("c_sin", [64, TS])
    gla_alpha_up = B.din("gla_alpha_up", [DEPTH, 2, 16, 128])
    gla_alpha_b = B.din("gla_alpha_b", [DEPTH, 2, 128])
    gla_norm_g = B.din("gla_norm_g", [DEPTH, 64])
    state_gla = B.din("state_gla", [DEPTH, 2, 4, 32, 64])
    c_tri = B.din("c_tri", [2, 128, 3, 128])
    c_mult = B.din("c_mult", [2, 128, 4, 128])
    c_ms = B.din("c_ms", [2, 128, 128])
    c_nm = B.din("c_nm", [2, 128, 3, 128])
    st_gla = B.dout("st_gla", [2, DEPTH, 2, 4, 32, 64])
    delta_conv = B.din("delta_conv", [DEPTH, 3, 768])
    delta_a_log = B.din("delta_a_log", [DEPTH, 2, 4])
    delta_dt_bias = B.din("delta_dt_bias", [DEPTH, 2, 4])
    delta_norm_g = B.din("delta_norm_g", [DEPTH, 64])
    state_delta = B.din("state_delta", [DEPTH, 2, 4, 64, 64])
    c_sel = B.din("c_sel", [16, 16, 64])
    st_delta = B.dout("st_delta", [2, DEPTH, 2, 4, 64, 64])
    rwkv_mu = B.din("rwkv_mu", [DEPTH, 1184])
    rwkv_w0 = B.din("rwkv_w0", [DEPTH, 2, 256])
    rwkv_w_up = B.din("rwkv_w_up", [DEPTH, 2, 64, 256])
    rwkv_a0 = B.din("rwkv_a0", [DEPTH, 2, 256])
    rwkv_a_up = B.din("rwkv_a_up", [DEPTH, 2, 64, 256])
    rwkv_g_up = B.din("rwkv_g_up", [DEPTH, 160, 256])
    rwkv_k_k = B.din("rwkv_k_k", [DEPTH, 256])
    rwkv_k_a = B.din("rwkv_k_a", [DEPTH, 256])
    rwkv_r_k = B.din("rwkv_r_k", [DEPTH, 4, 64])
    rwkv_ln_g = B.din("rwkv_ln_g", [DEPTH, 256])
    rwkv_ln_b = B.din("rwkv_ln_b", [DEPTH, 256])
    state_rwkv = B.din("state_rwkv", [DEPTH, 2, 4, 64, 64])
    st_rwkv = B.dout("st_rwkv", [2, DEPTH, 2, 4, 64, 64])
    kc_out = B.dout("kc_out", [2, DEPTH, 256, 128])
    vc_out = B.dout("vc_out", [2, DEPTH, 256, 128])

    B.ident = p.sb([128, 128], F32, "ident")
    B.ones = p.sb([128, 128], F32, "ones")
    B.eps6 = p.sb([128, 1], F32, "eps6")
    p.op("dve", lambda e: e.memset(B.eps6.ap, 1e-6), writes=[B.eps6])
    B.one1 = p.sb([128, 1], F32, "one1")
    p.op("dve", lambda e: e.memset(B.one1.ap, 1.0), writes=[B.one1])
    p.dma("sp", B.ident.ap, c_ident, writes=[B.ident])
    p.dma("sp", B.ones.ap, c_ones, writes=[B.ones])

    w_in_bf = nc.dram_tensor("w_in_bf", [DEPTH, D, NIN], BF16).ap()
    wbfT = Tile(w_in_bf, "w_in_bf")
    for l in range(DEPTH):
        src_v = w_in[l].rearrange("(kt p) c -> p kt c", p=128)
        dst_v = w_in_bf[l].rearrange("(kt p) c -> p kt c", p=128)
        for c0_ in range(0, NIN, 954):
            p.dma("pool", dst_v[:, :, c0_:c0_ + 954], src_v[:, :, c0_:c0_ + 954], writes=[wbfT])

    PS = [p.ps([128, 512], F32, "bank%d" % i) for i in range(8)]
    rr = {"b": 0, "n": 8}

    def nb():
        rr["b"] = (rr["b"] + 1) % rr["n"]
        return PS[rr["b"]]

    ones_b = p.sb([128, 64], BF16, "ones_b")
    p.op("dve", lambda e: e.memset(ones_b.ap, 1.0), writes=[ones_b])
    rot_t = p.sb([64, 64], F32, "rot_t")
    p.dma("sp", rot_t.ap, c_rot, writes=[rot_t])
    qkng = p.sb([64, 2 * DEPTH], F32, "qkng")
    for l in range(DEPTH):
        p.dma("sp", qkng[:, 2 * l:2 * l + 1], attn_q_norm[l:l + 1, :].rearrange("o d -> d o"), writes=[qkng], allow_slow_non_contiguous=True)
        p.dma("sp", qkng[:, 2 * l + 1:2 * l + 2], attn_k_norm[l:l + 1, :].rearrange("o d -> d o"), writes=[qkng], allow_slow_non_contiguous=True)
    qkng2 = p.sb([64, 2 * DEPTH], F32, "qkng2")
    p.op("dve", lambda e: e.tensor_copy(out=qkng2.ap, in_=qkng.ap), reads=[qkng], writes=[qkng2])
    for l in range(DEPTH):
        p.op("dve", lambda e, l=l: e.tensor_scalar(out=qkng2[:, 2 * l:2 * l + 1], in0=qkng[:, 2 * l:2 * l + 1], scalar1=0.125, scalar2=None, op0=ALU.mult),
             reads=[qkng], writes=[qkng2])

    xT = [p.sb([128, 8, 512], F32, "xT%d" % tb) for tb in range(3)]
    rstd = [p.sb([128, 512], F32, "rstd%d" % tb) for tb in range(3)]
    B.arena_init(37000)
    stage_tm = [B.al([128, D], F32, "stage_tm%d" % i) for i in range(2)]

    for tt in range(12):
        st = stage_tm[tt % 2]
        p.dma("sp", st.ap, x_tm[tt * 128:(tt + 1) * 128, :], writes=[st])
        tb, tq = divmod(tt, 4)
        for half in range(2):
            bank = PS[(tt * 2 + half) % 8]
            for j in range(4):
                ft = half * 4 + j
                B.tr(bank, bank[:, j * 128:(j + 1) * 128], st[:, ft * 128:(ft + 1) * 128], [st], inc=(j == 3))
            dst = xT[tb][:, half * 4:half * 4 + 4, tq * 128:(tq + 1) * 128]
            src = bank.ap.rearrange("p (j t) -> p j t", j=4)
            if half == 0:
                p.op("dve", lambda e, dst=dst, src=src: e.tensor_copy(out=dst, in_=src), reads=[bank], writes=[xT[tb]])
            else:
                p.op("act", lambda e, dst=dst, src=src: e.copy(out=dst, in_=src), reads=[bank], writes=[xT[tb]])

    cT = p.sb([128, 8, 2], F32, "cT")
    for j in range(2):
        p.dma("sp", cT[:, :, j], c2[j, :].rearrange("(kt p) -> p kt", p=128), writes=[cT], allow_slow_non_contiguous=True)
    scT = p.sb([128, 8, 2], F32, "scT")
    p.op("act", lambda e: e.activation(out=scT.ap, in_=cT.ap, func=AF.Silu), reads=[cT], writes=[scT])
    modT = [p.sb([128, 72, 2], F32, "modT%d" % l) for l in range(DEPTH)]
    bmT = [p.sb([128, 72], F32, "bmT%d" % l) for l in range(DEPTH)]
    ngT = [p.sb([128, 24], F32, "ngT%d" % l) for l in range(DEPTH)]
    fngT = p.sb([128, 8], F32, "fngT")
    p.dma("sp", fngT.ap, final_norm_g.rearrange("(kt p) -> p kt", p=128), writes=[fngT], allow_slow_non_contiguous=True)
    B.arena_reset()
    wmod_st = [B.al([128, 8, 512], F32, "wmod_st%d" % i) for i in range(2)]
    for l in range(DEPTH):
        p.dma("sp", bmT[l].ap, b_mod[l, :].rearrange("(o p) -> p o", p=128), writes=[bmT[l]], allow_slow_non_contiguous=True)
        p.dma("sp", ngT[l].ap, norm_g[l].rearrange("m (kt p) -> p (m kt)", p=128), writes=[ngT[l]], allow_slow_non_contiguous=True)
        wv = w_mod[l].rearrange("(kt p) c -> p kt c", p=128)
        for ch in range(18):
            st = wmod_st[ch % 2]
            p.dma("sp" if ch % 2 == 0 else "act", st.ap, wv[:, :, ch * 512:(ch + 1) * 512], writes=[st])
            bank = PS[ch % 8]
            for ob in range(4):
                for kt in range(8):
                    B.mm(bank, bank[:, ob * 2:ob * 2 + 2], st[:, kt, ob * 128:(ob + 1) * 128], scT[:, kt, :],
                         kt == 0, kt == 7, [st, scT])
            o0 = ch * 4
            dst = modT[l][:, o0:o0 + 4, :]
            src = bank[:, 0:8].rearrange("p (o j) -> p o j", j=2)
            bias = bmT[l][:, o0:o0 + 4].unsqueeze(2).to_broadcast([128, 4, 2])
            p.op("dve", lambda e, dst=dst, src=src, bias=bias: e.tensor_tensor(out=dst, in0=src, in1=bias, op=ALU.add),
                 reads=[bank, bmT[l]], writes=[modT[l]])
    gmT = [p.sb([128, 24, 2], F32, "gmT%d" % l) for l in range(DEPTH)]
    gtT = [p.sb([128, 24, 2], F32, "gtT%d" % l) for l in range(DEPTH)]
    for l in range(DEPTH):
        for i in range(3):
            sc = modT[l][:, (3 * i + 1) * 8:(3 * i + 2) * 8, :]
            g = ngT[l][:, i * 8:(i + 1) * 8].unsqueeze(2).to_broadcast([128, 8, 2])
            dst = gmT[l][:, i * 8:(i + 1) * 8, :]
            p.op("dve", lambda e, dst=dst, sc=sc, g=g: e.scalar_tensor_tensor(out=dst, in0=sc, scalar=1.0, in1=g, op0=ALU.add, op1=ALU.mult),
                 reads=[modT[l], ngT[l]], writes=[gmT[l]])
            gt = modT[l][:, (3 * i + 2) * 8:(3 * i + 3) * 8, :]
            dst2 = gtT[l][:, i * 8:(i + 1) * 8, :]
            fac = 1.0 if i == 1 else 0.5
            p.op("dve", lambda e, dst2=dst2, gt=gt, fac=fac: e.tensor_scalar(out=dst2, in0=gt, scalar1=fac, scalar2=None, op0=ALU.mult),
                 reads=[modT[l]], writes=[gtT[l]])
    B.dump("modT0", modT[0], [128, 72, 2])
    B.dump("gmT0", gmT[0], [128, 24, 2])

    S = {}

    def norm_setup():
        S["sq_scr"] = [B.al([128, 512], F32, "sq_scr%d" % i) for i in range(2)]
        S["t_scr"] = [B.al([128, 512], F32, "t_scr%d" % i) for i in range(2)]

    def rms_stats(tb, bank):
        sq_scr = S["sq_scr"]
        for ft in range(8):
            s = sq_scr[ft % 2]
            src = xT[tb][:, ft, :]
            p.op("act", lambda e, s=s, src=src: e.activation(out=s.ap, in_=src, func=AF.Square), reads=[xT[tb]], writes=[s])
            B.mm(bank, bank.ap, B.ones.ap, s.ap, ft == 0, ft == 7, [B.ones, s], inc=True)
        r = rstd[tb]
        p.op("act", lambda e: e.activation(out=r.ap, in_=bank.ap, func=AF.Sqrt, bias=B.eps6[:, 0:1], scale=1.0 / D),
             reads=[bank, B.eps6], writes=[r])
        p.op("dve", lambda e: e.reciprocal(out=r.ap, in_=r.ap), reads=[r], writes=[r])

    def adaln(l, i, tb, h, bank, hap=None):
        hap = h.ap if hap is None else hap
        j = 0 if tb < 2 else 1
        rms_stats(tb, bank)
        t_scr = S["t_scr"]
        for ft in range(8):
            t = t_scr[ft % 2]
            src = xT[tb][:, ft, :]
            p.op("dve", lambda e, t=t, src=src: e.tensor_tensor(out=t.ap, in0=src, in1=rstd[tb].ap, op=ALU.mult),
                 reads=[xT[tb], rstd[tb]], writes=[t])
            gm = gmT[l][:, i * 8 + ft, j:j + 1]
            sh = modT[l][:, (3 * i) * 8 + ft, j:j + 1]
            dst = hap[:, ft, :]
            p.op("act", lambda e, dst=dst, t=t, gm=gm, sh=sh: e.activation(out=dst, in_=t.ap, func=AF.Identity, bias=sh, scale=gm),
                 reads=[t, gmT[l], modT[l]], writes=[h])

    cnt = {"w": 0, "o": 0}

    def ffn(l, f, i):
        B.arena_reset()
        norm_setup()
        hT = [B.al([128, 8, 512], BF16, "hT%d" % k) for k in range(3)]
        wg_st = [B.al([128, 8, 256], BF16, "wg_st%d" % k) for k in range(2)]
        wu_st = [B.al([128, 8, 256], BF16, "wu_st%d" % k) for k in range(2)]
        wo_st = [B.al([128, 22, 128], BF16, "wo_st%d" % k) for k in range(2)]
        aT = [B.al([128, 22, 512], BF16, "aT%d" % k) for k in range(3)]
        sg_scr = [B.al([128, 512], F32, "sg_scr%d" % k) for k in range(2)]
        win = ffn_w_in[l, f].rearrange("(kt p) c -> p kt c", p=128)
        wout = ffn_w_out[l, f].rearrange("(ft p) c -> p ft c", p=128)
        for tb in range(3):
            adaln(l, i, tb, hT[tb], PS[6])
        n = 0
        for ch in range(11):
            k = cnt["w"] % 2
            cnt["w"] += 1
            wg, wu = wg_st[k], wu_st[k]
            p.dma("pool", wg.ap, win[:, :, ch * 256:(ch + 1) * 256], writes=[wg])
            p.dma("pool", wu.ap, win[:, :, DFF + ch * 256:DFF + (ch + 1) * 256], writes=[wu])
            for half in range(2):
                fo = ch * 2 + half
                for tb in range(3):
                    h = hT[tb]
                    bg, bu = PS[(n % 2) * 2], PS[(n % 2) * 2 + 1]
                    for kt in range(8):
                        B.mm(bg, bg.ap, wg[:, kt, half * 128:(half + 1) * 128], h[:, kt, :], kt == 0, kt == 7, [wg, h])
                    for kt in range(8):
                        B.mm(bu, bu.ap, wu[:, kt, half * 128:(half + 1) * 128], h[:, kt, :], kt == 0, kt == 7, [wu, h])
                    s = sg_scr[n % 2]
                    n += 1
                    p.op("act", lambda e, s=s, bg=bg: e.activation(out=s.ap, in_=bg.ap, func=AF.Silu), reads=[bg], writes=[s])
                    dst = aT[tb][:, fo, :]
                    p.op("dve", lambda e, dst=dst, s=s, bu=bu: e.tensor_tensor(out=dst, in0=bu.ap, in1=s.ap, op=ALU.mult),
                         reads=[bu, s], writes=[aT[tb]])
        n = 0
        for ob in range(8):
            k = cnt["o"] % 2
            cnt["o"] += 1
            wo = wo_st[k]
            p.dma("pool", wo.ap, wout[:, :, ob * 128:(ob + 1) * 128], writes=[wo])
            for tb in range(3):
                j = 0 if tb < 2 else 1
                bank = PS[4 + n % 2]
                n += 1
                for ft in range(22):
                    B.mm(bank, bank.ap, wo[:, ft, :], aT[tb][:, ft, :], ft == 0, ft == 21, [wo, aT[tb]])
                gt = gtT[l][:, i * 8 + ob, j:j + 1]
                dst = xT[tb][:, ob, :]
                p.op("dve", lambda e, dst=dst, bank=bank, gt=gt: e.scalar_tensor_tensor(out=dst, in0=bank.ap, scalar=gt, in1=dst, op0=ALU.mult, op1=ALU.add),
                     reads=[bank, gtT[l], xT[tb]], writes=[xT[tb]])

    SEQS = [
        dict(name="a", T=TP, j=1, pi=0, hoff=0, xs=[(2, 0)]),
        dict(name="b", T=TP, j=1, pi=1, hoff=256, xs=[(2, 256)]),
        dict(name="s", T=TS, j=0, pi=None, hoff=0, xs=[(0, 0), (1, 0)]),
    ]
    wcnt = {"n": 0}

    def mixing(l):
        B.arena_reset()
        hT_ = B.al([128, 8, TS], BF16, "hT_")
        wsm = [B.al([128, 8, 64], BF16, "wsm%d" % k) for k in range(6)]
        ring = {"big": None, "nb": 0, "ns": 0}
        yT = B.al([128, 8, TS], BF16, "yT")
        ytmp = B.al([64, 4, 128], BF16, "ytmp")

        def yput(m, t0):
            ev = ytmp.ap.rearrange("p (t hh) n -> p hh t n", hh=2)
            p.dma("sp", yT[0:64, m * 2:m * 2 + 2, t0:t0 + 128], ev[:, 0, :, :], reads=[ytmp], writes=[yT])
            p.dma("sp", yT[64:128, m * 2:m * 2 + 2, t0:t0 + 128], ev[:, 1, :, :], reads=[ytmp], writes=[yT])

        mark = B.arena_off
        winv = w_in_bf[l].rearrange("(kt p) c -> p kt c", p=128)
        KMIX = int(os.environ.get("KMIX", "9"))

        def wbig():
            w = ring["big"][ring["nb"] % 2]
            ring["nb"] += 1
            return w

        def wload(col0, M):
            if M <= 64:
                w = wsm[ring["ns"] % 6]
                ring["ns"] += 1
            else:
                w = wbig()
            p.dma("sp", w[:, :, 0:M], winv[:, :, col0:col0 + M], reads=[wbfT], writes=[w], wpool=True)
            return w

        def proj(col0, M, sq, evac, ta=0, tb_=None):
            T = sq["T"]
            tb_ = T if tb_ is None else tb_
            w = wload(col0, M)
            for s0 in range(ta, tb_, 512):
                n = min(512, tb_ - s0)
                bank = nb()
                for kt in range(8):
                    B.mm(bank, bank[0:M, 0:n], w[:, kt, 0:M], hT_[:, kt, sq["hoff"] + s0:sq["hoff"] + s0 + n], kt == 0, kt == 7, [w, hT_])
                evac(bank, bank[0:M, 0:n], s0, n)

        def scan_pass(sq, dirn, dk, dv, lowrank, scalar, prep, S0, yout, ysum_in, post, cst):
            T = sq["T"]
            G = cst["G"]
            tri, mult, msk, nm = cst["tri"], cst["mult"], cst["ms"], cst["nm"]
            NI = cst["NI"]
            nblk = T // 128
            order = range(nblk) if dirn == 0 else range(nblk - 1, -1, -1)
            dkv = dk + dv
            order = list(order)
            pf = cst.get("pf", False)
            nxt_ops = prep(order[0] * 128, 0) if pf else None
            for oi, bi in enumerate(order):
                t0 = bi * 128
                ops = nxt_ops if pf else prep(t0, 0)
                if pf and oi + 1 < len(order):
                    nxt_ops = prep(order[oi + 1] * 128, (oi + 1) % 2)
                ld, q, k, v = ops["ld"], ops["q"], ops["k"], ops["v"]
                z, b = ops.get("z"), ops.get("b")
                bankA = nb()
                for h in range(4):
                    B.tr(bankA, bankA[:, h * dk:(h + 1) * dk], ld[:, h, :], [ld], inc=(h == 3))
                p.op("act", lambda e, bankA=bankA: e.copy(out=G["ldtm"].ap, in_=bankA[:, 0:4 * dk].rearrange("p (h d) -> p h d", h=4)),
                     reads=[bankA], writes=[G["ldtm"]])
                banks = []
                for vi in range(3):
                    bk = nb()
                    banks.append(bk)
                    for h in range(4):
                        B.mm(bk, bk[0:dk, h * 128:(h + 1) * 128], G["ldtm"][:, h, :], tri[:, vi, :], True, True, [G["ldtm"], tri], inc=(h == 3))
                b_in, b_ex, b_rev = banks
                v3 = lambda t: t.ap.rearrange("p h t -> p (h t)")
                p.op("act", lambda e, b_in=b_in: e.activation(out=v3(G["EGin"]), in_=b_in[0:dk, :], func=AF.Exp), reads=[b_in], writes=[G["EGin"]])
                p.op("act", lambda e, b_rev=b_rev: e.activation(out=v3(G["EGrev"]), in_=b_rev[0:dk, :], func=AF.Exp), reads=[b_rev], writes=[G["EGrev"]])
                if lowrank:
                    p.op("act", lambda e, b_ex=b_ex: e.activation(out=v3(G["EGex"]), in_=b_ex[0:dk, :], func=AF.Exp), reads=[b_ex], writes=[G["EGex"]])
                if scalar:
                    p.op("act", lambda e, b_in=b_in: e.copy(out=v3(G["GinB"]), in_=b_in[0:dk, :]), reads=[b_in], writes=[G["GinB"]])
                    p.op("act", lambda e, b_in=b_in: e.mul(out=v3(G["nGinB"]), in_=b_in[0:dk, :], mul=-1.0), reads=[b_in], writes=[G["nGinB"]])
                    p.op("act", lambda e, b_ex=b_ex: e.copy(out=v3(G["GexB"]), in_=b_ex[0:dk, :]), reads=[b_ex], writes=[G["GexB"]])
                else:
                    p.op("act", lambda e, b_in=b_in: e.activation(out=v3(G["EnGin"]), in_=b_in[0:dk, :], func=AF.Exp, scale=-1.0), reads=[b_in], writes=[G["EnGin"]])
                def ew(eng, dst, a, b_):
                    p.op(eng, lambda e: e.tensor_tensor(out=dst.ap, in0=a.ap, in1=b_.ap, op=ALU.mult), reads=[a, b_], writes=[dst])
                ew("pool", G["QtT"], q, G["EGin"])
                ew("dve", G["KdT"], k, G["EGrev"])
                if lowrank:
                    ew("pool", G["ZtT"], z, G["EGex"])
                    ew("dve", G["BdT"], b, G["EGrev"])
                def cp(eng, dst, src):
                    if eng == "act":
                        p.op("act", lambda e: e.copy(out=dst.ap, in_=src.ap), reads=[src], writes=[dst])
                    else:
                        p.op(eng, lambda e: e.tensor_copy(out=dst.ap, in_=src.ap), reads=[src], writes=[dst])
                if not scalar:
                    ew("pool", G["KrT"], k, G["EnGin"])
                    cp("pool", G["QLb"], G["QtT"])
                    if lowrank:
                        ew("pool", G["BrT"], b, G["EnGin"])
                        cp("act", G["ZLb"], G["ZtT"])
                    QL, KR, ZL, BR = G["QLb"], G["KrT"], G.get("ZLb"), G.get("BrT")
                else:
                    cp("pool", G["QLb"], q)
                    cp("act", G["KRb"], k)
                    if lowrank:
                        cp("pool", G["ZLb"], z)
                        cp("act", G["BRb"], b)
                    QL, KR, ZL, BR = G["QLb"], G["KRb"], G.get("ZLb"), G.get("BRb")
                bankV = nb()
                for h in range(4):
                    B.tr(bankV, bankV[:, h * dv:(h + 1) * dv], v[:, h, :], [v], inc=(h == 3))
                p.op("act", lambda e, bankV=bankV: e.copy(out=G["Vtm"].ap, in_=bankV[:, 0:4 * dv].rearrange("p (h d) -> p h d", h=4)), reads=[bankV], writes=[G["Vtm"]])
                bankK = nb()
                for h in range(4):
                    B.tr(bankK, bankK[:, h * dk:(h + 1) * dk], G["KdT"][:, h, :], [G["KdT"]], inc=(h == 3))
                p.op("dve", lambda e, bankK=bankK: e.tensor_copy(out=G["Kdtm"].ap, in_=bankK[:, 0:4 * dk].rearrange("p (h d) -> p h d", h=4)), reads=[bankK], writes=[G["Kdtm"]])
                if lowrank:
                    bankB = nb()
                    for h in range(4):
                        B.tr(bankB, bankB[:, h * dk:(h + 1) * dk], G["BdT"][:, h, :], [G["BdT"]], inc=(h == 3))
                    p.op("act", lambda e, bankB=bankB: e.copy(out=G["Bdtm"].ap, in_=bankB[:, 0:4 * dk].rearrange("p (h d) -> p h d", h=4)), reads=[bankB], writes=[G["Bdtm"]])
                    bankZ = nb()
                    for h in range(4):
                        B.tr(bankZ, bankZ[:, h * dk:(h + 1) * dk], G["ZtT"][:, h, :], [G["ZtT"]], inc=(h == 3))
                    p.op("dve", lambda e, bankZ=bankZ: e.tensor_copy(out=G["RHS"][:, :, 0:dk], in_=bankZ[:, 0:4 * dk].rearrange("p (h d) -> p h d", h=4)), reads=[bankZ], writes=[G["RHS"]])
                def head_gen(h, KR=KR, QL=QL, ZL=ZL, BR=BR, t0=t0):
                    sl = h % NI
                    M5 = G["mats%d" % sl]
                    LbT_t = G.get("LbT%d" % sl)
                    NT0 = G.get("NT0_%d" % sl)
                    bankM = nb()
                    B.mm(bankM, bankM[:, 0:128], KR[:, h, :], QL[:, h, :], True, True, [KR, QL], inc=not lowrank)
                    if lowrank:
                        B.mm(bankM, bankM[:, 128:256], BR[:, h, :], QL[:, h, :], True, True, [BR, QL], inc=False)
                        B.mm(bankM, bankM[:, 256:384], KR[:, h, :], ZL[:, h, :], True, True, [KR, ZL], inc=False)
                        B.mm(bankM, bankM[:, 384:512], BR[:, h, :], ZL[:, h, :], True, True, [BR, ZL], inc=True)
                        bankL = nb()
                        B.mm(bankL, bankL[:, 0:128], ZL[:, h, :], BR[:, h, :], True, True, [ZL, BR], inc=True)
                    if scalar:
                        bankE = nb()
                        c64 = G["c64"]
                        combos = [(G["nGinB"], G["GinB"], 0), (G["nGinB"], G["GexB"], 1)]
                        for ci, (La, Rb, ni) in enumerate(combos):
                            o = bankE[:, ci * 128:(ci + 1) * 128]
                            B.mm(bankE, o, La[:, h, :], c64.ap, True, False, [La, c64], inc=False)
                            B.mm(bankE, o, c64.ap, Rb[:, h, :], False, False, [Rb, c64], inc=False)
                            B.mm(bankE, o, B.ident.ap, nm[:, ni, :], False, True, [B.ident, nm], inc=False)
                        o = bankE[:, 256:384]
                        B.mm(bankE, o, G["GexB"][:, h, :], c64.ap, True, False, [G["GexB"], c64], inc=False)
                        B.mm(bankE, o, c64.ap, G["nGinB"][:, h, :], False, False, [G["nGinB"], c64], inc=False)
                        B.mm(bankE, o, B.ident.ap, nm[:, 2, :], False, True, [B.ident, nm], inc=True)
                        Em = G["Emat%d" % sl]
                        p.op("act", lambda e, bankE=bankE, Em=Em: e.activation(out=Em.ap.rearrange("p a t -> p (a t)"), in_=bankE[:, 0:384], func=AF.Exp), reads=[bankE], writes=[Em])
                        nmm = 2 if lowrank else 1
                        p.op("dve", lambda e, bankM=bankM, Em=Em, nmm=nmm: e.tensor_tensor(out=M5[:, 0:nmm, :], in0=bankM[:, 0:128 * nmm].rearrange("p (a t) -> p a t", a=nmm),
                                                                                 in1=Em[:, 0:1, :].to_broadcast([128, nmm, 128]), op=ALU.mult), reads=[bankM, Em], writes=[M5])
                        if lowrank:
                            p.op("dve", lambda e, bankM=bankM, Em=Em: e.tensor_tensor(out=M5[:, 2, :], in0=bankM[:, 256:384], in1=Em[:, 1, :], op=ALU.mult), reads=[bankM, Em], writes=[M5])
                            p.op("dve", lambda e, bankM=bankM, Em=Em: e.tensor_tensor(out=LbT_t.ap, in0=bankM[:, 384:512], in1=Em[:, 1, :], op=ALU.mult), reads=[bankM, Em], writes=[LbT_t])
                            p.op("dve", lambda e, bankL=bankL, Em=Em: e.tensor_tensor(out=NT0.ap, in0=bankL[:, 0:128], in1=Em[:, 2, :], op=ALU.mult), reads=[bankL, Em], writes=[NT0])
                    else:
                        nmm = 3 if lowrank else 1
                        p.op("dve", lambda e, bankM=bankM, nmm=nmm: e.tensor_tensor(out=M5[:, 0:nmm, :], in0=bankM[:, 0:128 * nmm].rearrange("p (a t) -> p a t", a=nmm),
                                                                             in1=mult[:, 0:nmm, :], op=ALU.mult), reads=[bankM, mult], writes=[M5])
                        if lowrank:
                            p.op("dve", lambda e, bankM=bankM: e.tensor_tensor(out=LbT_t.ap, in0=bankM[:, 384:512], in1=mult[:, 3, :], op=ALU.mult), reads=[bankM, mult], writes=[LbT_t])
                            p.op("dve", lambda e, bankL=bankL: e.tensor_tensor(out=NT0.ap, in0=bankL[:, 0:128], in1=msk.ap, op=ALU.mult), reads=[bankL, msk], writes=[NT0])
                    MkT = M5[:, 0, :]
                    MbT, LkT = (M5[:, 1, :], M5[:, 2, :]) if lowrank else (None, None)
                    LbT = LbT_t.ap if lowrank else None
                    yield
                    if lowrank:
                        Ncur, NTcur = None, NT0
                        R = G["R0_%d" % sl]
                        p.op("pool", lambda e, R=R, LbT=LbT: e.tensor_tensor(out=R.ap, in0=LbT, in1=B.ident.ap, op=ALU.add), reads=[LbT_t, B.ident], writes=[R])
                        N_ap, N_t = LbT, LbT_t
                        for lv in range(5):
                            if lv % 2 == 0:
                                Pn_ap, Pn_t, PTn = G["P0_%d" % sl].ap, G["P0_%d" % sl], G["PT0_%d" % sl]
                            else:
                                Pn_ap, Pn_t, PTn = LbT_t.ap, LbT_t, NT0
                            Rn = G["R%d_%d" % ((lv + 1) % 2, sl)]
                            bp = nb()
                            last = lv == 4
                            if not last:
                                B.mm(bp, bp[:, 0:128], NTcur.ap, N_ap, True, True, [NTcur, N_t], inc=False)
                            B.mm(bp, bp[:, 128:256], N_ap, NTcur.ap, True, True, [NTcur, N_t], inc=True)
                            if not last:
                                p.op("act", lambda e, bp=bp, Pn_ap=Pn_ap: e.copy(out=Pn_ap, in_=bp[:, 0:128]), reads=[bp], writes=[Pn_t])
                            p.op("act", lambda e, bp=bp, PTn=PTn: e.copy(out=PTn.ap, in_=bp[:, 128:256]), reads=[bp], writes=[PTn])
                            yield
                            br = nb()
                            B.mm(br, br[:, 0:128], PTn.ap, R.ap, True, True, [PTn, R], inc=True)
                            p.op("dve", lambda e, br=br, R=R, Rn=Rn: e.tensor_tensor(out=Rn.ap, in0=br[:, 0:128], in1=R.ap, op=ALU.add), reads=[br, R], writes=[Rn])
                            R = Rn
                            N_ap, N_t, NTcur = Pn_ap, Pn_t, PTn
                            yield
                        bl = nb()
                        B.mm(bl, bl[:, 0:dv], LkT, G["Vtm"][:, h, :], True, True, [M5, G["Vtm"]], inc=True)
                        p.op("act", lambda e, bl=bl, h=h: e.copy(out=G["RHS"][:, h, dk:dkv], in_=bl[:, 0:dv]), reads=[bl], writes=[G["RHS"]])
                        yield
                        bw = nb()
                        B.mm(bw, bw[:, 0:dkv], R.ap, G["RHS"][:, h, :], True, True, [R, G["RHS"]], inc=True)
                        WU = G["WU%d" % sl]
                        p.op("act", lambda e, bw=bw, WU=WU: e.copy(out=WU.ap, in_=bw[:, 0:dkv]), reads=[bw], writes=[WU])
                        yield
                        bq = nb()
                        B.mm(bq, bq[0:dk, 0:128], WU[:, 0:dk], MbT, True, True, [WU, M5], inc=True)
                        p.op("dve", lambda e, bq=bq, h=h: e.tensor_tensor(out=G["QeffT"][:, h, :], in0=bq[0:dk, 0:128], in1=G["QtT"][:, h, :], op=ALU.add),
                             reads=[bq, G["QtT"]], writes=[G["QeffT"]])
                        by = nb()
                        B.mm(by, by[0:dv, 0:128], WU[:, dk:dkv], MbT, True, False, [WU, M5], inc=False)
                        B.mm(by, by[0:dv, 0:128], G["Vtm"][:, h, :], MkT, False, True, [G["Vtm"], M5], inc=True)
                        p.op("act", lambda e, by=by, h=h: e.copy(out=G["Y0T"][:, h, :], in_=by[0:dv, 0:128]), reads=[by], writes=[G["Y0T"]])
                        for c in range(2):
                            r0 = 64 * c
                            gcol = G["EGin"][:, h, (r0 + 63 if dirn == 0 else r0):(r0 + 64 if dirn == 0 else r0 + 1)]
                            ba = nb()
                            B.mm(ba, ba[0:dk, 0:dk], WU[r0:r0 + 64, 0:dk], G["Bdtm"][r0:r0 + 64, h, :], True, True, [WU, G["Bdtm"]], inc=True)
                            p.op("dve", lambda e, ba=ba, gcol=gcol, h=h, c=c: e.scalar_tensor_tensor(out=G["AcT"][:, h * 2 + c, :], in0=B.ident[0:dk, 0:dk], scalar=gcol, in1=ba[0:dk, 0:dk],
                                                                                      op0=ALU.mult, op1=ALU.add), reads=[ba, G["EGin"], B.ident], writes=[G["AcT"]])
                            bb = nb()
                            B.mm(bb, bb[0:dk, 0:dv], G["Bdtm"][r0:r0 + 64, h, :], WU[r0:r0 + 64, dk:dkv], True, False, [WU, G["Bdtm"]], inc=False)
                            B.mm(bb, bb[0:dk, 0:dv], G["Kdtm"][r0:r0 + 64, h, :], G["Vtm"][r0:r0 + 64, h, :], False, True, [G["Kdtm"], G["Vtm"]], inc=True)
                            p.op("act", lambda e, bb=bb, h=h, c=c: e.copy(out=G["Bc"][:, h * 2 + c, :], in_=bb[0:dk, 0:dv]), reads=[bb], writes=[G["Bc"]])
                    else:
                        by = nb()
                        B.mm(by, by[0:dv, 0:128], G["Vtm"][:, h, :], MkT, True, True, [G["Vtm"], M5], inc=True)
                        p.op("act", lambda e, by=by, h=h: e.copy(out=G["Y0T"][:, h, :], in_=by[0:dv, 0:128]), reads=[by], writes=[G["Y0T"]])
                        for c in range(2):
                            r0 = 64 * c
                            bb = nb()
                            B.mm(bb, bb[0:dk, 0:dv], G["Kdtm"][r0:r0 + 64, h, :], G["Vtm"][r0:r0 + 64, h, :], True, True, [G["Kdtm"], G["Vtm"]], inc=True)
                            p.op("act", lambda e, bb=bb, h=h, c=c: e.copy(out=G["Bc"][:, h * 2 + c, :], in_=bb[0:dk, 0:dv]), reads=[bb], writes=[G["Bc"]])
                gens_all = [head_gen(h) for h in range(4)]
                for g0 in range(0, 4, NI):
                    live = gens_all[g0:g0 + NI]
                    while live:
                        for g_ in list(live):
                            try:
                                next(g_)
                            except StopIteration:
                                live.remove(g_)
                QE = G["QeffT"] if lowrank else G["QtT"]
                for c in ((0, 1) if dirn == 0 else (1, 0)):
                    r0 = 64 * c
                    bY = nb()
                    for h in range(4):
                        B.mm(bY, bY[0:dv, h * 64:(h + 1) * 64], S0[:, h, :], QE[:, h, r0:r0 + 64], True, True, [S0, QE], inc=(h == 3))
                    ysrc = bY[0:dv, 0:256].rearrange("p (h t) -> p h t", h=4)
                    if ysum_in is None:
                        p.op("dve", lambda e, ysrc=ysrc, r0=r0, t0=t0: e.tensor_tensor(out=yout[:, :, t0 + r0:t0 + r0 + 64], in0=ysrc, in1=G["Y0T"][:, :, r0:r0 + 64], op=ALU.add),
                             reads=[bY, G["Y0T"]], writes=[yout])
                    else:
                        p.op("dve", lambda e, ysrc=ysrc, r0=r0: e.tensor_tensor(out=G["ysum"][:, :, r0:r0 + 64], in0=ysrc, in1=G["Y0T"][:, :, r0:r0 + 64], op=ALU.add),
                             reads=[bY, G["Y0T"]], writes=[G["ysum"]])
                        p.op("pool", lambda e, r0=r0, t0=t0: e.tensor_tensor(out=G["ysum"][:, :, r0:r0 + 64], in0=G["ysum"][:, :, r0:r0 + 64], in1=ysum_in[:, :, t0 + r0:t0 + r0 + 64], op=ALU.add),
                             reads=[G["ysum"], ysum_in], writes=[G["ysum"]])
                    if lowrank:
                        bS = nb()
                        for h in range(4):
                            B.mm(bS, bS[0:dk, h * dv:(h + 1) * dv], G["AcT"][:, h * 2 + c, :], S0[:, h, :], True, True, [G["AcT"], S0], inc=(h == 3))
                        p.op("dve", lambda e, bS=bS, c=c: e.tensor_tensor(out=S0.ap, in0=bS[0:dk, 0:4 * dv].rearrange("p (h d) -> p h d", h=4),
                                                                        in1=G["Bc"].ap.rearrange("p (h c) d -> p h c d", c=2)[:, :, c, :], op=ALU.add),
                             reads=[bS, G["Bc"], S0], writes=[S0])
                    else:
                        for h in range(4):
                            gcol = G["EGin"][:, h, (r0 + 63 if dirn == 0 else r0):(r0 + 64 if dirn == 0 else r0 + 1)]
                            p.op("dve", lambda e, h=h, c=c, gcol=gcol: e.scalar_tensor_tensor(out=S0[:, h, :], in0=S0[:, h, :], scalar=gcol, in1=G["Bc"][:, h * 2 + c, :], op0=ALU.mult, op1=ALU.add),
                                 reads=[S0, G["EGin"], G["Bc"]], writes=[S0])
                if ysum_in is not None:
                    post(t0, G["ysum"], ops)

        def scan_alloc(dk, dv, lowrank, scalar):
            NI_ = 4
            G = {}
            A = lambda nm_, shp, dt_=F32: G.__setitem__(nm_, B.al(shp, dt_, nm_))
            A("ldtm", [128, 4, dk])
            for nm_ in ("EGin", "EGrev", "QtT", "KdT"):
                A(nm_, [dk, 4, 128])
            if lowrank:
                for nm_ in ("EGex", "ZtT", "BdT"):
                    A(nm_, [dk, 4, 128])
            if scalar:
                for nm_ in ("GinB", "nGinB", "GexB"):
                    A(nm_, [dk, 4, 128])
                for sl_ in range(NI_):
                    A("Emat%d" % sl_, [128, 3, 128])
                A("c64", [dk, 128])
                A("KRb", [dk, 4, 128], BF16)
                if lowrank:
                    A("BRb", [dk, 4, 128], BF16)
                p.op("pool", lambda e: e.memset(G["c64"].ap, 1.0 / dk), writes=[G["c64"]])
            else:
                A("EnGin", [dk, 4, 128])
                A("KrT", [dk, 4, 128], BF16)
                if lowrank:
                    A("BrT", [dk, 4, 128], BF16)
            A("QLb", [dk, 4, 128], BF16)
            if lowrank:
                A("ZLb", [dk, 4, 128], BF16)
            A("Vtm", [128, 4, dv], BF16)
            A("Kdtm", [128, 4, dk], BF16)
            for sl_ in range(NI_):
                A("mats%d" % sl_, [128, 3 if lowrank else 1, 128] if not lowrank else [128, 3, 128], BF16) if lowrank else A("mats%d" % sl_, [128, 2, 128], BF16)
            A("Y0T", [dv, 4, 128])
            A("Bc", [dk, 8, dv])
            A("ysum", [dv, 4, 128])
            if lowrank:
                A("Bdtm", [128, 4, dk], BF16)
                A("RHS", [128, 4, dk + dv])
                for sl_ in range(NI_):
                    for nm_ in ("NT0", "R0", "R1", "P0", "PT0"):
                        A("%s_%d" % (nm_, sl_), [128, 128])
                    A("WU%d" % sl_, [128, dk + dv], BF16)
                    A("LbT%d" % sl_, [128, 128])
                A("QeffT", [dk, 4, 128])
                A("AcT", [dk, 8, dk])
            cst = dict(G=G, NI=NI_, tri=B.al([128, 3, 128], F32, "tri"),
                       mult=None if scalar else B.al([128, 4, 128], F32, "mult"),
                       ms=None if scalar else B.al([128, 128], F32, "ms"),
                       nm=B.al([128, 3, 128], F32, "nm") if scalar else None)
            return cst

        def load_dir_consts(cst, dirn):
            p.dma("sp", cst["tri"].ap, c_tri[dirn], writes=[cst["tri"]])
            if cst["mult"] is not None:
                p.dma("sp", cst["mult"].ap, c_mult[dirn], writes=[cst["mult"]])
                p.dma("sp", cst["ms"].ap, c_ms[dirn], writes=[cst["ms"]])
            if cst["nm"] is not None:
                p.dma("sp", cst["nm"].ap, c_nm[dirn], writes=[cst["nm"]])

        def gla(sq):
            T = sq["T"]
            dk, dv = 32, 64
            cst = scan_alloc(dk, dv, False, False)
            cst["pf"] = True
            yf = B.al([dv, 4, T], BF16, "yf")
            S0 = B.al([dk, 4, dv], F32, "S0")
            Pq_ = [B.al([dk, 4, 128], F32, "Pq%d" % k_) for k_ in range(2)]
            Pk_ = [B.al([dk, 4, 128], F32, "Pk%d" % k_) for k_ in range(2)]
            Pv_ = [B.al([dv, 4, 128], F32, "Pv%d" % k_) for k_ in range(2)]
            Pr_ = [B.al([dv, 4, 128], F32, "Pr%d" % k_) for k_ in range(2)]
            ad = B.al([16, 128], F32, "ad")
            ee = B.al([dk, 4, 128], F32, "ee")
            ld_ = [B.al([dk, 4, 128], F32, "ld%d" % k_) for k_ in range(2)]
            aup = B.al([16, 128], F32, "aup")
            nab = B.al([dk, 4], F32, "nab")
            gng = B.al([dv, 1], F32, "gng")
            sqs = B.al([dv, 512], F32, "sqs")
            rr_ = B.al([dv, 512], F32, "rr_")
            p.dma("sp", gng.ap, gla_norm_g[l:l + 1, :].rearrange("o d -> d o"), writes=[gng], allow_slow_non_contiguous=True)
            for dirn in range(2):
                load_dir_consts(cst, dirn)
                p.dma("sp", aup.ap, gla_alpha_up[l, dirn], writes=[aup])
                p.dma("sp", nab.ap, gla_alpha_b[l, dirn].rearrange("(h d) -> d h", d=dk), writes=[nab], allow_slow_non_contiguous=True)
                p.op("dve", lambda e: e.tensor_scalar(out=nab.ap, in0=nab.ap, scalar1=-1.0, scalar2=None, op0=ALU.mult), reads=[nab], writes=[nab])
                if sq["pi"] is None:
                    p.dma("sp", S0.ap, state_gla[l, dirn].rearrange("h k v -> k h v"), writes=[S0])
                else:
                    p.op("pool", lambda e: e.memset(S0.ap, 0.0), writes=[S0])

                def prep(t0, sl, dirn=dirn):
                    Pq, Pk, Pv, Pr, ld = Pq_[sl], Pk_[sl], Pv_[sl], Pr_[sl], ld_[sl]
                    for h in range(4):
                        proj(B0 + 32 * h, 32, sq, lambda bank, ap, s0, n, h=h: p.op("act", lambda e: e.mul(out=Pq[:, h, :], in_=ap, mul=float(32 ** -0.5)), reads=[bank], writes=[Pq]), t0, t0 + 128)
                        proj(B0 + 128 + 32 * h, 32, sq, lambda bank, ap, s0, n, h=h: p.op("act", lambda e: e.copy(out=Pk[:, h, :], in_=ap), reads=[bank], writes=[Pk]), t0, t0 + 128)
                        proj(B0 + 256 + 64 * h, 64, sq, lambda bank, ap, s0, n, h=h: p.op("dve", lambda e: e.tensor_copy(out=Pv[:, h, :], in_=ap), reads=[bank], writes=[Pv]), t0, t0 + 128)
                        if dirn == 1:
                            proj(B0 + 544 + 64 * h, 64, sq, lambda bank, ap, s0, n, h=h: p.op("act", lambda e: e.activation(out=Pr[:, h, :], in_=ap, func=AF.Silu), reads=[bank], writes=[Pr]), t0, t0 + 128)
                    proj(B0 + 512 + 16 * dirn, 16, sq, lambda bank, ap, s0, n: p.op("act", lambda e: e.copy(out=ad.ap, in_=ap), reads=[bank], writes=[ad]), t0, t0 + 128)
                    bk = nb()
                    for h in range(4):
                        B.mm(bk, bk[0:dk, h * 128:(h + 1) * 128], aup[:, h * dk:(h + 1) * dk], ad.ap, True, True, [aup, ad], inc=(h == 3))
                    for h in range(4):
                        p.op("act", lambda e, h=h, bk=bk: e.activation(out=ee[:, h, :], in_=bk[0:dk, h * 128:(h + 1) * 128], func=AF.Exp, bias=nab[:, h:h + 1], scale=-1.0),
                             reads=[bk, nab], writes=[ee])
                    p.op("act", lambda e: e.activation(out=ld.ap, in_=ee.ap, func=AF.Ln, bias=B.one1[0:dk, 0:1], scale=1.0), reads=[ee, B.one1], writes=[ld])
                    p.op("pool", lambda e: e.tensor_scalar(out=ld.ap, in0=ld.ap, scalar1=-1.0 / 16.0, scalar2=None, op0=ALU.mult), reads=[ld], writes=[ld])
                    return dict(ld=ld, q=Pq, k=Pk, v=Pv, r=Pr)

                def post(t0, ys, ops):
                    yv = ys.ap.rearrange("p h t -> p (h t)")
                    p.op("act", lambda e: e.activation(out=sqs.ap, in_=yv, func=AF.Square), reads=[ys], writes=[sqs])
                    bk = nb()
                    B.mm(bk, bk[0:dv, :], B.ones[0:dv, 0:dv], sqs.ap, True, True, [B.ones, sqs], inc=True)
                    p.op("act", lambda e, bk=bk: e.activation(out=rr_.ap, in_=bk[0:dv, :], func=AF.Sqrt, bias=B.eps6[0:dv, 0:1], scale=1.0 / dv), reads=[bk, B.eps6], writes=[rr_])
                    p.op("dve", lambda e: e.reciprocal(out=rr_.ap, in_=rr_.ap), reads=[rr_], writes=[rr_])
                    p.op("dve", lambda e: e.scalar_tensor_tensor(out=sqs.ap, in0=yv, scalar=gng[:, 0:1], in1=rr_.ap, op0=ALU.mult, op1=ALU.mult), reads=[ys, gng, rr_], writes=[sqs])
                    p.op("pool", lambda e: e.tensor_tensor(out=ytmp.ap, in0=sqs.ap.rearrange("p (h t) -> p h t", h=4), in1=ops["r"].ap, op=ALU.mult),
                         reads=[sqs, ops["r"]], writes=[ytmp])
                    yput(1, t0)

                scan_pass(sq, dirn, dk, dv, False, False, prep, S0, yf, None if dirn == 0 else yf, post, cst)
                if sq["pi"] is not None:
                    p.dma("sp", st_gla[sq["pi"], l, dirn].rearrange("h k v -> k h v"), S0.ap, reads=[S0])

        def rwkv(sq):
            T = sq["T"]
            dk, dv = 64, 64
            cst = scan_alloc(dk, dv, True, False)
            yf = B.al([dv, 4, T], BF16, "yf")
            S0 = B.al([dk, 4, dv], F32, "S0")
            raw = [B.al([64, 130], F32, "raw%d" % k_) for k_ in range(1)]
            lt_ = B.al([64, 128], F32, "lt_")
            Pr_ = [B.al([64, 4, 128], F32, "Pr%d" % k_) for k_ in range(1)]
            Pk_ = [B.al([64, 4, 128], F32, "Pk%d" % k_) for k_ in range(1)]
            Pv_ = [B.al([64, 4, 128], F32, "Pv%d" % k_) for k_ in range(1)]
            kk_ = [B.al([64, 4, 128], F32, "kk%d" % k_) for k_ in range(1)]
            ld_ = [B.al([64, 4, 128], F32, "ld%d" % k_) for k_ in range(1)]
            aa = B.al([64, 4, 128], F32, "aa")
            af_ = [B.al([64, 4, 128], F32, "af%d" % k_) for k_ in range(1)]
            bop_ = [B.al([64, 4, 128], F32, "bop%d" % k_) for k_ in range(1)]
            gT_ = [B.al([64, 4, 128], F32, "gT%d" % k_) for k_ in range(1)]
            wd = B.al([64, 128], F32, "wd")
            adt = B.al([64, 128], F32, "adt")
            adf = B.al([64, 128], F32, "adf")
            sgd = [B.al([64, 128], F32, "sgd%d" % k_) for k_ in range(3)]
            sqs = B.al([64, 512], F32, "sqs")
            rr_ = B.al([64, 512], F32, "rr_")
            wup = B.al([64, 256], F32, "wup")
            aup = B.al([64, 256], F32, "aup")
            aupf = B.al([64, 256], F32, "aupf")
            gup = [B.al([64, 256], F32, "gup%d" % k_) for k_ in range(3)]
            hm = B.al([64, 19], F32, "hm")
            omm = B.al([64, 19], F32, "omm")
            prm = B.al([64, 8, 4], F32, "prm")
            epsg = B.al([64, 1], F32, "epsg")
            stt_ap = sqs.ap[:, 0:256].rearrange("p (h d) -> p h d", h=4)
            p.op("pool", lambda e: e.memset(hm.ap, 0.0), writes=[hm])
            p.dma("sp", hm[:, 0:18], rwkv_mu[l, 0:1152].rearrange("(g d) -> d g", d=64), writes=[hm], allow_slow_non_contiguous=True)
            p.dma("sp", hm[0:32, 18:19], rwkv_mu[l, 1152:1184].rearrange("(g d) -> d g", d=32), writes=[hm], allow_slow_non_contiguous=True)
            p.op("dve", lambda e: e.tensor_scalar(out=omm.ap, in0=hm.ap, scalar1=-1.0, scalar2=1.0, op0=ALU.mult, op1=ALU.add), reads=[hm], writes=[omm])
            p.op("dve", lambda e: e.tensor_scalar(out=hm.ap, in0=hm.ap, scalar1=0.5, scalar2=None, op0=ALU.mult), reads=[hm], writes=[hm])
            hd = lambda apx: apx.rearrange("(h d) -> d h", d=64)
            p.dma("sp", prm[:, 0, :], hd(rwkv_k_k[l]), writes=[prm], allow_slow_non_contiguous=True)
            p.dma("sp", prm[:, 1, :], hd(rwkv_k_a[l]), writes=[prm], allow_slow_non_contiguous=True)
            p.dma("sp", prm[:, 2, :], rwkv_r_k[l].rearrange("h d -> d h"), writes=[prm], allow_slow_non_contiguous=True)
            p.dma("sp", prm[:, 5, :], hd(rwkv_a0[l, 0]), writes=[prm], allow_slow_non_contiguous=True)
            p.dma("sp", prm[:, 6, :], hd(rwkv_ln_g[l]), writes=[prm], allow_slow_non_contiguous=True)
            p.dma("sp", prm[:, 7, :], hd(rwkv_ln_b[l]), writes=[prm], allow_slow_non_contiguous=True)
            p.op("pool", lambda e: e.memset(epsg.ap, 64e-5), writes=[epsg])
            p.dma("sp", aupf.ap, rwkv_a_up[l, 0], writes=[aupf])
            p.dma("sp", gup[0].ap, rwkv_g_up[l, 0:64, :], writes=[gup[0]])
            p.dma("sp", gup[1].ap, rwkv_g_up[l, 64:128, :], writes=[gup[1]])
            p.dma("sp", gup[2][0:32, :], rwkv_g_up[l, 128:160, :], writes=[gup[2]])
            rawc = {"n": 0}
            bc = lambda col: prm[:, col, :].unsqueeze(2).to_broadcast([64, 4, 128])

            def lerp_group(col0, M, g, dst_ap, dst_t, t0):
                r = raw[0]
                rawc["n"] += 1
                ta, tb_ = max(t0 - 1, 0), min(t0 + 129, T)
                o = ta - (t0 - 1)
                if o > 0 or tb_ < t0 + 129:
                    p.op("pool", lambda e, r=r: e.memset(r.ap, 0.0), writes=[r])
                proj(col0, M, sq, lambda bank, ap, s0, n, r=r, o=o: p.op("act", lambda e: e.copy(out=r[0:M, o:o + n], in_=ap), reads=[bank], writes=[r]), ta, tb_)
                p.op("pool", lambda e, r=r: e.tensor_tensor(out=lt_[0:M, :], in0=r[0:M, 0:128], in1=r[0:M, 2:130], op=ALU.add), reads=[r], writes=[lt_])
                p.op("dve", lambda e: e.tensor_scalar(out=lt_[0:M, :], in0=lt_[0:M, :], scalar1=hm[0:M, g:g + 1], scalar2=None, op0=ALU.mult), reads=[lt_, hm], writes=[lt_])
                p.op("dve", lambda e, r=r: e.scalar_tensor_tensor(out=dst_ap, in0=r[0:M, 1:129], scalar=omm[0:M, g:g + 1], in1=lt_[0:M, :], op0=ALU.mult, op1=ALU.add), reads=[r, omm, lt_], writes=[dst_t])

            def l2n(x):
                xv = x.ap.rearrange("p h t -> p (h t)")
                p.op("act", lambda e: e.activation(out=sqs.ap, in_=xv, func=AF.Square), reads=[x], writes=[sqs])
                bk = nb()
                B.mm(bk, bk[0:64, :], B.ones[0:64, 0:64], sqs.ap, True, True, [B.ones, sqs], inc=True)
                p.op("act", lambda e, bk=bk: e.activation(out=rr_.ap, in_=bk[0:64, :], func=AF.Sqrt, bias=B.eps6[0:64, 0:1], scale=1.0), reads=[bk, B.eps6], writes=[rr_])
                p.op("dve", lambda e: e.reciprocal(out=rr_.ap, in_=rr_.ap), reads=[rr_], writes=[rr_])
                p.op("dve", lambda e: e.tensor_tensor(out=xv, in0=xv, in1=rr_.ap, op=ALU.mult), reads=[x, rr_], writes=[x])

            def lora_sig(dst, src, wmat, bias_col):
                bk = nb()
                for h in range(4):
                    B.mm(bk, bk[0:64, h * 128:(h + 1) * 128], wmat[:, h * 64:(h + 1) * 64], src.ap, True, True, [wmat, src], inc=(h == 3))
                for h in range(4):
                    p.op("act", lambda e, h=h, bk=bk: e.activation(out=dst[:, h, :], in_=bk[0:64, h * 128:(h + 1) * 128], func=AF.Sigmoid, bias=prm[:, bias_col, h:h + 1], scale=1.0),
                         reads=[bk, prm], writes=[dst])

            for dirn in range(2):
                load_dir_consts(cst, dirn)
                p.dma("sp", wup.ap, rwkv_w_up[l, dirn], writes=[wup])
                p.dma("sp", aup.ap, rwkv_a_up[l, dirn], writes=[aup])
                p.dma("sp", prm[:, 3, :], hd(rwkv_w0[l, dirn]), writes=[prm], allow_slow_non_contiguous=True)
                p.dma("sp", prm[:, 4, :], hd(rwkv_a0[l, dirn]), writes=[prm], allow_slow_non_contiguous=True)
                if sq["pi"] is None:
                    p.dma("sp", stt_ap, state_rwkv[l, dirn].rearrange("h v k -> v h k"), writes=[sqs])
                    bk = nb()
                    for h in range(4):
                        B.tr(bk, bk[0:64, h * 64:(h + 1) * 64], stt_ap[:, h, :], [sqs], inc=(h == 3))
                    p.op("act", lambda e, bk=bk: e.copy(out=S0.ap, in_=bk[0:64, 0:256].rearrange("p (h d) -> p h d", h=4)), reads=[bk], writes=[S0])
                else:
                    p.op("pool", lambda e: e.memset(S0.ap, 0.0), writes=[S0])

                def prep(t0, sl, dirn=dirn):
                    Pr, Pk, Pv, kk, ld, af, bop, gT = Pr_[0], Pk_[0], Pv_[0], kk_[0], ld_[0], af_[0], bop_[0], gT_[0]
                    for h in range(4):
                        lerp_group(A0 + 64 * h, 64, h, Pr[:, h, :], Pr, t0)
                        lerp_group(A0 + 256 + 64 * h, 64, 4 + h, Pk[:, h, :], Pk, t0)
                        lerp_group(A0 + 512 + 64 * h, 64, 8 + h, Pv[:, h, :], Pv, t0)
                    lerp_group(A0 + 768 + 64 * dirn, 64, 12 + dirn, wd.ap, wd, t0)
                    lerp_group(A0 + 896 + 64 * dirn, 64, 14 + dirn, adt.ap, adt, t0)
                    p.op("dve", lambda e: e.tensor_tensor(out=kk.ap, in0=Pk.ap, in1=bc(0), op=ALU.mult), reads=[Pk, prm], writes=[kk])
                    l2n(kk)
                    p.op("act", lambda e: e.activation(out=wd.ap, in_=wd.ap, func=AF.Tanh), reads=[wd], writes=[wd])
                    lora_sig(ld, wd, wup, 3)
                    p.op("pool", lambda e: e.tensor_scalar(out=ld.ap, in0=ld.ap, scalar1=-float(np.exp(-0.5)), scalar2=None, op0=ALU.mult), reads=[ld], writes=[ld])
                    lora_sig(aa, adt, aup, 4)
                    if dirn == 1:
                        lerp_group(A0 + 896, 64, 14, adf.ap, adf, t0)
                        lora_sig(af, adf, aupf, 5)
                        for gi, (c0_, M_) in enumerate(((1024, 64), (1088, 64), (1152, 32))):
                            lerp_group(A0 + c0_, M_, 16 + gi, sgd[gi][0:M_, :], sgd[gi], t0)
                            p.op("act", lambda e, gi=gi, M_=M_: e.activation(out=sgd[gi][0:M_, :], in_=sgd[gi][0:M_, :], func=AF.Sigmoid), reads=[sgd[gi]], writes=[sgd[gi]])
                        bk = nb()
                        for h in range(4):
                            for gi, M_ in enumerate((64, 64, 32)):
                                B.mm(bk, bk[0:64, h * 128:(h + 1) * 128], gup[gi][0:M_, h * 64:(h + 1) * 64], sgd[gi][0:M_, :], gi == 0, gi == 2, [gup[gi], sgd[gi]], inc=(h == 3 and gi == 2))
                        p.op("act", lambda e, bk=bk: e.copy(out=gT.ap.rearrange("p h t -> p (h t)"), in_=bk[0:64, :]), reads=[bk], writes=[gT])
                        p.op("pool", lambda e: e.tensor_tensor(out=af.ap, in0=af.ap, in1=aa.ap, op=ALU.add), reads=[af, aa], writes=[af])
                        p.op("dve", lambda e: e.scalar_tensor_tensor(out=af.ap, in0=af.ap, scalar=-2.0, in1=bc(1), op0=ALU.add, op1=ALU.mult), reads=[af, prm], writes=[af])
                        p.op("dve", lambda e: e.scalar_tensor_tensor(out=af.ap, in0=af.ap, scalar=2.0, in1=Pk.ap, op0=ALU.add, op1=ALU.mult), reads=[af, Pk], writes=[af])
                        p.op("pool", lambda e: e.tensor_tensor(out=af.ap, in0=af.ap, in1=Pr.ap, op=ALU.mult), reads=[af, Pr], writes=[af])
                        p.op("pool", lambda e: e.tensor_tensor(out=af.ap, in0=af.ap, in1=bc(2), op=ALU.mult), reads=[af, prm], writes=[af])
                        bk2 = nb()
                        B.mm(bk2, bk2[0:64, :], B.ones[0:64, 0:64], af.ap.rearrange("p h t -> p (h t)"), True, True, [B.ones, af], inc=True)
                        p.op("dve", lambda e, bk2=bk2: e.tensor_tensor(out=af.ap.rearrange("p h t -> p (h t)"), in0=bk2[0:64, :], in1=Pv.ap.rearrange("p h t -> p (h t)"), op=ALU.mult), reads=[bk2, Pv], writes=[af])
                    p.op("pool", lambda e: e.tensor_tensor(out=bop.ap, in0=kk.ap, in1=aa.ap, op=ALU.mult), reads=[kk, aa], writes=[bop])
                    p.op("pool", lambda e: e.tensor_scalar(out=kk.ap, in0=kk.ap, scalar1=-1.0, scalar2=None, op0=ALU.mult), reads=[kk], writes=[kk])
                    p.op("dve", lambda e: e.scalar_tensor_tensor(out=aa.ap, in0=aa.ap, scalar=-1.0, in1=bc(1), op0=ALU.add, op1=ALU.mult), reads=[aa, prm], writes=[aa])
                    p.op("dve", lambda e: e.scalar_tensor_tensor(out=Pk.ap, in0=aa.ap, scalar=1.0, in1=Pk.ap, op0=ALU.add, op1=ALU.mult), reads=[aa, Pk], writes=[Pk])
                    return dict(ld=ld, q=Pr, k=Pk, v=Pv, z=kk, b=bop, bonus=af, g=gT)

                def post(t0, ys, ops):
                    yv = ys.ap.rearrange("p h t -> p (h t)")
                    bk = nb()
                    B.mm(bk, bk[0:64, :], B.ones[0:64, 0:64], yv, True, True, [B.ones, ys], inc=True)
                    p.op("dve", lambda e, bk=bk: e.scalar_tensor_tensor(out=yv, in0=bk[0:64, :], scalar=-1.0 / 64, in1=yv, op0=ALU.mult, op1=ALU.add), reads=[bk, ys], writes=[ys])
                    p.op("act", lambda e: e.activation(out=sqs.ap, in_=yv, func=AF.Square), reads=[ys], writes=[sqs])
                    bk2 = nb()
                    B.mm(bk2, bk2[0:64, :], B.ones[0:64, 0:64], sqs.ap, True, True, [B.ones, sqs], inc=True)
                    p.op("act", lambda e, bk2=bk2: e.activation(out=rr_.ap, in_=bk2[0:64, :], func=AF.Sqrt, bias=epsg[:, 0:1], scale=1.0 / 64), reads=[bk2, epsg], writes=[rr_])
                    p.op("dve", lambda e: e.reciprocal(out=rr_.ap, in_=rr_.ap), reads=[rr_], writes=[rr_])
                    p.op("dve", lambda e: e.tensor_tensor(out=yv, in0=yv, in1=rr_.ap, op=ALU.mult), reads=[ys, rr_], writes=[ys])
                    p.op("pool", lambda e: e.tensor_tensor(out=ys.ap, in0=ys.ap, in1=bc(6), op=ALU.mult), reads=[ys, prm], writes=[ys])
                    p.op("pool", lambda e: e.tensor_tensor(out=ys.ap, in0=ys.ap, in1=bc(7), op=ALU.add), reads=[ys, prm], writes=[ys])
                    p.op("pool", lambda e: e.tensor_tensor(out=ys.ap, in0=ys.ap, in1=ops["bonus"].ap, op=ALU.add), reads=[ys, ops["bonus"]], writes=[ys])
                    p.op("pool", lambda e: e.tensor_tensor(out=ytmp.ap, in0=ys.ap, in1=ops["g"].ap, op=ALU.mult), reads=[ys, ops["g"]], writes=[ytmp])
                    yput(0, t0)

                scan_pass(sq, dirn, dk, dv, True, False, prep, S0, yf, None if dirn == 0 else yf, post, cst)
                if sq["pi"] is not None:
                    bk = nb()
                    for h in range(4):
                        B.tr(bk, bk[0:64, h * 64:(h + 1) * 64], S0[:, h, :], [S0], inc=(h == 3))
                    p.op("act", lambda e, bk=bk: e.copy(out=stt_ap, in_=bk[0:64, 0:256].rearrange("p (h d) -> p h d", h=4)), reads=[bk], writes=[sqs])
                    p.dma("sp", st_rwkv[sq["pi"], l, dirn].rearrange("h v k -> v h k"), stt_ap, reads=[sqs])

        def delta(sq):
            T = sq["T"]
            dk, dv = 64, 64
            cst = scan_alloc(dk, dv, True, True)
            yf = B.al([dv, 4, T], BF16, "yf")
            S0 = B.al([dk, 4, dv], F32, "S0")
            raw = [B.al([64, 130], F32, "raw%d" % k_) for k_ in range(2)]
            cq_ = [B.al([64, 4, 128], F32, "cq%d" % k_) for k_ in range(1)]
            ck_ = [B.al([64, 4, 128], F32, "ck%d" % k_) for k_ in range(1)]
            cv_ = [B.al([64, 4, 128], F32, "cv%d" % k_) for k_ in range(1)]
            sqs = B.al([64, 512], F32, "sqs")
            rr_ = B.al([64, 512], F32, "rr_")
            X16 = B.al([16, 128], F32, "X16")
            betaB = B.al([64, 4, 128], F32, "betaB")
            ld_ = [B.al([64, 4, 128], F32, "ld%d" % k_) for k_ in range(1)]
            nbt = B.al([64, 4, 128], F32, "nbt")
            bop_ = [B.al([64, 4, 128], F32, "bop%d" % k_) for k_ in range(1)]
            Pg_ = [B.al([64, 4, 128], F32, "Pg%d" % k_) for k_ in range(1)]
            sel = B.al([16, 16, 64], F32, "sel")
            wc = B.al([64, 3, 12], F32, "wc")
            pa = B.al([64, 8], F32, "pa")
            pb = B.al([64, 8], F32, "pb")
            dng = B.al([64, 1], F32, "dng")
            p.dma("sp", sel.ap, c_sel, writes=[sel])
            p.dma("sp", wc.ap, delta_conv[l].rearrange("k (g d) -> d k g", d=64), writes=[wc], allow_slow_non_contiguous=True)
            p.dma("sp", pa.ap, delta_a_log[l].rearrange("d h -> (d h)").partition_broadcast(64), writes=[pa])
            p.dma("sp", pb.ap, delta_dt_bias[l].rearrange("d h -> (d h)").partition_broadcast(64), writes=[pb])
            p.dma("sp", dng.ap, delta_norm_g[l:l + 1, :].rearrange("o d -> d o"), writes=[dng], allow_slow_non_contiguous=True)
            p.op("act", lambda e: e.activation(out=pa.ap, in_=pa.ap, func=AF.Exp), reads=[pa], writes=[pa])
            p.op("dve", lambda e: e.tensor_scalar(out=pa.ap, in0=pa.ap, scalar1=-1.0, scalar2=None, op0=ALU.mult), reads=[pa], writes=[pa])
            rawc = {"n": 0}

            def conv_group(col0, g, dst_ap, dst_t, t0):
                r = raw[rawc["n"] % 2]
                rawc["n"] += 1
                ta, tb_ = max(t0 - 1, 0), min(t0 + 129, T)
                o = ta - (t0 - 1)
                if o > 0 or tb_ < t0 + 129:
                    p.op("pool", lambda e, r=r: e.memset(r.ap, 0.0), writes=[r])
                proj(col0, 64, sq, lambda bank, ap, s0, n, r=r, o=o: p.op("act", lambda e: e.copy(out=r[:, o:o + n], in_=ap), reads=[bank], writes=[r]), ta, tb_)
                p.op("dve", lambda e, r=r: e.tensor_scalar(out=dst_ap, in0=r[:, 0:128], scalar1=wc[:, 0, g:g + 1], scalar2=None, op0=ALU.mult), reads=[r, wc], writes=[dst_t])
                p.op("dve", lambda e, r=r: e.scalar_tensor_tensor(out=dst_ap, in0=r[:, 1:129], scalar=wc[:, 1, g:g + 1], in1=dst_ap, op0=ALU.mult, op1=ALU.add), reads=[r, wc, dst_t], writes=[dst_t])
                p.op("dve", lambda e, r=r: e.scalar_tensor_tensor(out=dst_ap, in0=r[:, 2:130], scalar=wc[:, 2, g:g + 1], in1=dst_ap, op0=ALU.mult, op1=ALU.add), reads=[r, wc, dst_t], writes=[dst_t])

            def l2n(x, scale):
                xv = x.ap.rearrange("p h t -> p (h t)")
                p.op("act", lambda e: e.activation(out=sqs.ap, in_=xv, func=AF.Square), reads=[x], writes=[sqs])
                bk = nb()
                B.mm(bk, bk[0:64, :], B.ones[0:64, 0:64], sqs.ap, True, True, [B.ones, sqs], inc=True)
                p.op("act", lambda e, bk=bk: e.activation(out=rr_.ap, in_=bk[0:64, :], func=AF.Sqrt, bias=B.eps6[0:64, 0:1], scale=1.0), reads=[bk, B.eps6], writes=[rr_])
                p.op("dve", lambda e: e.reciprocal(out=rr_.ap, in_=rr_.ap), reads=[rr_], writes=[rr_])
                p.op("dve", lambda e: e.scalar_tensor_tensor(out=xv, in0=xv, scalar=float(scale), in1=rr_.ap, op0=ALU.mult, op1=ALU.mult), reads=[x, rr_], writes=[x])

            for dirn in range(2):
                load_dir_consts(cst, dirn)
                if sq["pi"] is None:
                    p.dma("sp", S0.ap, state_delta[l, dirn].rearrange("h k v -> k h v"), writes=[S0])
                else:
                    p.op("pool", lambda e: e.memset(S0.ap, 0.0), writes=[S0])

                def prep(t0, sl, dirn=dirn):
                    cq, ck, cv, ld, bop, Pg = cq_[0], ck_[0], cv_[0], ld_[0], bop_[0], Pg_[0]
                    for h in range(4):
                        conv_group(C0 + 64 * h, h, cq[:, h, :], cq, t0)
                        conv_group(C0 + 256 + 64 * h, 4 + h, ck[:, h, :], ck, t0)
                        conv_group(C0 + 512 + 64 * h, 8 + h, cv[:, h, :], cv, t0)
                        if dirn == 1:
                            proj(C0 + 768 + 64 * h, 64, sq, lambda bank, ap, s0, n, h=h: p.op("act", lambda e: e.activation(out=Pg[:, h, :], in_=ap, func=AF.Silu), reads=[bank], writes=[Pg]), t0, t0 + 128)
                    for x in (cq, ck, cv):
                        p.op("act", lambda e, x=x: e.activation(out=x.ap, in_=x.ap, func=AF.Silu), reads=[x], writes=[x])
                    l2n(cq, 0.125)
                    l2n(ck, 1.0)
                    proj(C0 + 1024, 16, sq, lambda bank, ap, s0, n: p.op("act", lambda e: e.copy(out=X16.ap, in_=ap), reads=[bank], writes=[X16]), t0, t0 + 128)
                    bkb, bka = nb(), nb()
                    for h in range(4):
                        B.mm(bkb, bkb[0:64, h * 128:(h + 1) * 128], sel[:, dirn * 4 + h, :], X16.ap, True, True, [sel, X16], inc=(h == 3))
                    for h in range(4):
                        B.mm(bka, bka[0:64, h * 128:(h + 1) * 128], sel[:, 8 + dirn * 4 + h, :], X16.ap, True, True, [sel, X16], inc=(h == 3))
                    p.op("act", lambda e, bkb=bkb: e.activation(out=betaB.ap.rearrange("p h t -> p (h t)"), in_=bkb[0:64, :], func=AF.Sigmoid), reads=[bkb], writes=[betaB])
                    for h in range(4):
                        p.op("act", lambda e, bka=bka, h=h: e.activation(out=ld[:, h, :], in_=bka[0:64, h * 128:(h + 1) * 128], func=AF.Exp, bias=pb[:, dirn * 4 + h:dirn * 4 + h + 1], scale=1.0),
                             reads=[bka, pb], writes=[ld])
                    p.op("act", lambda e: e.activation(out=ld.ap, in_=ld.ap, func=AF.Ln, bias=B.one1[0:64, 0:1], scale=1.0), reads=[ld, B.one1], writes=[ld])
                    for h in range(4):
                        p.op("dve", lambda e, h=h: e.tensor_scalar(out=ld[:, h, :], in0=ld[:, h, :], scalar1=pa[:, dirn * 4 + h:dirn * 4 + h + 1], scalar2=None, op0=ALU.mult), reads=[ld, pa], writes=[ld])
                    p.op("act", lambda e: e.activation(out=nbt.ap, in_=ld.ap, func=AF.Exp), reads=[ld], writes=[nbt])
                    p.op("dve", lambda e: e.scalar_tensor_tensor(out=nbt.ap, in0=betaB.ap, scalar=-1.0, in1=nbt.ap, op0=ALU.mult, op1=ALU.mult), reads=[betaB, nbt], writes=[nbt])
                    p.op("pool", lambda e: e.tensor_tensor(out=bop.ap, in0=ck.ap, in1=nbt.ap, op=ALU.mult), reads=[ck, nbt], writes=[bop])
                    p.op("pool", lambda e: e.tensor_tensor(out=cv.ap, in0=cv.ap, in1=betaB.ap, op=ALU.mult), reads=[cv, betaB], writes=[cv])
                    if t0 == 0 and dirn == 0:
                        for nm_, tl in (("d_ld", ld), ("d_q", cq), ("d_k", ck), ("d_v", cv), ("d_b", bop)):
                            B.dump(nm_ + "_" + sq["name"], tl, [64, 4, 128])
                    return dict(ld=ld, q=cq, k=ck, v=cv, z=ck, b=bop, r=Pg)

                def post(t0, ys, ops):
                    yv = ys.ap.rearrange("p h t -> p (h t)")
                    p.op("act", lambda e: e.activation(out=sqs.ap, in_=yv, func=AF.Square), reads=[ys], writes=[sqs])
                    bk = nb()
                    B.mm(bk, bk[0:dv, :], B.ones[0:dv, 0:dv], sqs.ap, True, True, [B.ones, sqs], inc=True)
                    p.op("act", lambda e, bk=bk: e.activation(out=rr_.ap, in_=bk[0:dv, :], func=AF.Sqrt, bias=B.eps6[0:dv, 0:1], scale=1.0 / dv), reads=[bk, B.eps6], writes=[rr_])
                    p.op("dve", lambda e: e.reciprocal(out=rr_.ap, in_=rr_.ap), reads=[rr_], writes=[rr_])
                    p.op("dve", lambda e: e.scalar_tensor_tensor(out=sqs.ap, in0=yv, scalar=dng[:, 0:1], in1=rr_.ap, op0=ALU.mult, op1=ALU.mult), reads=[ys, dng, rr_], writes=[sqs])
                    p.op("pool", lambda e: e.tensor_tensor(out=ytmp.ap, in0=sqs.ap.rearrange("p (h t) -> p h t", h=4), in1=ops["r"].ap, op=ALU.mult),
                         reads=[sqs, ops["r"]], writes=[ytmp])
                    yput(2, t0)

                scan_pass(sq, dirn, dk, dv, True, True, prep, S0, yf, None if dirn == 0 else yf, post, cst)
                if dirn == 0:
                    B.dump("d_yf_" + sq["name"], yf, [dv, 4, T])
                if sq["pi"] is not None:
                    p.dma("sp", st_delta[sq["pi"], l, dirn].rearrange("h k v -> k h v"), S0.ap, reads=[S0])

        def attention(sq):
            T = sq["T"]
            rr["n"] = 6
            rr["b"] = 0
            ctx = sq["pi"] is None
            L = T + (256 if ctx else 0)
            koff = 256 if ctx else 0
            nlt = L // 128
            ring["big"] = [B.al([128, 8, 128], BF16, "wbig%d" % k) for k in range(2)]
            qraw = B.al([64, 4, T], F32, "qraw")
            kraw = B.al([64, 2, T], F32, "kraw")
            qb = B.al([64, 4, T], BF16, "qb")
            kb = B.al([64, 2, L], BF16, "kb")
            Vb = B.al([128, nlt, 128], BF16, "Vb")
            scr = [B.al([64, 512], F32, "ascr%d" % k) for k in range(5)]
            Eb = [B.al([128, 512], BF16, "Eb%d" % k) for k in range(2)]
            rs = B.al([64, 512], F32, "rs")
            atmp = B.al([64, 512], BF16, "atmp")
            if ctx:
                cosT = B.al([64, TS], F32, "cosT")
                sinT = B.al([64, TS], F32, "sinT")
                p.dma("sp", cosT.ap, c_cos, writes=[cosT])
                p.dma("sp", sinT.ap, c_sin, writes=[sinT])
                kst = B.al([128, 2, 128], F32, "kst")
                p.dma("sp", kst.ap, cache_k[l].rearrange("(t p) f -> p t f", p=128), writes=[kst])
                p.dma("pool", Vb[:, 0:2, :], cache_v[l].rearrange("(t p) f -> p t f", p=128), writes=[Vb])
                for t in range(2):
                    for n in range(2):
                        bank = nb()
                        B.tr(bank, bank[0:64, 0:128], kst[:, t, n * 64:(n + 1) * 64], [kst])
                        p.op("act", lambda e, bank=bank, t=t, n=n: e.copy(out=kb[:, n, t * 128:(t + 1) * 128], in_=bank[0:64, 0:128]),
                             reads=[bank], writes=[kb])
            else:
                Vf = B.al([128, 2, 128], F32, "Vf")
                kTf = B.al([64, 2, T], F32, "kTf")
                kout = B.al([128, 2, 128], F32, "kout")
            KSUB = int(os.environ.get("KSUB", "9"))
            if KSUB <= 1:
                return
            for h in range(4):
                proj(D0 + 64 * h, 64, sq,
                     lambda bank, ap, s0, n, h=h: p.op("act", lambda e: e.copy(out=qraw[:, h, s0:s0 + n], in_=ap), reads=[bank], writes=[qraw]))
            if KSUB <= 2:
                return
            for n_ in range(2):
                proj(D0 + 256 + 64 * n_, 64, sq,
                     lambda bank, ap, s0, n, n_=n_: p.op("act", lambda e: e.copy(out=kraw[:, n_, s0:s0 + n], in_=ap), reads=[bank], writes=[kraw]))
            if KSUB <= 3:
                return
            w = wload(D0 + 384, 128)
            for s0 in range(0, T, 128):
                bank = nb()
                a = sq["hoff"] + s0
                for kt in range(8):
                    B.mm(bank, bank[:, 0:128], hT_[:, kt, a:a + 128], w[:, kt, :], kt == 0, kt == 7, [w, hT_])
                lt = (koff + s0) // 128
                if ctx:
                    p.op("act", lambda e, bank=bank, lt=lt: e.copy(out=Vb[:, lt, :], in_=bank[:, 0:128]), reads=[bank], writes=[Vb])
                else:
                    p.op("act", lambda e, bank=bank, lt=lt: e.copy(out=Vf[:, lt, :], in_=bank[:, 0:128]), reads=[bank], writes=[Vf])
                    p.op("dve", lambda e, lt=lt: e.tensor_copy(out=Vb[:, lt, :], in_=Vf[:, lt, :]), reads=[Vf], writes=[Vb])
            if not ctx and not os.environ.get("KNOVC"):
                p.dma("sp", vc_out[sq["pi"], l].rearrange("(t p) f -> p t f", p=128), Vf.ap, reads=[Vf])
            KS = int(os.environ.get("KSTOP", "9"))
            if KS <= 1:
                return

            def normrope(raw, nh, gcol, dst, doff, keepf=None):
                for h in range(nh):
                    for c0 in range(0, T, 512):
                        n = min(512, T - c0)
                        x = raw[:, h, c0:c0 + n]
                        s_sq, s_r, s_x, s_1, s_2 = scr[0], scr[1], scr[2], scr[3], scr[4]
                        p.op("act", lambda e, x=x, n=n: e.activation(out=s_sq[:, 0:n], in_=x, func=AF.Square), reads=[raw], writes=[s_sq])
                        bank = nb()
                        B.mm(bank, bank[0:64, 0:n], B.ones[0:64, 0:64], s_sq[:, 0:n], True, True, [B.ones, s_sq])
                        p.op("act", lambda e, bank=bank, n=n: e.activation(out=s_r[:, 0:n], in_=bank[0:64, 0:n], func=AF.Sqrt, bias=B.eps6[0:64, 0:1], scale=1.0 / 64),
                             reads=[bank, B.eps6], writes=[s_r])
                        p.op("dve", lambda e, n=n: e.reciprocal(out=s_r[:, 0:n], in_=s_r[:, 0:n]), reads=[s_r], writes=[s_r])
                        d = dst[:, h, doff + c0:doff + c0 + n]
                        if ctx:
                            p.op("dve", lambda e, x=x, n=n: e.scalar_tensor_tensor(out=s_x[:, 0:n], in0=x, scalar=gcol, in1=s_r[:, 0:n], op0=ALU.mult, op1=ALU.mult),
                                 reads=[raw, s_r, qkng2], writes=[s_x])
                            bank2 = nb()
                            B.mm(bank2, bank2[0:64, 0:n], rot_t.ap, s_x[:, 0:n], True, True, [rot_t, s_x])
                            p.op("dve", lambda e, n=n, c0=c0: e.tensor_tensor(out=s_1[:, 0:n], in0=s_x[:, 0:n], in1=cosT[:, c0:c0 + n], op=ALU.mult),
                                 reads=[s_x, cosT], writes=[s_1])
                            p.op("dve", lambda e, n=n, c0=c0, bank2=bank2: e.tensor_tensor(out=s_2[:, 0:n], in0=bank2[0:64, 0:n], in1=sinT[:, c0:c0 + n], op=ALU.mult),
                                 reads=[bank2, sinT], writes=[s_2])
                            p.op("pool", lambda e, n=n, d=d: e.tensor_tensor(out=d, in0=s_1[:, 0:n], in1=s_2[:, 0:n], op=ALU.add),
                                 reads=[s_1, s_2], writes=[dst])
                        else:
                            p.op("dve", lambda e, x=x, n=n, d=d: e.scalar_tensor_tensor(out=d, in0=x, scalar=gcol, in1=s_r[:, 0:n], op0=ALU.mult, op1=ALU.mult),
                                 reads=[raw, s_r, qkng2], writes=[dst])
                            if keepf is not None:
                                kf = keepf[:, h, c0:c0 + n]
                                p.op("dve", lambda e, x=x, n=n, kf=kf: e.scalar_tensor_tensor(out=kf, in0=x, scalar=gcol, in1=s_r[:, 0:n], op0=ALU.mult, op1=ALU.mult),
                                     reads=[raw, s_r, qkng2], writes=[keepf])

            normrope(qraw, 4, qkng2[:, 2 * l:2 * l + 1], qb, 0)
            normrope(kraw, 2, qkng2[:, 2 * l + 1:2 * l + 2], kb, koff, None if ctx else kTf)
            if KS <= 2:
                return
            if not ctx:
                for t in range(2):
                    bank = nb()
                    for n_ in range(2):
                        B.tr(bank, bank[:, n_ * 64:(n_ + 1) * 64], kTf[:, n_, t * 128:(t + 1) * 128], [kTf], inc=(n_ == 1))
                    p.op("act", lambda e, bank=bank, t=t: e.copy(out=kout[:, t, :], in_=bank[:, 0:128]), reads=[bank], writes=[kout])
                p.dma("sp", kc_out[sq["pi"], l].rearrange("(t p) f -> p t f", p=128), kout.ap, reads=[kout])
            if KS <= 3:
                return
            ei = 0
            for h in range(4):
                nk = h // 2
                for q0 in range(0, T, 512):
                    n = min(512, T - q0)
                    b1, b2 = PS[6], PS[7]
                    for lt in range(nlt):
                        bs = nb()
                        B.mm(bs, bs[:, 0:n], kb[:, nk, lt * 128:(lt + 1) * 128], qb[:, h, q0:q0 + n], True, True, [kb, qb])
                        E = Eb[ei % 2]
                        ei += 1
                        p.op("act", lambda e, E=E, bs=bs, n=n: e.activation(out=E[:, 0:n], in_=bs[:, 0:n], func=AF.Exp), reads=[bs], writes=[E])
                        B.mm(b1, b1[0:64, 0:n], Vb[:, lt, nk * 64:(nk + 1) * 64], E[:, 0:n], lt == 0, lt == nlt - 1, [Vb, E], inc=True)
                        B.mm(b2, b2[0:64, 0:n], ones_b.ap, E[:, 0:n], lt == 0, lt == nlt - 1, [ones_b, E], inc=True)
                    p.op("dve", lambda e, b2=b2, n=n: e.reciprocal(out=rs[:, 0:n], in_=b2[0:64, 0:n]), reads=[b2], writes=[rs])
                    if h % 2 == 0:
                        d = yT[0:64, 6 + h // 2, q0:q0 + n]
                        p.op("dve", lambda e, b1=b1, n=n, d=d: e.tensor_tensor(out=d, in0=b1[0:64, 0:n], in1=rs[:, 0:n], op=ALU.mult),
                             reads=[b1, rs], writes=[yT])
                    else:
                        p.op("dve", lambda e, b1=b1, n=n: e.tensor_tensor(out=atmp[:, 0:n], in0=b1[0:64, 0:n], in1=rs[:, 0:n], op=ALU.mult),
                             reads=[b1, rs], writes=[atmp])
                        p.dma("sp", yT[64:128, 6 + h // 2, q0:q0 + n], atmp[:, 0:n], reads=[atmp], writes=[yT])

        def merge(sq):
            T = sq["T"]
            j = sq["j"]
            ring["big"] = [B.al([128, 8, 128], BF16, "wbig%d" % k) for k in range(2)]
            mergedT = B.al([128, 8, T], BF16, "mergedT")
            acc = B.al([128, T], F32, "acc")
            tmpm = [B.al([128, 512], F32, "tmpm%d" % k) for k in range(2)]
            sgm = [B.al([128, 512], F32, "sgm%d" % k) for k in range(2)]
            wbr = [B.al([128, 2, 128], BF16, "wbr%d" % k) for k in range(2)]
            n_ = 0
            for ob in range(8):
                for m in range(4):
                    wg = wload(G0 + m * 1024 + ob * 128, 128)
                    wb = wbr[(ob * 4 + m) % 2]
                    p.dma("pool", wb.ap, w_branch[l, m].rearrange("(t p) c -> p t c", p=128)[:, :, ob * 128:(ob + 1) * 128], writes=[wb])
                    for s0 in range(0, T, 512):
                        n = min(512, T - s0)
                        bg, bu = nb(), nb()
                        for kt in range(8):
                            B.mm(bg, bg[:, 0:n], wg[:, kt, :], hT_[:, kt, sq["hoff"] + s0:sq["hoff"] + s0 + n], kt == 0, kt == 7, [wg, hT_])
                        for t_ in range(2):
                            B.mm(bu, bu[:, 0:n], wb[:, t_, :], yT[:, m * 2 + t_, s0:s0 + n], t_ == 0, t_ == 1, [wb, yT])
                        sg = sgm[n_ % 2]
                        n_ += 1
                        p.op("act", lambda e, sg=sg, bg=bg, n=n: e.activation(out=sg[:, 0:n], in_=bg[:, 0:n], func=AF.Sigmoid), reads=[bg], writes=[sg])
                        a = acc[:, s0:s0 + n]
                        if m == 0:
                            p.op("dve", lambda e, a=a, sg=sg, bu=bu, n=n: e.tensor_tensor(out=a, in0=bu[:, 0:n], in1=sg[:, 0:n], op=ALU.mult), reads=[bu, sg], writes=[acc])
                        else:
                            tm = tmpm[n_ % 2]
                            p.op("dve", lambda e, tm=tm, sg=sg, bu=bu, n=n: e.tensor_tensor(out=tm[:, 0:n], in0=bu[:, 0:n], in1=sg[:, 0:n], op=ALU.mult), reads=[bu, sg], writes=[tm])
                            if m < 3:
                                p.op("dve", lambda e, a=a, tm=tm, n=n: e.tensor_tensor(out=a, in0=a, in1=tm[:, 0:n], op=ALU.add), reads=[acc, tm], writes=[acc])
                            else:
                                d = mergedT[:, ob, s0:s0 + n]
                                p.op("dve", lambda e, a=a, tm=tm, n=n, d=d: e.tensor_tensor(out=d, in0=a, in1=tm[:, 0:n], op=ALU.add), reads=[acc, tm], writes=[mergedT, acc])
            woutv = w_out[l].rearrange("(kt p) c -> p kt c", p=128)
            for ob in range(8):
                w = wbig()
                p.dma("pool", w.ap, woutv[:, :, ob * 128:(ob + 1) * 128], writes=[w])
                for xi, (tb, c0) in enumerate(sq["xs"]):
                    s0 = xi * 512
                    n = min(512, T - s0)
                    bank = nb()
                    for kt in range(8):
                        B.mm(bank, bank[:, 0:n], w[:, kt, :], mergedT[:, kt, s0:s0 + n], kt == 0, kt == 7, [w, mergedT])
                    gt = gtT[l][:, 8 + ob, j:j + 1]
                    dst = xT[tb][:, ob, c0:c0 + n]
                    p.op("dve", lambda e, dst=dst, bank=bank, gt=gt, n=n: e.scalar_tensor_tensor(out=dst, in0=bank[:, 0:n], scalar=gt, in1=dst, op0=ALU.mult, op1=ALU.add),
                         reads=[bank, gtT[l], xT[tb]], writes=[xT[tb]])

        hdone = {"p": False, "s": False}
        for si, sq in enumerate(SEQS):
            if si >= int(os.environ.get("KSEQ", "9")):
                break
            grp = "s" if sq["pi"] is None else "p"
            p.barrier()
            B.arena_off = mark
            if not hdone[grp]:
                hdone[grp] = True
                norm_setup()
                if grp == "p":
                    adaln(l, 1, 2, hT_, PS[6], hT_[:, :, 0:512])
                else:
                    adaln(l, 1, 0, hT_, PS[6], hT_[:, :, 0:512])
                    adaln(l, 1, 1, hT_, PS[6], hT_[:, :, 512:1024])
                p.barrier()
                B.arena_off = mark
            for m in MIXERS_OFF:
                d = yT[:, m * 2:(m + 1) * 2, 0:sq["T"]]
                p.op("pool", lambda e, d=d: e.memset(d, 0.0), writes=[yT])
            if 0 not in MIXERS_OFF:
                rwkv(sq)
                B.dump("yTa_" + sq["name"], yT, [128, 8, TS])
                p.barrier()
                B.arena_off = mark
            if 1 not in MIXERS_OFF:
                gla(sq)
                B.dump("yT_" + sq["name"], yT, [128, 8, TS])
                p.barrier()
                B.arena_off = mark
            if 2 not in MIXERS_OFF:
                delta(sq)
                B.dump("yTc_" + sq["name"], yT, [128, 8, TS])
                p.barrier()
                B.arena_off = mark
            if 3 not in MIXERS_OFF:
                attention(sq)
                rr["n"] = 8
                p.barrier()
                B.arena_off = mark
            if KMIX > 1:
                merge(sq)

    def final_out():
        B.arena_reset()
        norm_setup()
        t_scr = S["t_scr"]
        for tb in range(3):
            rms_stats(tb, PS[6])
            for ft in range(8):
                t = t_scr[ft % 2]
                src = xT[tb][:, ft, :]
                p.op("dve", lambda e, t=t, src=src, rs_=rstd[tb]: e.tensor_tensor(out=t.ap, in0=src, in1=rs_.ap, op=ALU.mult),
                     reads=[xT[tb], rstd[tb]], writes=[t])
                g = fngT[:, ft:ft + 1]
                dst = xT[tb][:, ft, :]
                p.op("act", lambda e, dst=dst, t=t, g=g: e.activation(out=dst, in_=t.ap, func=AF.Copy, scale=g),
                     reads=[t, fngT], writes=[xT[tb]])
        store_x()

    def store_x():
        stage_tm = [B.al([128, D], F32, "stage_tm%d" % i) for i in range(2)]
        for tt in range(12):
            st = stage_tm[tt % 2]
            tb, tq = divmod(tt, 4)
            for half in range(2):
                bank = PS[(tt * 2 + half) % 8]
                for jj in range(4):
                    ft = half * 4 + jj
                    B.tr(bank, bank[:, jj * 128:(jj + 1) * 128], xT[tb][:, ft, tq * 128:(tq + 1) * 128], [xT[tb]], inc=(jj == 3))
                dst = st[:, half * 512:(half + 1) * 512]
                if half == 0:
                    p.op("dve", lambda e, dst=dst, bank=bank: e.tensor_copy(out=dst, in_=bank.ap), reads=[bank], writes=[st])
                else:
                    p.op("act", lambda e, dst=dst, bank=bank: e.copy(out=dst, in_=bank.ap), reads=[bank], writes=[st])
            p.dma("sp", y_tm[tt * 128:(tt + 1) * 128, :], st.ap, reads=[st])

    if stop_after == "in":
        B.arena_reset()
        store_x()
    else:
        for l in range(DEPTH):
            ffn(l, 0, 0)
            if stop_after == "ffn1" or (stop_after == "l1ffn1" and l == 1):
                break
            mixing(l)
            if stop_after == "mix1" or (stop_after == "l1mix" and l == 1):
                break
            ffn(l, 1, 2)
            if stop_after == "l0":
                break
        if stop_after in ("ffn1", "mix1", "l0", "l1ffn1", "l1mix"):
            B.arena_reset()
            store_x()
        else:
            final_out()

    p.finish()
    p.emit()
    return B


CONSTS = None


def make_consts():
    c = {}
    c["c_ident"] = np.eye(128, dtype=np.float32)
    c["c_ones"] = np.ones((128, 128), dtype=np.float32)
    rot = np.zeros((64, 64), np.float32)
    for base in (0, 32):
        for d in range(16):
            rot[base + d + 16, base + d] = -1.0
            rot[base + d, base + d + 16] = 1.0
    c["c_rot"] = rot
    t = np.arange(TS)
    inv = (1.0 / (np.float32(10000.0) ** (np.arange(0, 32, 2, dtype=np.float32) / np.float32(32)))).astype(np.float32)
    ang = np.zeros((64, TS), np.float32)
    for d in range(64):
        pos = (t // 64) if d < 32 else (t % 64)
        ang[d] = pos.astype(np.float32) * inv[d % 16]
    r_, c_ = np.meshgrid(np.arange(128), np.arange(128), indexing="ij")
    same = (r_ // 64) == (c_ // 64)
    IU = (same & (r_ <= c_)).astype(np.float32)
    SU = (same & (r_ < c_)).astype(np.float32)
    SL = (same & (r_ > c_)).astype(np.float32)
    IL = (same & (r_ >= c_)).astype(np.float32)
    neg = lambda m_: ((1.0 - m_) * -30000.0).astype(np.float32)
    c["c_tri"] = np.ascontiguousarray(np.stack([np.stack([IU, SU, SL], 1), np.stack([IL, SL, SU], 1)], 0))
    c["c_mult"] = np.ascontiguousarray(np.stack([np.stack([IU, IU, SU, SU], 1), np.stack([IL, IL, SL, SL], 1)], 0))
    c["c_ms"] = np.ascontiguousarray(np.stack([SL, SU], 0))
    c["c_nm"] = np.ascontiguousarray(np.stack([np.stack([neg(IU), neg(SU), neg(SL)], 1), np.stack([neg(IL), neg(SL), neg(SU)], 1)], 0))
    sel = np.zeros((16, 16, 64), np.float32)
    for r__ in range(16):
        sel[r__, r__, :] = 1.0
    c["c_sel"] = sel
    c["c_cos"] = np.cos(ang).astype(np.float32)
    c["c_sin"] = np.sin(ang).astype(np.float32)
    return c


def make_in_maps(inputs, names, bf16_inputs=()):
    consts = make_consts()
    maps = []
    f = lambda a: np.ascontiguousarray(np.asarray(a, dtype=np.float32))
    shared = {k: f(inputs[k]) for k in ("norm_g", "w_mod", "b_mod", "ffn_w_in", "ffn_w_out", "final_norm_g",
                                        "w_in", "w_branch", "w_out", "attn_q_norm", "attn_k_norm",
                                        "gla_alpha_up", "gla_alpha_b", "gla_norm_g",
                                        "delta_conv", "delta_a_log", "delta_dt_bias", "delta_norm_g",
                                        "rwkv_mu", "rwkv_w0", "rwkv_w_up", "rwkv_a0", "rwkv_a_up", "rwkv_g_up", "rwkv_k_k", "rwkv_k_a",
                                        "rwkv_r_k", "rwkv_ln_g", "rwkv_ln_b") if k in names}
    xs, xp = f(inputs["x_sample"]), f(inputs["x_prompt"])
    cc, cctx = f(inputs["c"]), f(inputs["c_ctx"])
    ck, cv = f(inputs["cache_attn_k"]), f(inputs["cache_attn_v"])
    for b in range(8):
        m = dict(shared)
        m.update(consts)
        m["x_tm"] = np.ascontiguousarray(np.concatenate([xs[b], xp[2 * b], xp[2 * b + 1]], axis=0))
        m["c2"] = np.ascontiguousarray(np.stack([cc[b], cctx], axis=0))
        m["cache_k"] = np.ascontiguousarray(ck[b].reshape(DEPTH, 256, 128))
        m["cache_v"] = np.ascontiguousarray(cv[b].reshape(DEPTH, 256, 128))
        m["state_gla"] = f(inputs["state_gla"][b])
        m["state_delta"] = f(inputs["state_delta"][b])
        m["state_rwkv"] = f(inputs["state_rwkv"][b])
        if bf16_inputs:
            import ml_dtypes
            for k in bf16_inputs:
                if k in m:
                    key = ("bf", k) if k in shared else None
                    if key and key in _bfcache:
                        m[k] = _bfcache[key]
                    else:
                        m[k] = m[k].astype(ml_dtypes.bfloat16)
                        if key:
                            _bfcache[key] = m[k]
        maps.append({k: v for k, v in m.items() if k in names})
    return maps


_bfcache = {}


def kernel(**inputs):
    B = build_program()
    maps = make_in_maps(inputs, set(B.inp.keys()), B.bf16_inputs)
    res = run_bass_kernel_spmd(B.nc, maps, core_ids=list(range(8)))
    r = res.results
    f32 = lambda a: np.ascontiguousarray(np.asarray(a, dtype=np.float32))
    y_sample = np.stack([f32(r[b]["y_tm"])[:TS] for b in range(8)], axis=0)
    y_prompt = np.stack([f32(r[b // 2]["y_tm"])[TS + (b % 2) * TP:TS + (b % 2 + 1) * TP] for b in range(16)], axis=0)
    kc = np.stack([f32(r[b // 2]["kc_out"])[b % 2].reshape(DEPTH, 256, 2, 64) for b in range(16)], axis=0)
    vc = np.stack([f32(r[b // 2]["vc_out"])[b % 2].reshape(DEPTH, 256, 2, 64) for b in range(16)], axis=0)
    sr = np.stack([f32(r[b // 2]["st_rwkv"])[b % 2] for b in range(16)], axis=0)
    sg = np.stack([f32(r[b // 2]["st_gla"])[b % 2] for b in range(16)], axis=0)
    sd = np.stack([f32(r[b // 2]["st_delta"])[b % 2] for b in range(16)], axis=0)
    return (y_prompt, y_sample, kc, vc, sr, sg, sd)
```
